# Optimizing a Trainium2 kernel written in Bass

```python
import math
import jax, jax.numpy as jnp
from jax import lax
import numpy as np

D_MODEL = 2048
BATCH = 2
SEQ = 4096
DEPTH = 1
DEC_BATCH = 4
DEC_SEQ = 2048
PAST_LEN = 128

HEAD_DIM = 128
MIX_WIDTH = D_MODEL
N_ATTN_HEADS = 8
N_KV_HEADS = 2
GQA_GROUP = N_ATTN_HEADS // N_KV_HEADS
ATTN_WIDTH = N_ATTN_HEADS * HEAD_DIM
KV_WIDTH = N_KV_HEADS * HEAD_DIM
N_GMLP_HEADS = 8
GMLP_WIDTH = N_GMLP_HEADS * HEAD_DIM
IN_WIDTH = ATTN_WIDTH + 2 * KV_WIDTH + 2 * GMLP_WIDTH
WINDOW = 128
BLOCK = 128
CHUNK = 128
ROPE_THETA = 500000.0
ROT_DIM = HEAD_DIM // 4
D_FF = 4 * D_MODEL
EPS = 1e-6
NEG_INF = -1e30

kernel_name = "hymba_style_window_gqa_gmlp_encoder"


def rms_norm(x, g):
    xf = x.astype(jnp.float32)
    y = xf * lax.rsqrt(jnp.mean(xf * xf, axis=-1, keepdims=True) + EPS)
    return (y * g.astype(jnp.float32)).astype(x.dtype)


def layer_norm(x, g, b):
    xf = x.astype(jnp.float32)
    mu = jnp.mean(xf, axis=-1, keepdims=True)
    xc = xf - mu
    y = xc * lax.rsqrt(jnp.mean(xc * xc, axis=-1, keepdims=True) + EPS)
    return (y * g.astype(jnp.float32) + b.astype(jnp.float32)).astype(x.dtype)


def partial_rope(x):
    S = x.shape[1]
    pos = jnp.arange(S, dtype=jnp.float32)
    inv_freq = ROPE_THETA ** (-jnp.arange(0, ROT_DIM, 2, dtype=jnp.float32) / ROT_DIM)
    ang = pos[:, None] * inv_freq[None, :]
    cos = jnp.cos(ang)[None, :, None, :]
    sin = jnp.sin(ang)[None, :, None, :]
    xf = x.astype(jnp.float32)
    half = ROT_DIM // 2
    x1 = xf[..., :half]
    x2 = xf[..., half:ROT_DIM]
    out = jnp.concatenate([x1 * cos - x2 * sin, x2 * cos + x1 * sin, xf[..., ROT_DIM:]], axis=-1)
    return out.astype(x.dtype)


def windowed_gqa_sink(q, k, v, sink):
    B, S = q.shape[0], q.shape[1]
    nb = S // BLOCK
    qb = q.reshape(B, nb, BLOCK, N_KV_HEADS, GQA_GROUP, HEAD_DIM)
    pad = ((0, 0), (BLOCK, BLOCK), (0, 0), (0, 0))
    kp = jnp.pad(k, pad).reshape(B, nb + 2, BLOCK, N_KV_HEADS, HEAD_DIM)
    vp = jnp.pad(v, pad).reshape(B, nb + 2, BLOCK, N_KV_HEADS, HEAD_DIM)
    kwin = jnp.concatenate([kp[:, :-2], kp[:, 1:-1], kp[:, 2:]], axis=2)
    vwin = jnp.concatenate([vp[:, :-2], vp[:, 1:-1], vp[:, 2:]], axis=2)
    scale = 1.0 / math.sqrt(HEAD_DIM)
    s = jnp.einsum('bnqkgd,bnmkd->bnkgqm', qb, kwin,
                   preferred_element_type=jnp.float32) * scale
    blk = jnp.arange(nb)[:, None]
    qpos = blk * BLOCK + jnp.arange(BLOCK)[None, :]
    kpos = (blk - 1) * BLOCK + jnp.arange(3 * BLOCK)[None, :]
    valid = ((jnp.abs(qpos[:, :, None] - kpos[:, None, :]) <= WINDOW)
             & (kpos >= 0)[:, None, :] & (kpos < S)[:, None, :])
    s = jnp.where(valid[None, :, None, None], s, NEG_INF)
    sink_l = sink.astype(jnp.float32).reshape(N_KV_HEADS, GQA_GROUP)[None, None, :, :, None, None]
    m = jnp.maximum(jnp.max(s, axis=-1, keepdims=True), sink_l)
    p = jnp.exp(s - m)
    p = p / (jnp.sum(p, axis=-1, keepdims=True) + jnp.exp(sink_l - m))
    o = jnp.einsum('bnkgqm,bnmkd->bnqkgd', p.astype(v.dtype), vwin)
    return o.reshape(B, S, ATTN_WIDTH)


def chunked_gmlp(u, vg, g_ln, b_ln, w_sp, b_sp):
    B, S = u.shape[0], u.shape[1]
    nc = S // CHUNK
    u = jax.nn.gelu(u)
    vg = layer_norm(jax.nn.gelu(vg), g_ln, b_ln)
    vc = vg.reshape(B, nc, CHUNK, N_GMLP_HEADS, HEAD_DIM)
    mixed = jnp.einsum('hij,bnjhc->bnihc', w_sp, vc) + b_sp.T[None, None, :, :, None]
    return u * mixed.reshape(B, S, GMLP_WIDTH)


def encoder_layer(x, g_mix, w_in, g_q, g_k, sink, g_v_ln, b_v_ln, w_spatial, b_spatial,
                  g_attn_out, g_gmlp_out, w_out, g_ffn, w_up, w_down):
    B, S, _ = x.shape
    h = rms_norm(x, g_mix)
    proj = h @ w_in
    o1 = ATTN_WIDTH
    o2 = o1 + KV_WIDTH
    o3 = o2 + KV_WIDTH
    o4 = o3 + GMLP_WIDTH
    q = proj[..., :o1].reshape(B, S, N_ATTN_HEADS, HEAD_DIM)
    k = proj[..., o1:o2].reshape(B, S, N_KV_HEADS, HEAD_DIM)
    v = proj[..., o2:o3].reshape(B, S, N_KV_HEADS, HEAD_DIM)
    u = proj[..., o3:o4]
    vg = proj[..., o4:]
    q = partial_rope(rms_norm(q, g_q))
    k = partial_rope(rms_norm(k, g_k))
    attn = windowed_gqa_sink(q, k, v, sink)
    gm = chunked_gmlp(u, vg, g_v_ln, b_v_ln, w_spatial, b_spatial)
    mix = jnp.concatenate([rms_norm(attn, g_attn_out), rms_norm(gm, g_gmlp_out)], axis=-1)
    x = x + mix @ w_out
    h2 = rms_norm(x, g_ffn)
    x = x + jnp.square(jax.nn.relu(h2 @ w_up)) @ w_down
    return x


def setup_inputs(seed: int = 0) -> dict:
    key = jax.random.key(seed)
    ks = jax.random.split(key, 20)
    f32 = jnp.float32
    nrm = lambda k, shape, s: jax.random.normal(k, shape, f32) * s
    L = DEPTH
    return {
        "x_prompt": jax.random.normal(ks[0], (BATCH, SEQ, D_MODEL), f32),
        "x_sample": jax.random.normal(ks[1], (DEC_BATCH, DEC_SEQ, D_MODEL), f32),
        "g_mix": 1.0 + nrm(ks[2], (L, D_MODEL), 0.02),
        "w_in": nrm(ks[3], (L, D_MODEL, IN_WIDTH), D_MODEL ** -0.5),
        "g_q": 1.0 + nrm(ks[4], (L, HEAD_DIM), 0.02),
        "g_k": 1.0 + nrm(ks[5], (L, HEAD_DIM), 0.02),
        "sink": nrm(ks[6], (L, N_ATTN_HEADS), 0.5),
        "g_v_ln": 1.0 + nrm(ks[7], (L, GMLP_WIDTH), 0.02),
        "b_v_ln": nrm(ks[8], (L, GMLP_WIDTH), 0.02),
        "w_spatial": nrm(ks[9], (L, N_GMLP_HEADS, CHUNK, CHUNK), CHUNK ** -0.5),
        "b_spatial": 1.0 + nrm(ks[10], (L, N_GMLP_HEADS, CHUNK), 0.1),
        "g_attn_out": 1.0 + nrm(ks[11], (L, ATTN_WIDTH), 0.02),
        "g_gmlp_out": 1.0 + nrm(ks[12], (L, GMLP_WIDTH), 0.02),
        "w_out": nrm(ks[13], (L, MIX_WIDTH, D_MODEL), MIX_WIDTH ** -0.5),
        "g_ffn": 1.0 + nrm(ks[14], (L, D_MODEL), 0.02),
        "w_up": nrm(ks[15], (L, D_MODEL, D_FF), D_MODEL ** -0.5),
        "w_down": nrm(ks[16], (L, D_FF, D_MODEL), D_FF ** -0.5),
    }


def reference(x_prompt, x_sample, g_mix, w_in, g_q, g_k, sink, g_v_ln, b_v_ln, w_spatial,
              b_spatial, g_attn_out, g_gmlp_out, w_out, g_ffn, w_up, w_down):
    y_prompt = x_prompt
    y_sample = x_sample
    for l in range(DEPTH):
        params = (g_mix[l], w_in[l], g_q[l], g_k[l], sink[l], g_v_ln[l], b_v_ln[l],
                  w_spatial[l], b_spatial[l], g_attn_out[l], g_gmlp_out[l], w_out[l],
                  g_ffn[l], w_up[l], w_down[l])
        y_prompt = encoder_layer(y_prompt, *params)
        y_sample = encoder_layer(y_sample, *params)
    return (y_prompt, y_sample)
```

```python
import contextlib
import math
import numpy as np
import concourse.bass as bass
import concourse.mybir as mybir
from concourse.bass_utils import run_bass_kernel_spmd

F32 = mybir.dt.float32
BF16 = mybir.dt.bfloat16
AF = mybir.ActivationFunctionType
ALU = mybir.AluOpType
AX = mybir.AxisListType

N_CORES = 8
NT = 4
T = 512
TE = 768
D = 2048
DFF = 8192
EPS = 1e-6
NEG = -1e30
SCALE = 1.0 / math.sqrt(128.0)
NSLOT = 3


class Res:
    __slots__ = ("name", "w", "r")

    def __init__(self, name):
        self.name = name
        self.w = None
        self.r = {}


class DSem:
    def __init__(self, prog, name):
        self.h = prog.new_sem(name)
        self.key = name
        self.count = 0


class Prog:
    ENGS = ("pe", "act", "dve", "pool", "sp")

    def __init__(self, nc, stack):
        self.nc = nc
        self.stack = stack
        self.eng = {"pe": nc.tensor, "act": nc.scalar, "dve": nc.vector,
                    "pool": nc.gpsimd, "sp": nc.sync}
        self.semh = {}
        self.sem = {}
        self.cnt = {}
        self.waited = {e: {} for e in self.ENGS}
        for e in self.ENGS:
            self.sem[e] = "E_" + e
            self.semh["E_" + e] = stack.enter_context(nc.semaphore("E_" + e))
            self.cnt[e] = 0

    def new_sem(self, name):
        h = self.stack.enter_context(self.nc.semaphore(name))
        self.semh[name] = h
        return h

    def _emit_waits(self, e, reads, writes):
        need = {}

        def add(pt):
            if pt is None:
                return
            k, v = pt
            if need.get(k, 0) < v:
                need[k] = v
        for r in reads:
            add(r.w)
        for w in writes:
            add(w.w)
            for k, v in w.r.items():
                add((k, v))
        own = self.sem[e]
        for k, v in need.items():
            if e == "pe" and k == own:
                continue
            if self.waited[e].get(k, 0) >= v:
                continue
            self.eng[e].wait_ge(self.semh[k], v)
            self.waited[e][k] = v

    def _commit(self, pt, reads, writes):
        k, v = pt
        for r in reads:
            if r.r.get(k, 0) < v:
                r.r[k] = v
        for w in writes:
            w.w = pt
            w.r = {}

    def op(self, e, emit, reads=(), writes=()):
        self._emit_waits(e, reads, writes)
        ins = emit()
        self.cnt[e] += 1
        assert self.cnt[e] < 60000
        ins.then_inc(self.semh[self.sem[e]], 1)
        self._commit((self.sem[e], self.cnt[e]), reads, writes)

    def group(self, e, emits, reads=(), writes=()):
        self._emit_waits(e, reads, writes)
        ins = None
        for f in emits:
            ins = f()
        self.cnt[e] += 1
        ins.then_inc(self.semh[self.sem[e]], 1)
        self._commit((self.sem[e], self.cnt[e]), reads, writes)

    def dma(self, q, dsem, emit, reads=(), writes=()):
        self._emit_waits(q, reads, writes)
        ins = emit()
        dsem.count += 16
        ins.then_inc(dsem.h, 16)
        self._commit((dsem.key, dsem.count), reads, writes)

    def wait_all(self, e, dsems):
        for d in dsems:
            if d.count:
                self.eng[e].wait_ge(d.h, d.count)


class _Stop(Exception):
    pass


def alias_edges(src, dst):
    for d in dst:
        for s_ in src:
            if s_.w is not None:
                k, v = s_.w
                if d.r.get(k, 0) < v:
                    d.r[k] = v
            for k, v in s_.r.items():
                if d.r.get(k, 0) < v:
                    d.r[k] = v


class Item:
    def __init__(self, name, gen, after=()):
        self.name = name
        self.gen = gen
        self.after = tuple(after)


def run_pipeline(items, width, stagger):
    held = {}
    done = set()
    active = []
    nxt = 0
    rnd = 0
    last_start = -10 ** 9
    while active or nxt < len(items):
        if nxt < len(items) and len(active) < width and (rnd - last_start >= stagger or not active):
            it = items[nxt]
            if all(a in done for a in it.after):
                active.append([it, None])
                nxt += 1
                last_start = rnd
            else:
                assert active, f"pipeline stuck at {it.name}"
        progressed = False
        for ent in list(active):
            it, pending = ent
            if pending is not None:
                if pending in held:
                    continue
                held[pending] = it.name
                ent[1] = None
                progressed = True
                continue
            try:
                r = next(it.gen)
            except StopIteration:
                active.remove(ent)
                done.add(it.name)
                for k in [k for k, v in held.items() if v == it.name]:
                    del held[k]
                progressed = True
                continue
            progressed = True
            if isinstance(r, tuple):
                if r[0] == "acq":
                    if r[1] in held:
                        ent[1] = r[1]
                    else:
                        held[r[1]] = it.name
                elif r[0] == "rel":
                    assert held.get(r[1]) == it.name, (r, held)
                    del held[r[1]]
                elif r[0] == "mark":
                    done.add(r[1])
        assert progressed or nxt < len(items), "pipeline deadlock"
        rnd += 1


def build_program(n_tasks=NT, stop_at=None):
    nc = bass.Bass("TRN2", target_bir_lowering=False)

    def ckpt(name):
        if stop_at is not None and name == stop_at:
            raise _Stop()

    def din(name, shape):
        return nc.dram_tensor(name, shape, F32, kind="ExternalInput").ap()

    xe = din("xe", [NT, TE, D])
    y = nc.dram_tensor("y", [NT, T, D], F32, kind="ExternalOutput").ap()
    w_in = din("w_in", [D, 3584])
    w_out = din("w_out", [D, D])
    w_up = din("w_up", [D, DFF])
    w_down = din("w_down", [DFF, D])
    msk_d = din("msk", [NT, 128, 3 * 384])
    cs_d = din("cs", [NT, 128, 96])
    sn_d = din("sn", [NT, 128, 96])
    gmixT_d = din("gmixT", [128, 16])
    gffnT_d = din("gffnT", [128, 16])
    goutT_d = din("goutT", [128, 16])
    gq_d = din("gq", [128, 128])
    gk_d = din("gk", [128, 128])
    gln_d = din("gln", [128, 1024])
    bln_d = din("bln", [128, 1024])
    sink_d = din("sinkb", [128, 8])
    wspT_d = din("wspT", [128, 1024])
    bsp_d = din("bsp", [128, 8])

    w_in_v = w_in.rearrange("(c p) n -> p c n", p=128)
    w_out_v = w_out.rearrange("(c p) n -> p c n", p=128)
    w_up_v = w_up.rearrange("(c p) n -> p c n", p=128)
    w_down_v = w_down.rearrange("(s c p) n -> s p c n", p=128, c=16)

    with contextlib.ExitStack() as st:
        P = Prog(nc, st)

        def sb(name, shape, dt):
            return st.enter_context(nc.sbuf_tensor(name, shape, dt))

        xres = sb("xres", [128, 4, D], F32)
        xst = sb("xst", [128, D], F32)
        xb0 = sb("xb", [128, D], BF16)
        big = sb("big", [128, 32768], BF16)
        mh = sb("mh", [128, 16, T], BF16)
        wsl = [sb(f"wsl{i}", [128, 16, 512], BF16) for i in range(NSLOT)]
        yst = sb("yst", [128, 4, 512], F32)
        tmpf0 = sb("tmpf", [128, 1024], F32)
        ident = sb("ident", [128, 128], BF16)
        identf = sb("identf", [128, 128], F32)
        gmixT = sb("gmixT_s", [128, 16], F32)
        gffnT = sb("gffnT_s", [128, 16], F32)
        goutT = sb("goutT_s", [128, 16], F32)
        gq = sb("gq_s", [128, 128], F32)
        gk = sb("gk_s", [128, 128], F32)
        gln = sb("gln_s", [128, 1024], F32)
        bln = sb("bln_s", [128, 1024], F32)
        sinkb = sb("sink_s", [128, 8], F32)
        nsink = sb("nsink_s", [128, 8], F32)
        wspT = sb("wspT_s", [128, 8, 128], BF16)
        bsp = sb("bsp_s", [128, 8], F32)
        msk = sb("msk_s", [128, 3, 384], F32)
        cs = sb("cs_s", [128, 6, 16], F32)
        sn = sb("sn_s", [128, 6, 16], F32)
        stats = [sb(f"stat{i}", [128, 32], F32) for i in range(9)]
        ropes = [sb(f"rope{i}", [128, 8, 2, 16], F32) for i in range(4)]
        bnsts = [sb(f"bnst{i}", [128, 2, 6], F32) for i in range(4)]
        rsA = sb("rsA", [128, 8], F32)
        rsA2 = sb("rsA2", [128, 8], F32)
        rsF = sb("rsF", [128, 4], F32)

        hT = big[:, 0:12288].rearrange("p (c t) -> p c t", t=TE)
        qT = big[:, 12288:16384].rearrange("p (h t) -> p h t", t=T)
        kT = big[:, 16384:17920].rearrange("p (h t) -> p h t", t=TE)
        vv = big[:, 17920:19456].rearrange("p (b n) -> p b n", n=256)
        vgn = big[:, 19456:23552].rearrange("p (b n) -> p b n", n=1024)
        uu = big[:, 23552:31744].bitcast(F32).rearrange("p (b n) -> p b n", n=1024)
        actT = big[:, :].rearrange("p (f t) -> p f t", t=T)
        smS, pbS, pTsS = [], [], []
        for i in range(2):
            o = i * 6144
            smS.append(big[:, o:o + 3072].bitcast(F32).rearrange("p (h k) -> p h k", k=384))
            pbS.append(big[:, o + 3072:o + 4608].rearrange("p (h k) -> p h k", k=384))
            pTsS.append(big[:, o + 4608:o + 6144].rearrange("p (j q) -> p j q", q=128))
        fxb = [xb0[:], big[:, 0:2048], big[:, 2048:4096]]
        mhf = mh[:].rearrange("p c t -> p (c t)")
        ystf = yst[:].rearrange("p a b -> p (a b)")
        xbS = [xb0[:], mhf[:, 0:2048], mhf[:, 2048:4096], ystf[:, 0:1024].bitcast(BF16)]
        tmpfS = [tmpf0[:], mhf[:, 4096:6144].bitcast(F32), mhf[:, 6144:8192].bitcast(F32), ystf[:, 1024:2048]]
        mixtok = [xst[:], ystf]
        xbD = [xb0[:], tmpf0[:].bitcast(BF16)]

        pmm = st.enter_context(nc.psum_tensor("pmm", [128, 8, 512], F32))
        TB = [pmm[:, 4:6, :].bitcast(BF16), pmm[:, 6:8, :].bitcast(BF16)]
        TB16 = [tb.rearrange("p a (c t) -> p (a c) t", t=128) for tb in TB]

        R_xres = [Res(f"xres{b}") for b in range(4)]
        R_xst = Res("xst")
        R_yst = Res("yst")
        R_mix = [R_xst, R_yst]
        R_xbS = [Res(f"xbS{i}") for i in range(4)]
        R_tmpfS = [Res(f"tmpfS{i}") for i in range(4)]
        R_fxb = [R_xbS[0], Res("fxb1"), Res("fxb2")]
        R_stat = [Res(f"stat{i}") for i in range(9)]
        R_rope = [Res(f"rope{i}") for i in range(4)]
        R_hT = [Res(f"hT{b}") for b in range(6)]
        R_qT = [Res(f"qT{b}") for b in range(4)]
        R_kT = [Res(f"kT{b}") for b in range(6)]
        R_v = [Res(f"v{b}") for b in range(6)]
        R_vgn = [Res(f"vgn{b}") for b in range(4)]
        R_u = [Res(f"u{b}") for b in range(4)]
        R_sm = [Res("sm0"), Res("sm1")]
        R_pb = [Res("pb0"), Res("pb1")]
        R_pTs = [Res("pTs0"), Res("pTs1")]
        R_actT = [Res(f"actT{g}") for g in range(16)]
        R_mh = [Res(f"mh{b}") for b in range(4)]
        R_wsl = [Res(f"wsl{i}") for i in range(NSLOT)]
        R_rl = [Res("rl0"), Res("rl1")]
        R_const = Res("const")
        R_task = Res("taskconst")
        R_bns = [Res(f"bn{i}") for i in range(4)]
        R_rsA = [Res(f"rsA{i}") for i in range(6)]
        R_rsF = [Res(f"rsF{i}") for i in range(4)]
        R_junkF = Res("junkF")
        R_bank = [Res(f"bank{i}") for i in range(8)]
        R_TB = [[R_bank[4], R_bank[5]], [R_bank[6], R_bank[7]]]
        arena_phase1 = R_hT + R_qT + R_kT + R_v + R_vgn + R_u
        dscratch_core = R_sm + R_pb + R_pTs
        fxb_res = [R_fxb[1], R_fxb[2], R_junkF]
        all_phase1 = arena_phase1 + dscratch_core + fxb_res
        mh_scratch = [R_xbS[1], R_xbS[2], R_tmpfS[1], R_tmpfS[2]]
        yst_scratch = [R_xbS[3], R_tmpfS[3]]

        d_const = DSem(P, "d_const")
        d_task = DSem(P, "d_task")
        d_xst = DSem(P, "d_xst")
        d_xres = [DSem(P, f"d_xres{b}") for b in range(4)]
        d_w = [DSem(P, f"d_w{i}") for i in range(NSLOT)]
        d_y = DSem(P, "d_y")
        d_yh = DSem(P, "d_yh")
        d_xc = [DSem(P, f"d_xc{i}") for i in range(4)]

        def xdst(eb):
            if eb in (0, 5):
                return xst[:], R_xst, d_xst
            return xres[:, eb - 1, :], R_xres[eb - 1], d_xres[eb - 1]

        NSET = 4
        SSTAT = [0, 1, 2, 8]

        sync, act, dve, pool, pe = nc.sync, nc.scalar, nc.vector, nc.gpsimd, nc.tensor

        for dst, src in ((gmixT, gmixT_d), (gffnT, gffnT_d), (goutT, goutT_d), (gq, gq_d),
                         (gk, gk_d), (gln, gln_d), (bln, bln_d), (sinkb, sink_d), (bsp, bsp_d)):
            P.dma("sp", d_const, lambda dst=dst, src=src: sync.dma_start(out=dst[:], in_=src[:, :]),
                  writes=[Res("c")])
        R_const.w = (d_const.key, d_const.count)
        d_wsp = DSem(P, "d_wsp")
        R_wsp = Res("wsp")
        P.dma("pool", d_wsp, lambda: pool.dma_start(out=wspT[:].rearrange("p h i -> p (h i)"), in_=wspT_d[:, :]),
              writes=[R_wsp])
        R_ident = Res("ident")
        P.op("pool", lambda: pool.memset(identf[:], 0.0), writes=[R_ident])
        P.op("pool", lambda: pool.affine_select(out=identf[:], in_=identf[:], pattern=[[-1, 128]],
                                                compare_op=ALU.not_equal, fill=1.0, base=0,
                                                channel_multiplier=1),
             reads=[R_ident], writes=[R_ident])
        P.op("dve", lambda: dve.tensor_copy(out=ident[:], in_=identf[:]), reads=[R_ident], writes=[R_ident])
        R_nsink = Res("nsink")
        P.op("dve", lambda: dve.tensor_scalar(out=nsink[:], in0=sinkb[:], scalar1=-1.0, scalar2=None,
                                              op0=ALU.mult), reads=[R_const], writes=[R_nsink])

        slabs = []
        for cg in (2, 0, 1, 3, 4, 5, 6):
            slabs.append(w_in_v[:, :, cg * 512:(cg + 1) * 512])
        for cg in range(4):
            slabs.append(w_out_v[:, :, cg * 512:(cg + 1) * 512])
        for fg in range(16):
            slabs.append(w_up_v[:, :, fg * 512:(fg + 1) * 512])
        for cg in range(4):
            for sub in range(4):
                slabs.append(w_down_v[sub][:, :, cg * 512:(cg + 1) * 512])
        n_per_task = len(slabs)
        total_slabs = n_per_task * n_tasks
        wstate = {"issued": 0}

        def ensure_loaded(upto):
            upto = min(upto, total_slabs - 1)
            while wstate["issued"] <= upto:
                n = wstate["issued"]
                s = n % NSLOT
                src = slabs[n % n_per_task]
                P.dma("pool", d_w[s], lambda s=s, src=src: pool.dma_start(out=wsl[s][:], in_=src),
                      writes=[R_wsl[s]])
                wstate["issued"] += 1

        def g_load(upto):
            ensure_loaded(upto)
            yield

        def rstd_chain(stt, rs, col, n, width=1):
            a = stt[:, col:col + width]
            P.op("dve", lambda: dve.tensor_scalar(out=a, in0=a, scalar1=1.0 / n, scalar2=EPS,
                                                  op0=ALU.mult, op1=ALU.add), reads=[rs], writes=[rs])
            yield
            P.op("act", lambda: act.activation(out=a, in_=a, func=AF.Ln), reads=[rs], writes=[rs])
            yield
            P.op("act", lambda: act.activation(out=a, in_=a, func=AF.Exp, scale=-0.5), reads=[rs], writes=[rs])
            yield

        def g_rms_to_T(src_ap, src_res, gT, dst_ap, dst_res, xbt, r_xb, junk, r_junk, stt, rs, tbname,
                       mode, col, out_res, cast_dma=None):
            if cast_dma == "done":
                pass
            elif cast_dma is not None:
                dsem, src_d = cast_dma
                P.dma("pool", dsem, lambda: pool.dma_start(out=xbt, in_=src_d), writes=[r_xb])
            else:
                P.op("act", lambda: act.activation(out=xbt, in_=src_ap, func=AF.Copy),
                     reads=[src_res], writes=[r_xb])
            yield
            P.op("act", lambda: act.activation(out=junk, in_=src_ap, func=AF.Square, accum_out=stt[:, 0:1]),
                 reads=[src_res], writes=[r_junk, rs])
            yield ("mark_src_read",)
            ti = int(tbname[-1])
            yield ("acq", f"bank{4 + 2 * ti}")
            yield ("acq", f"bank{5 + 2 * ti}")
            P.group("pe", [(lambda c=c: pe.transpose(out=TB16[ti][:, c, :], in_=xbt[:, c * 128:(c + 1) * 128],
                                                     identity=ident[:])) for c in range(16)],
                    reads=[r_xb, R_ident], writes=R_TB[ti])
            yield
            a = stt[:, 0:1]
            P.op("dve", lambda: dve.tensor_scalar(out=a, in0=a, scalar1=1.0 / D, scalar2=EPS,
                                                  op0=ALU.mult, op1=ALU.add), reads=[rs], writes=[rs])
            yield
            P.op("dve", lambda: dve.tensor_tensor(out=dst_ap, in0=TB16[ti],
                                                  in1=gT[:].unsqueeze(2).broadcast_to([128, 16, 128]),
                                                  op=ALU.mult),
                 reads=R_TB[ti] + [R_const], writes=[dst_res])
            yield ("rel", f"bank{4 + 2 * ti}")
            yield ("rel", f"bank{5 + 2 * ti}")
            if mode == "A":
                P.op("act", lambda: act.activation(out=a, in_=a, func=AF.Ln), reads=[rs], writes=[rs])
                yield
                P.op("act", lambda: act.activation(out=rsA[:, col:col + 1], in_=a, func=AF.Exp, scale=-0.5),
                     reads=[rs], writes=[out_res])
                yield
                P.op("dve", lambda: dve.tensor_tensor(out=rsA2[:, col:col + 1], in0=rsA[:, col:col + 1],
                                                      in1=rsA[:, col:col + 1], op=ALU.mult),
                     reads=[out_res], writes=[out_res])
                yield
            else:
                P.op("dve", lambda: dve.reciprocal(out=rsF[:, col:col + 1], in_=a), reads=[rs], writes=[out_res])
                yield

        def g_qk_chain(src3, src_res, H, g, eb, dstb, r_dst, tf, r_tf, stt, rs, rp, r_rp):
            tv = tf[:, 0:H * 128]
            tv3 = tv.rearrange("p (h d) -> p h d", d=128)
            P.op("act", lambda: act.activation(out=tv3, in_=src3, func=AF.Square), reads=[src_res], writes=[r_tf])
            yield
            P.op("dve", lambda: dve.tensor_reduce(out=stt[:, 8:8 + H], in_=tv3, axis=AX.X, op=ALU.add),
                 reads=[r_tf], writes=[rs])
            yield
            P.op("dve", lambda: dve.tensor_scalar(out=stt[:, 8:8 + H], in0=stt[:, 8:8 + H],
                                                  scalar1=rsA2[:, eb:eb + 1], scalar2=None, op0=ALU.mult),
                 reads=[rs, R_rsA[eb]], writes=[rs])
            yield
            yield from rstd_chain(stt, rs, 8, 128, H)
            P.op("dve", lambda: dve.tensor_scalar(out=stt[:, 8:8 + H], in0=stt[:, 8:8 + H],
                                                  scalar1=rsA[:, eb:eb + 1], scalar2=None, op0=ALU.mult),
                 reads=[rs, R_rsA[eb]], writes=[rs])
            yield
            P.op("dve", lambda: dve.tensor_tensor(out=tv3, in0=src3,
                                                  in1=stt[:, 8:8 + H].unsqueeze(2).broadcast_to([128, H, 128]),
                                                  op=ALU.mult), reads=[src_res, rs], writes=[r_tf])
            yield ("rel", "srcbank")
            P.op("dve", lambda: dve.tensor_tensor(out=tv3, in0=tv3,
                                                  in1=g[:].unsqueeze(1).broadcast_to([128, H, 128]),
                                                  op=ALU.mult), reads=[r_tf, R_const], writes=[r_tf])
            yield
            cosb = cs[:, eb, :].unsqueeze(1).broadcast_to([128, H, 16])
            sinb = sn[:, eb, :].unsqueeze(1).broadcast_to([128, H, 16])
            x1 = tv3[:, :, 0:16]
            x2 = tv3[:, :, 16:32]
            dst3 = dstb.rearrange("p (h d) -> p h d", d=128)
            r3 = rp[:, 0:H, :, :]
            P.op("act", lambda: act.activation(out=dstb, in_=tv, func=AF.Copy), reads=[r_tf], writes=[r_dst])
            yield
            for (a1, b1, a2, b2, opx, lo) in ((x1, cosb, x2, sinb, ALU.subtract, 0),
                                               (x2, cosb, x1, sinb, ALU.add, 16)):
                P.op("dve", lambda a1=a1, b1=b1: dve.tensor_tensor(out=r3[:, :, 0, :], in0=a1, in1=b1, op=ALU.mult),
                     reads=[r_tf, R_task], writes=[r_rp])
                yield
                P.op("dve", lambda a2=a2, b2=b2: dve.tensor_tensor(out=r3[:, :, 1, :], in0=a2, in1=b2, op=ALU.mult),
                     reads=[r_tf, R_task, r_rp], writes=[r_rp])
                yield
                P.op("dve", lambda opx=opx, lo=lo: dve.tensor_tensor(out=dst3[:, :, lo:lo + 16], in0=r3[:, :, 0, :],
                                                                     in1=r3[:, :, 1, :], op=opx),
                     reads=[r_rp], writes=[r_dst])
                yield

        def g_A_kv(t, eb, si_, slot):
            sset = eb % NSET
            xbt, r_xb = xbS[sset], R_xbS[sset]
            tf, r_tf = tmpfS[sset], R_tmpfS[sset]
            stt, rs = stats[SSTAT[sset]], R_stat[SSTAT[sset]]
            rp, r_rp = ropes[sset], R_rope[sset]
            yield ("acq", f"set{sset}")
            dst_ap, dres, dsem = xdst(eb)
            if eb in (0, 5):
                yield ("acq", "xst")
            if eb == 5:
                P.dma("sp", dsem, lambda: sync.dma_start(out=dst_ap, in_=xe[t, eb * 128:(eb + 1) * 128, :]),
                      writes=[dres])
                yield
            tbname = f"tb{eb % 2}"
            for r in g_rms_to_T(dst_ap, dres, gmixT, hT[:, :, eb * 128:(eb + 1) * 128], R_hT[eb],
                                xbt, r_xb, tf.bitcast(BF16), r_tf, stt, rs, tbname, "A", eb, R_rsA[eb],
                                cast_dma=("done" if (t, eb) in prefetched else
                                          (d_xc[sset], xe[t, eb * 128:(eb + 1) * 128, :]))):
                if r == ("mark_src_read",):
                    if eb in (0, 5):
                        yield ("rel", "xst")
                    else:
                        yield
                else:
                    yield r
            bk = eb % 4
            yield ("acq", f"bank{bk}")
            P.group("pe", [(lambda c=c: pe.matmul(out=pmm[:, bk, :], lhsT=hT[:, c, eb * 128:(eb + 1) * 128],
                                                  rhs=wsl[slot][:, c, :], start=(c == 0), stop=(c == 15)))
                           for c in range(16)],
                    reads=[R_hT[eb], R_wsl[slot]], writes=[R_bank[bk]])
            yield
            P.op("act", lambda: act.activation(out=vv[:, eb, :], in_=pmm[:, bk, 256:512], func=AF.Copy,
                                               scale=rsA[:, eb:eb + 1]),
                 reads=[R_bank[bk], R_rsA[eb]], writes=[R_v[eb]])
            yield
            src3 = pmm[:, bk, 0:256].rearrange("p (h d) -> p h d", d=128)
            for r in g_qk_chain(src3, R_bank[bk], 2, gk, eb, xbt[:, 0:256], r_xb, tf, r_tf, stt, rs, rp, r_rp):
                if r == ("rel", "srcbank"):
                    yield ("rel", f"bank{bk}")
                else:
                    yield r
            ti = eb % 2
            yield ("acq", f"bank{4 + 2 * ti}")
            P.group("pe", [(lambda h=h: pe.transpose(out=TB[ti][:, 0, h * 128:(h + 1) * 128],
                                                     in_=xbt[:, h * 128:(h + 1) * 128], identity=ident[:]))
                           for h in range(2)],
                    reads=[r_xb, R_ident], writes=[R_TB[ti][0]])
            yield
            P.op("dve", lambda: dve.tensor_copy(out=kT[:, :, eb * 128:(eb + 1) * 128],
                                                in_=TB[ti][:, 0, 0:256].rearrange("p (h t) -> p h t", t=128)),
                 reads=[R_TB[ti][0]], writes=[R_kT[eb]])
            yield ("rel", f"bank{4 + 2 * ti}")
            yield ("rel", f"set{sset}")

        def g_q(t, b, slot0, slot1, idx):
            eb = b + 1
            sset = (idx + 2) % NSET
            xbt, r_xb = xbS[sset], R_xbS[sset]
            tf, r_tf = tmpfS[sset], R_tmpfS[sset]
            stt, rs = stats[SSTAT[sset]], R_stat[SSTAT[sset]]
            rp, r_rp = ropes[sset], R_rope[sset]
            yield ("acq", f"set{sset}")
            pr = (idx % 2) * 2
            yield ("acq", f"bank{pr}")
            yield ("acq", f"bank{pr + 1}")
            for j, slot in ((0, slot0), (1, slot1)):
                P.group("pe", [(lambda c=c, j=j, slot=slot: pe.matmul(
                    out=pmm[:, pr + j, :], lhsT=hT[:, c, eb * 128:(eb + 1) * 128], rhs=wsl[slot][:, c, :],
                    start=(c == 0), stop=(c == 15))) for c in range(16)],
                    reads=[R_hT[eb], R_wsl[slot]], writes=[R_bank[pr + j]])
                yield
            src3 = pmm[:, pr:pr + 2, :].rearrange("p a (h d) -> p (a h) d", d=128)

            for r in g_qk_chain2(src3, [R_bank[pr], R_bank[pr + 1]], 8, gq, eb, xbt[:, 0:1024], r_xb,
                                 tf, r_tf, stt, rs, rp, r_rp):
                if r == ("rel", "srcbank"):
                    yield ("rel", f"bank{pr}")
                    yield ("rel", f"bank{pr + 1}")
                else:
                    yield r
            ti = idx % 2
            yield ("acq", f"bank{4 + 2 * ti}")
            P.group("pe", [(lambda h=h: pe.transpose(out=TB[ti][:, 0, h * 128:(h + 1) * 128],
                                                     in_=xbt[:, h * 128:(h + 1) * 128], identity=ident[:]))
                           for h in range(8)],
                    reads=[r_xb, R_ident], writes=[R_TB[ti][0]])
            yield
            P.op("dve", lambda: dve.tensor_copy(out=qT[:, :, b * 128:(b + 1) * 128],
                                                in_=TB[ti][:, 0, :].rearrange("p (h t) -> p h t", t=128)),
                 reads=[R_TB[ti][0]], writes=[R_qT[b]])
            yield ("rel", f"bank{4 + 2 * ti}")
            yield ("rel", f"set{sset}")

        def g_qk_chain2(src3, src_rl, H, g, eb, dstb, r_dst, tf, r_tf, stt, rs, rp, r_rp):
            tv = tf[:, 0:H * 128]
            tv3 = tv.rearrange("p (h d) -> p h d", d=128)
            P.op("act", lambda: act.activation(out=tv3, in_=src3, func=AF.Square), reads=src_rl, writes=[r_tf])
            yield
            P.op("dve", lambda: dve.tensor_reduce(out=stt[:, 8:8 + H], in_=tv3, axis=AX.X, op=ALU.add),
                 reads=[r_tf], writes=[rs])
            yield
            P.op("dve", lambda: dve.tensor_scalar(out=stt[:, 8:8 + H], in0=stt[:, 8:8 + H],
                                                  scalar1=rsA2[:, eb:eb + 1], scalar2=None, op0=ALU.mult),
                 reads=[rs, R_rsA[eb]], writes=[rs])
            yield
            yield from rstd_chain(stt, rs, 8, 128, H)
            P.op("dve", lambda: dve.tensor_scalar(out=stt[:, 8:8 + H], in0=stt[:, 8:8 + H],
                                                  scalar1=rsA[:, eb:eb + 1], scalar2=None, op0=ALU.mult),
                 reads=[rs, R_rsA[eb]], writes=[rs])
            yield
            P.op("dve", lambda: dve.tensor_tensor(out=tv3, in0=src3,
                                                  in1=stt[:, 8:8 + H].unsqueeze(2).broadcast_to([128, H, 128]),
                                                  op=ALU.mult), reads=src_rl + [rs], writes=[r_tf])
            yield ("rel", "srcbank")
            P.op("dve", lambda: dve.tensor_tensor(out=tv3, in0=tv3,
                                                  in1=g[:].unsqueeze(1).broadcast_to([128, H, 128]),
                                                  op=ALU.mult), reads=[r_tf, R_const], writes=[r_tf])
            yield
            cosb = cs[:, eb, :].unsqueeze(1).broadcast_to([128, H, 16])
            sinb = sn[:, eb, :].unsqueeze(1).broadcast_to([128, H, 16])
            x1 = tv3[:, :, 0:16]
            x2 = tv3[:, :, 16:32]
            dst3 = dstb.rearrange("p (h d) -> p h d", d=128)
            r3 = rp[:, 0:H, :, :]
            P.op("act", lambda: act.activation(out=dstb, in_=tv, func=AF.Copy), reads=[r_tf], writes=[r_dst])
            yield
            for (a1, b1, a2, b2, opx, lo) in ((x1, cosb, x2, sinb, ALU.subtract, 0),
                                               (x2, cosb, x1, sinb, ALU.add, 16)):
                P.op("dve", lambda a1=a1, b1=b1: dve.tensor_tensor(out=r3[:, :, 0, :], in0=a1, in1=b1, op=ALU.mult),
                     reads=[r_tf, R_task], writes=[r_rp])
                yield
                P.op("dve", lambda a2=a2, b2=b2: dve.tensor_tensor(out=r3[:, :, 1, :], in0=a2, in1=b2, op=ALU.mult),
                     reads=[r_tf, R_task, r_rp], writes=[r_rp])
                yield
                P.op("dve", lambda opx=opx, lo=lo: dve.tensor_tensor(out=dst3[:, :, lo:lo + 16], in0=r3[:, :, 0, :],
                                                                     in1=r3[:, :, 1, :], op=opx),
                     reads=[r_rp], writes=[r_dst])
                yield

        def g_u(b, j, slot, i):
            eb = b + 1
            bk5 = i % 4
            yield ("acq", f"bank{bk5}")
            P.group("pe", [(lambda c=c: pe.matmul(out=pmm[:, bk5, :], lhsT=hT[:, c, eb * 128:(eb + 1) * 128],
                                                  rhs=wsl[slot][:, c, :], start=(c == 0), stop=(c == 15)))
                           for c in range(16)],
                    reads=[R_hT[eb], R_wsl[slot]], writes=[R_bank[bk5]])
            yield
            P.op("act", lambda: act.activation(out=uu[:, b, j * 512:(j + 1) * 512], in_=pmm[:, bk5, :],
                                               func=AF.Gelu_apprx_tanh, scale=rsA[:, eb:eb + 1]),
                 reads=[R_bank[bk5], R_rsA[eb]], writes=[R_u[b]])
            yield ("rel", f"bank{bk5}")

        def g_vg(b, slot0, slot1, i):
            eb = b + 1
            sset = i % NSET
            stt, rs = stats[SSTAT[sset]], R_stat[SSTAT[sset]]
            tf, r_tf = tmpfS[sset], R_tmpfS[sset]
            bnst, R_bn = bnsts[sset], R_bns[sset]
            yield ("acq", f"set{sset}")
            for j, slot in ((0, slot0), (1, slot1)):
                bk5 = (2 * i + j) % 4
                yield ("acq", f"bank{bk5}")
                P.group("pe", [(lambda c=c, slot=slot, bk5=bk5: pe.matmul(
                    out=pmm[:, bk5, :], lhsT=hT[:, c, eb * 128:(eb + 1) * 128], rhs=wsl[slot][:, c, :],
                    start=(c == 0), stop=(c == 15))) for c in range(16)],
                    reads=[R_hT[eb], R_wsl[slot]], writes=[R_bank[bk5]])
                yield
                P.op("act", lambda j=j, bk5=bk5: act.activation(out=tf[:, j * 512:(j + 1) * 512], in_=pmm[:, bk5, :],
                                                       func=AF.Gelu_apprx_tanh, scale=rsA[:, eb:eb + 1]),
                     reads=[R_bank[bk5], R_rsA[eb]], writes=[r_tf])
                yield ("rel", f"bank{bk5}")
                P.op("dve", lambda j=j: dve.bn_stats(out=bnst[:, j, :], in_=tf[:, j * 512:(j + 1) * 512]),
                     reads=[r_tf], writes=[R_bn])
                yield
            P.op("dve", lambda: dve.bn_aggr(out=stt[:, 16:18], in_=bnst[:].rearrange("p a b -> p (a b)")),
                 reads=[R_bn], writes=[rs])
            yield
            P.op("dve", lambda: dve.tensor_scalar(out=stt[:, 17:18], in0=stt[:, 17:18], scalar1=EPS, scalar2=None,
                                                  op0=ALU.add), reads=[rs], writes=[rs])
            yield
            P.op("act", lambda: act.activation(out=stt[:, 17:18], in_=stt[:, 17:18], func=AF.Ln),
                 reads=[rs], writes=[rs])
            yield
            P.op("act", lambda: act.activation(out=stt[:, 17:18], in_=stt[:, 17:18], func=AF.Exp, scale=-0.5),
                 reads=[rs], writes=[rs])
            yield
            P.op("dve", lambda: dve.tensor_scalar(out=tf[:], in0=tf[:], scalar1=stt[:, 16:17],
                                                  scalar2=stt[:, 17:18], op0=ALU.subtract, op1=ALU.mult),
                 reads=[r_tf, rs], writes=[r_tf])
            yield
            P.op("dve", lambda: dve.tensor_tensor(out=tf[:], in0=tf[:], in1=gln[:], op=ALU.mult),
                 reads=[r_tf, R_const], writes=[r_tf])
            yield
            P.op("dve", lambda: dve.tensor_tensor(out=vgn[:, b, :], in0=tf[:], in1=bln[:], op=ALU.add),
                 reads=[r_tf, R_const], writes=[R_vgn[b]])
            yield ("rel", f"set{sset}")

        def g_att(b, kvg, unit):
            eb = b + 1
            ds = unit % 2
            sm_, pb_, pT_ = smS[ds], pbS[ds], pTsS[ds]
            sti = 3 + unit % 4
            stt, rs = stats[sti], R_stat[sti]
            kind = 0 if b == 0 else (2 if b == 3 else 1)
            mt, r_mt = mixtok[b % 2], R_mix[b % 2]
            yield ("acq", f"st{sti}")
            yield ("acq", f"sm{ds}")
            for k_ in range(4):
                yield ("acq", f"bank{k_}")
            P.group("pe", [(lambda hh=hh: pe.matmul(
                out=pmm[:, hh, 0:384], lhsT=qT[:, 4 * kvg + hh, b * 128:(b + 1) * 128],
                rhs=kT[:, kvg, (eb - 1) * 128:(eb + 2) * 128], start=True, stop=True)) for hh in range(4)],
                reads=[R_qT[b], R_kT[eb - 1], R_kT[eb], R_kT[eb + 1]], writes=R_bank[0:4])
            yield
            P.op("dve", lambda: dve.scalar_tensor_tensor(
                out=sm_, in0=pmm[:, 0:4, 0:384], scalar=SCALE,
                in1=msk[:, kind, :].unsqueeze(1).broadcast_to([128, 4, 384]), op0=ALU.mult, op1=ALU.add),
                reads=R_bank[0:4] + [R_task], writes=[R_sm[ds]])
            for k_ in range(4):
                yield ("rel", f"bank{k_}")
            P.op("dve", lambda: dve.tensor_reduce(out=stt[:, 0:4], in_=sm_, axis=AX.X, op=ALU.max),
                 reads=[R_sm[ds]], writes=[rs])
            yield
            P.op("dve", lambda: dve.scalar_tensor_tensor(out=stt[:, 4:8], in0=stt[:, 0:4], scalar=-1.0,
                                                         in1=nsink[:, 4 * kvg:4 * kvg + 4], op0=ALU.mult,
                                                         op1=ALU.min),
                 reads=[rs, R_nsink], writes=[rs])
            yield
            yield ("acq", f"pb{ds}")
            for hh in range(4):
                P.op("act", lambda hh=hh: act.activation(out=pb_[:, hh, :], in_=sm_[:, hh, :], func=AF.Exp,
                                                         bias=stt[:, 4 + hh:5 + hh], scale=1.0,
                                                         accum_out=stt[:, 8 + hh:9 + hh]),
                     reads=[R_sm[ds], rs], writes=[R_pb[ds], rs])
                yield
            yield ("rel", f"sm{ds}")
            P.op("dve", lambda: dve.tensor_tensor(out=stt[:, 12:16], in0=sinkb[:, 4 * kvg:4 * kvg + 4],
                                                  in1=stt[:, 4:8], op=ALU.add), reads=[rs, R_const], writes=[rs])
            yield
            P.op("act", lambda: act.activation(out=stt[:, 12:16], in_=stt[:, 12:16], func=AF.Exp),
                 reads=[rs], writes=[rs])
            yield
            P.op("dve", lambda: dve.tensor_tensor(out=stt[:, 16:20], in0=stt[:, 8:12], in1=stt[:, 12:16],
                                                  op=ALU.add), reads=[rs], writes=[rs])
            yield
            P.op("dve", lambda: dve.reciprocal(out=stt[:, 20:24], in_=stt[:, 16:20]), reads=[rs], writes=[rs])
            yield
            yield ("acq", "bank6")
            yield ("acq", "bank7")
            tbf = TB16[1]
            P.group("pe", [(lambda hh=hh, kb=kb: pe.transpose(
                out=tbf[:, hh * 3 + kb, :], in_=pb_[:, hh, kb * 128:(kb + 1) * 128], identity=ident[:]))
                for hh in range(4) for kb in range(3)],
                reads=[R_pb[ds], R_ident], writes=R_TB[1])
            yield ("rel", f"pb{ds}")
            yield ("acq", f"pT{ds}")
            P.op("act", lambda: act.activation(out=pT_, in_=tbf[:, 0:12, :], func=AF.Copy),
                 reads=R_TB[1], writes=[R_pTs[ds]])
            yield ("rel", "bank6")
            yield ("rel", "bank7")
            yield ("acq", "bank4")
            P.group("pe", [(lambda hh=hh, kb=kb: pe.matmul(
                out=pmm[:, 4, hh * 128:(hh + 1) * 128], lhsT=pT_[:, hh * 3 + kb, :],
                rhs=vv[:, eb - 1 + kb, kvg * 128:(kvg + 1) * 128], start=(kb == 0), stop=(kb == 2)))
                for hh in range(4) for kb in range(3)],
                reads=[R_pTs[ds], R_v[eb - 1], R_v[eb], R_v[eb + 1]], writes=[R_bank[4]])
            yield ("rel", f"pT{ds}")
            P.op("dve", lambda: dve.tensor_tensor(
                out=mt[:, kvg * 512:(kvg + 1) * 512].rearrange("p (h d) -> p h d", d=128),
                in0=pmm[:, 4, :].rearrange("p (h d) -> p h d", d=128),
                in1=stt[:, 20:24].unsqueeze(2).broadcast_to([128, 4, 128]), op=ALU.mult),
                reads=[R_bank[4], rs], writes=[r_mt])
            yield ("rel", "bank4")
            yield ("rel", f"st{sti}")

        def g_gmn(b):
            mt, r_mt = mixtok[b % 2], R_mix[b % 2]
            stt, rs = stats[7], R_stat[7]
            xbt, r_xb = xbD[b % 2], (R_xbS[0] if b % 2 == 0 else R_tmpfS[0])
            yield ("acq", "bank0")
            yield ("acq", "bank1")
            P.group("pe", [(lambda h=h: pe.matmul(
                out=pmm[:, h // 4, (h % 4) * 128:(h % 4 + 1) * 128], lhsT=wspT[:, h, :],
                rhs=vgn[:, b, h * 128:(h + 1) * 128], start=True, stop=True)) for h in range(8)],
                reads=[R_wsp, R_vgn[b]], writes=[R_bank[0], R_bank[1]])
            yield
            for h in range(8):
                P.op("dve", lambda h=h: dve.scalar_tensor_tensor(
                    out=mt[:, 1024 + h * 128:1024 + (h + 1) * 128],
                    in0=pmm[:, h // 4, (h % 4) * 128:(h % 4 + 1) * 128], scalar=bsp[:, h:h + 1],
                    in1=uu[:, b, h * 128:(h + 1) * 128], op0=ALU.add, op1=ALU.mult),
                    reads=[R_bank[h // 4], R_const, R_u[b]], writes=[r_mt])
                yield
            yield ("rel", "bank0")
            yield ("rel", "bank1")
            yield ("acq", f"xbD{b % 2}")
            yield ("acq", "st7")
            for half in range(2):
                src = mt[:, half * 1024:(half + 1) * 1024]
                dstb = xbt[:, half * 1024:(half + 1) * 1024]
                P.op("act", lambda src=src, dstb=dstb: act.activation(out=dstb, in_=src, func=AF.Square,
                                                                      accum_out=stt[:, half:half + 1]),
                     reads=[r_mt], writes=[r_xb, rs])
                yield
            yield from rstd_chain(stt, rs, 0, 1024, 2)
            for half in range(2):
                src = mt[:, half * 1024:(half + 1) * 1024]
                dstb = xbt[:, half * 1024:(half + 1) * 1024]
                P.op("act", lambda src=src, dstb=dstb, half=half: act.activation(
                    out=dstb, in_=src, func=AF.Copy, scale=stt[:, half:half + 1]),
                    reads=[r_mt, rs], writes=[r_xb])
                yield
            yield ("rel", "st7")
            yield ("mark", f"m{b}")
            yield ("acq", "bank4")
            yield ("acq", "bank5")
            P.group("pe", [(lambda c=c: pe.transpose(out=TB16[0][:, c, :], in_=xbt[:, c * 128:(c + 1) * 128],
                                                     identity=ident[:])) for c in range(16)],
                    reads=[r_xb, R_ident], writes=R_TB[0])
            yield ("rel", f"xbD{b % 2}")
            P.op("dve", lambda: dve.tensor_tensor(out=mh[:, :, b * 128:(b + 1) * 128], in0=TB16[0],
                                                  in1=goutT[:].unsqueeze(2).broadcast_to([128, 16, 128]),
                                                  op=ALU.mult),
                 reads=R_TB[0] + [R_const], writes=[R_mh[b]])
            yield ("rel", "bank4")
            yield ("rel", "bank5")

        def g_F(b, i):
            xbt, r_xb = fxb[i % 3], R_fxb[i % 3]
            yield ("acq", f"fxb{i % 3}")
            for r in g_rms_to_T(xres[:, b, :], R_xres[b], gffnT, mh[:, :, b * 128:(b + 1) * 128], R_mh[b],
                                xbt, r_xb, big[:, 4096:6144], R_junkF, stats[i % 3], R_stat[i % 3],
                                f"tb{i % 2}", "F", b, R_rsF[b]):
                yield (None if r == ("mark_src_read",) else r)
            yield ("rel", f"fxb{i % 3}")

        prefetched = set()

        def issue_casts(tt, ebs):
            for eb in ebs:
                sset = eb % NSET
                P.dma("pool", d_xc[sset], lambda sset=sset, eb=eb: pool.dma_start(
                    out=xbS[sset], in_=xe[tt, eb * 128:(eb + 1) * 128, :]), writes=[R_xbS[sset]])
                prefetched.add((tt, eb))

        bank_rot = {"i": 0}

        def next_bank(n=4):
            b = bank_rot["i"]
            bank_rot["i"] = (b + 1) % n
            return b

        try:
          ckpt("const")
          for t in range(n_tasks):
            base = t * n_per_task
            alias_edges(R_mh, mh_scratch)
            alias_edges([R_yst] + R_rl, yst_scratch)
            issue_casts(t, [0, 1, 2, 3])
            ensure_loaded(base + 2)

            for i_, (dst, src) in enumerate(((msk, msk_d), (cs, cs_d), (sn, sn_d))):
                P.dma("sp", d_task, lambda dst=dst, src=src: sync.dma_start(
                    out=dst[:].rearrange("p a b -> p (a b)"), in_=src[t]),
                    writes=[R_task if i_ == 0 else Res("tc")])
            R_task.w = (d_task.key, d_task.count)
            for eb in range(5):
                dst_ap, dres, dsem = xdst(eb)
                P.dma("sp", dsem, lambda dst_ap=dst_ap, eb=eb: sync.dma_start(
                    out=dst_ap, in_=xe[t, eb * 128:(eb + 1) * 128, :]), writes=[dres])

            alias_edges(R_mh, mh_scratch)
            alias_edges([R_yst] + R_rl, yst_scratch)
            alias_edges(R_actT, arena_phase1)

            s_kv, s_q0, s_q1 = base % NSLOT, (base + 1) % NSLOT, (base + 2) % NSLOT
            s_u = [(base + 3) % NSLOT, (base + 4) % NSLOT]
            s_vg = [(base + 5) % NSLOT, (base + 6) % NSLOT]
            allA = [f"A{eb}" for eb in range(6)]
            allq = [f"q{b}" for b in range(4)]
            items = [Item(f"A{eb}", g_A_kv(t, eb, 0, s_kv)) for eb in range(6)]
            items += [Item(f"q{b}", g_q(t, b, s_q0, s_q1, b), after=[f"A{b + 1}"]) for b in range(4)]
            items.append(Item("ld_u0", g_load(base + 3), after=allA))
            items += [Item(f"u0{b}", g_u(b, 0, s_u[0], b)) for b in range(4)]
            items.append(Item("ld_u1", g_load(base + 5), after=allq))
            items += [Item(f"u1{b}", g_u(b, 1, s_u[1], b)) for b in range(4)]
            items.append(Item("ld_vg1", g_load(base + 6), after=[f"u0{b}" for b in range(4)]))
            items += [Item(f"vg{b}", g_vg(b, s_vg[0], s_vg[1], b)) for b in range(4)]
            run_pipeline(items, width=4, stagger=7)
            ckpt("AB")

            ensure_loaded(base + 7)
            alias_edges(R_hT, dscratch_core)
            alias_edges(mh_scratch, R_mh)
            alias_edges(yst_scratch, [R_yst])
            items = []
            unit = 0
            for b in range(4):
                for kvg in range(2):
                    items.append(Item(f"a{b}{kvg}", g_att(b, kvg, unit), after=([f"m{b - 2}"] if b >= 2 else [])))
                    unit += 1
                if b >= 1:
                    items.append(Item(f"g{b - 1}", g_gmn(b - 1), after=[f"a{b - 1}0", f"a{b - 1}1"]))
            items.append(Item("g3", g_gmn(3), after=["a30", "a31"]))
            run_pipeline(items, width=6, stagger=3)
            ckpt("D")
            si = 7

            bank_rot["i"] = 0
            for cg in range(4):
                n = base + si
                ensure_loaded(n + 2)
                s = n % NSLOT
                for b in range(4):
                    bk = next_bank()
                    P.group("pe", [(lambda c=c, b=b, s=s, bk=bk: pe.matmul(
                        out=pmm[:, bk, :], lhsT=mh[:, c, b * 128:(b + 1) * 128], rhs=wsl[s][:, c, :],
                        start=(c == 0), stop=(c == 15))) for c in range(16)],
                        reads=[R_mh[b], R_wsl[s]], writes=[R_bank[bk]])
                    P.op("dve", lambda b=b, cg=cg, bk=bk: dve.tensor_tensor(
                        out=xres[:, b, cg * 512:(cg + 1) * 512], in0=pmm[:, bk, :],
                        in1=xres[:, b, cg * 512:(cg + 1) * 512], op=ALU.add),
                        reads=[R_bank[bk], R_xres[b]], writes=[R_xres[b]])
                si += 1
            ckpt("E")

            alias_edges(dscratch_core, fxb_res)
            items = [Item(f"F{b}", g_F(b, b)) for b in range(4)]
            run_pipeline(items, width=3, stagger=4)
            ckpt("F")

            bank_rot["i"] = 0
            for fg in range(16):
                n = base + si
                ensure_loaded(n + 2)
                s = n % NSLOT
                for fi in range(4):
                    fc = fg * 4 + fi
                    bk = next_bank(2)
                    P.group("pe", [(lambda c=c, fi=fi, s=s, bk=bk: pe.matmul(
                        out=pmm[:, bk, :], lhsT=wsl[s][:, c, fi * 128:(fi + 1) * 128], rhs=mh[:, c, :],
                        start=(c == 0), stop=(c == 15))) for c in range(16)],
                        reads=R_mh + [R_wsl[s]], writes=[R_bank[bk]])
                    rl = yst[:, bk, :]
                    P.op("act", lambda bk=bk, rl=rl: act.activation(out=rl, in_=pmm[:, bk, :], func=AF.Relu),
                         reads=[R_bank[bk]], writes=[R_rl[bk], R_yst])
                    extra = all_phase1 if (fg == 0 and fi == 0) else []
                    P.op("dve", lambda fc=fc, rl=rl: dve.tensor_tensor(out=actT[:, fc, :], in0=rl, in1=rl,
                                                                       op=ALU.mult),
                         reads=[R_rl[bk]], writes=[R_actT[fg]] + extra)
                si += 1
            ckpt("G")

            for cg in range(4):
                for sub in range(4):
                    n = base + si
                    ensure_loaded(n + 2)
                    s = n % NSLOT
                    for b in range(4):
                        P.group("pe", [(lambda fc=fc, b=b, s=s, sub=sub: pe.matmul(
                            out=pmm[:, 2 + b, :], lhsT=actT[:, sub * 16 + fc, b * 128:(b + 1) * 128],
                            rhs=wsl[s][:, fc, :], start=(sub == 0 and fc == 0),
                            stop=(sub == 3 and fc == 15))) for fc in range(16)],
                            reads=[R_actT[sub * 4 + k] for k in range(4)] + [R_wsl[s]],
                            writes=[R_bank[2 + b]])
                        if sub == 3:
                            P.op("dve", lambda b=b, cg=cg: dve.scalar_tensor_tensor(
                                out=yst[:, b, :], in0=pmm[:, 2 + b, :], scalar=rsF[:, b:b + 1],
                                in1=xres[:, b, cg * 512:(cg + 1) * 512], op0=ALU.mult, op1=ALU.add),
                                reads=[R_bank[2 + b], R_xres[b], R_rsF[b]],
                                writes=[R_yst] + ([R_rl[b]] if b < 2 else []))
                    si += 1
                P.dma("sp", d_y, lambda cg=cg: sync.dma_start(
                    out=y[t].rearrange("(b p) n -> p b n", p=128)[:, :, cg * 512:(cg + 1) * 512],
                    in_=yst[:]), reads=[R_yst])
            assert si == n_per_task
        except _Stop:
            pass
        for e in ("pe", "act", "dve", "pool"):
            if P.cnt[e]:
                sync.wait_ge(P.semh[P.sem[e]], P.cnt[e])
        P.wait_all("sp", [d_const, d_task, d_xst, d_yh, d_wsp] + d_xres + d_w + d_xc + [d_y])
    return nc


def _task_table():
    tasks = []
    for s in range(2):
        for j in range(8):
            tasks.append(("p", s, j * T))
    for s in range(4):
        for j in range(4):
            tasks.append(("s", s, j * T))
    return tasks


def _rope_tables(pos):
    inv_freq = 500000.0 ** (-np.arange(0, 32, 2, dtype=np.float64) / 32.0)
    ang = pos.astype(np.float64)[:, None] * inv_freq[None, :]
    return np.cos(ang).astype(np.float32), np.sin(ang).astype(np.float32)


_NC_CACHE = {}


def kernel(x_prompt, x_sample, g_mix, w_in, g_q, g_k, sink, g_v_ln, b_v_ln, w_spatial,
           b_spatial, g_attn_out, g_gmlp_out, w_out, g_ffn, w_up, w_down):
    f32 = np.float32
    x_prompt = np.asarray(x_prompt, f32)
    x_sample = np.asarray(x_sample, f32)
    tasks = _task_table()

    def colT(v):
        return np.ascontiguousarray(np.asarray(v, f32).reshape(16, 128).T)

    def bcast(v, n):
        return np.ascontiguousarray(np.broadcast_to(np.asarray(v, f32).reshape(1, n), (128, n)))

    shared = {
        "w_in": np.ascontiguousarray(np.asarray(w_in, f32)[0]),
        "w_out": np.ascontiguousarray(np.asarray(w_out, f32)[0]),
        "w_up": np.ascontiguousarray(np.asarray(w_up, f32)[0]),
        "w_down": np.ascontiguousarray(np.asarray(w_down, f32)[0]),
        "gmixT": colT(g_mix[0]),
        "gffnT": colT(g_ffn[0]),
        "goutT": colT(np.concatenate([np.asarray(g_attn_out, f32)[0], np.asarray(g_gmlp_out, f32)[0]])),
        "gq": bcast(g_q[0], 128),
        "gk": bcast(g_k[0], 128),
        "gln": bcast(g_v_ln[0], 1024),
        "bln": bcast(b_v_ln[0], 1024),
        "sinkb": bcast(sink[0], 8),
        "wspT": np.ascontiguousarray(np.transpose(np.asarray(w_spatial, f32)[0], (2, 0, 1)).reshape(128, 1024)),
        "bsp": np.ascontiguousarray(np.asarray(b_spatial, f32)[0].T),
    }

    qi = np.arange(128)[:, None]
    kj = np.arange(384)[None, :]
    band = (kj >= qi) & (kj <= qi + 256)

    in_maps = []
    for c in range(N_CORES):
        xe = np.zeros((NT, TE, D), f32)
        msk = np.empty((NT, 128, 3, 384), f32)
        cs = np.empty((NT, 128, 6, 16), f32)
        sn = np.empty((NT, 128, 6, 16), f32)
        for i in range(NT):
            grp, s, start = tasks[c * NT + i]
            xs = x_prompt[s] if grp == "p" else x_sample[s]
            S = xs.shape[0]
            lo, hi = start - 128, start + T + 128
            a, b = max(lo, 0), min(hi, S)
            xe[i, a - lo:b - lo] = xs[a:b]
            left_ok = lo >= 0
            right_ok = hi <= S
            m0 = band if left_ok else (band & (kj >= 128))
            m2 = band if right_ok else (band & (kj < 256))
            for kind, m in enumerate((m0, band, m2)):
                msk[i, :, kind, :] = np.where(m, 0.0, NEG).astype(f32)
            pos = np.arange(lo, hi)
            co, si_ = _rope_tables(pos)
            cs[i] = co.reshape(6, 128, 16).transpose(1, 0, 2)
            sn[i] = si_.reshape(6, 128, 16).transpose(1, 0, 2)
        m = dict(shared)
        m["xe"] = xe
        m["msk"] = msk.reshape(NT, 128, 3 * 384)
        m["cs"] = cs.reshape(NT, 128, 96)
        m["sn"] = sn.reshape(NT, 128, 96)
        in_maps.append(m)

    if "nc" not in _NC_CACHE:
        _NC_CACHE["nc"] = build_program()
    nc = _NC_CACHE["nc"]
    res = run_bass_kernel_spmd(nc, in_maps, core_ids=list(range(N_CORES)))

    y_prompt = np.empty((2, 4096, D), f32)
    y_sample = np.empty((4, 2048, D), f32)
    for c in range(N_CORES):
        yc = np.asarray(res.results[c]["y"], f32)
        for i in range(NT):
            grp, s, start = tasks[c * NT + i]
            if grp == "p":
                y_prompt[s, start:start + T] = yc[i]
            else:
                y_sample[s, start:start + T] = yc[i]
    return (y_prompt, y_sample)
```

```python
import contextlib
import math
import numpy as np
import concourse.bass as bass
import concourse.mybir as mybir
from concourse.bass_utils import run_bass_kernel_spmd

F32 = mybir.dt.float32
BF16 = mybir.dt.bfloat16
AF = mybir.ActivationFunctionType
ALU = mybir.AluOpType
AX = mybir.AxisListType

N_CORES = 8
NT = 4
T = 512
TE = 768
D = 2048
DFF = 8192
EPS = 1e-6
NEG = -1e30
SCALE = 1.0 / math.sqrt(128.0)
NSLOT = 3


class Res:
    __slots__ = ("name", "w", "r")

    def __init__(self, name):
        self.name = name
        self.w = None
        self.r = {}


class DSem:
    def __init__(self, prog, name):
        self.h = prog.new_sem(name)
        self.key = name
        self.count = 0


class Prog:
    ENGS = ("pe", "act", "dve", "pool", "sp")

    def __init__(self, nc, stack):
        self.nc = nc
        self.stack = stack
        self.eng = {"pe": nc.tensor, "act": nc.scalar, "dve": nc.vector,
                    "pool": nc.gpsimd, "sp": nc.sync}
        self.semh = {}
        self.sem = {}
        self.cnt = {}
        self.waited = {e: {} for e in self.ENGS}
        for e in self.ENGS:
            self.sem[e] = "E_" + e
            self.semh["E_" + e] = stack.enter_context(nc.semaphore("E_" + e))
            self.cnt[e] = 0

    def new_sem(self, name):
        h = self.stack.enter_context(self.nc.semaphore(name))
        self.semh[name] = h
        return h

    def _emit_waits(self, e, reads, writes):
        need = {}

        def add(pt):
            if pt is None:
                return
            k, v = pt
            if need.get(k, 0) < v:
                need[k] = v
        for r in reads:
            add(r.w)
        for w in writes:
            add(w.w)
            for k, v in w.r.items():
                add((k, v))
        own = self.sem[e]
        for k, v in need.items():
            if e == "pe" and k == own:
                continue
            if self.waited[e].get(k, 0) >= v:
                continue
            self.eng[e].wait_ge(self.semh[k], v)
            self.waited[e][k] = v

    def _commit(self, pt, reads, writes):
        k, v = pt
        for r in reads:
            if r.r.get(k, 0) < v:
                r.r[k] = v
        for w in writes:
            w.w = pt
            w.r = {}

    def op(self, e, emit, reads=(), writes=()):
        self._emit_waits(e, reads, writes)
        ins = emit()
        self.cnt[e] += 1
        assert self.cnt[e] < 60000
        ins.then_inc(self.semh[self.sem[e]], 1)
        self._commit((self.sem[e], self.cnt[e]), reads, writes)

    def group(self, e, emits, reads=(), writes=()):
        self._emit_waits(e, reads, writes)
        ins = None
        for f in emits:
            ins = f()
        self.cnt[e] += 1
        ins.then_inc(self.semh[self.sem[e]], 1)
        self._commit((self.sem[e], self.cnt[e]), reads, writes)

    def dma(self, q, dsem, emit, reads=(), writes=()):
        self._emit_waits(q, reads, writes)
        ins = emit()
        dsem.count += 16
        ins.then_inc(dsem.h, 16)
        self._commit((dsem.key, dsem.count), reads, writes)

    def wait_all(self, e, dsems):
        for d in dsems:
            if d.count:
                self.eng[e].wait_ge(d.h, d.count)


class _Stop(Exception):
    pass


def alias_edges(src, dst):
    for d in dst:
        for s_ in src:
            if s_.w is not None:
                k, v = s_.w
                if d.r.get(k, 0) < v:
                    d.r[k] = v
            for k, v in s_.r.items():
                if d.r.get(k, 0) < v:
                    d.r[k] = v


class Item:
    def __init__(self, name, gen, after=()):
        self.name = name
        self.gen = gen
        self.after = tuple(after)


def run_pipeline(items, width, stagger):
    held = {}
    done = set()
    active = []
    nxt = 0
    rnd = 0
    last_start = -10 ** 9
    while active or nxt < len(items):
        if nxt < len(items) and len(active) < width and (rnd - last_start >= stagger or not active):
            it = items[nxt]
            if all(a in done for a in it.after):
                active.append([it, None])
                nxt += 1
                last_start = rnd
            else:
                assert active, f"pipeline stuck at {it.name}"
        progressed = False
        for ent in list(active):
            it, pending = ent
            if pending is not None:
                if pending in held:
                    continue
                held[pending] = it.name
                ent[1] = None
                progressed = True
                continue
            try:
                r = next(it.gen)
            except StopIteration:
                active.remove(ent)
                done.add(it.name)
                for k in [k for k, v in held.items() if v == it.name]:
                    del held[k]
                progressed = True
                continue
            progressed = True
            if isinstance(r, tuple):
                if r[0] == "acq":
                    if r[1] in held:
                        ent[1] = r[1]
                    else:
                        held[r[1]] = it.name
                elif r[0] == "rel":
                    assert held.get(r[1]) == it.name, (r, held)
                    del held[r[1]]
                elif r[0] == "mark":
                    done.add(r[1])
        assert progressed or nxt < len(items), "pipeline deadlock"
        rnd += 1


def build_program(n_tasks=NT, stop_at=None):
    nc = bass.Bass("TRN2", target_bir_lowering=False)

    def ckpt(name):
        if stop_at is not None and name == stop_at:
            raise _Stop()

    def din(name, shape):
        return nc.dram_tensor(name, shape, F32, kind="ExternalInput").ap()

    xe = din("xe", [NT, TE, D])
    y = nc.dram_tensor("y", [NT, T, D], F32, kind="ExternalOutput").ap()
    w_in = din("w_in", [D, 3584])
    w_out = din("w_out", [D, D])
    w_up = din("w_up", [D, DFF])
    w_down = din("w_down", [DFF, D])
    msk_d = din("msk", [NT, 128, 3 * 384])
    cs_d = din("cs", [NT, 128, 96])
    sn_d = din("sn", [NT, 128, 96])
    gmixT_d = din("gmixT", [128, 16])
    gffnT_d = din("gffnT", [128, 16])
    goutT_d = din("goutT", [128, 16])
    gq_d = din("gq", [128, 128])
    gk_d = din("gk", [128, 128])
    gln_d = din("gln", [128, 1024])
    bln_d = din("bln", [128, 1024])
    sink_d = din("sinkb", [128, 8])
    wspT_d = din("wspT", [128, 1024])
    bsp_d = din("bsp", [128, 8])

    w_in_v = w_in.rearrange("(c p) n -> p c n", p=128)
    w_out_v = w_out.rearrange("(c p) n -> p c n", p=128)
    w_up_v = w_up.rearrange("(c p) n -> p c n", p=128)
    w_down_v = w_down.rearrange("(s c p) n -> s p c n", p=128, c=16)

    with contextlib.ExitStack() as st:
        P = Prog(nc, st)

        def sb(name, shape, dt):
            return st.enter_context(nc.sbuf_tensor(name, shape, dt))

        xres = sb("xres", [128, 4, D], F32)
        xst = sb("xst", [128, D], F32)
        xb0 = sb("xb", [128, D], BF16)
        big = sb("big", [128, 32768], BF16)
        mh = sb("mh", [128, 16, T], BF16)
        wsl = [sb(f"wsl{i}", [128, 16, 512], BF16) for i in range(NSLOT)]
        yst = sb("yst", [128, 4, 512], F32)
        tmpf0 = sb("tmpf", [128, 1024], F32)
        ident = sb("ident", [128, 128], BF16)
        identf = sb("identf", [128, 128], F32)
        gmixT = sb("gmixT_s", [128, 16], F32)
        gffnT = sb("gffnT_s", [128, 16], F32)
        goutT = sb("goutT_s", [128, 16], F32)
        gq = sb("gq_s", [128, 128], F32)
        gk = sb("gk_s", [128, 128], F32)
        gln = sb("gln_s", [128, 1024], F32)
        bln = sb("bln_s", [128, 1024], F32)
        sinkb = sb("sink_s", [128, 8], F32)
        nsink = sb("nsink_s", [128, 8], F32)
        wspT = sb("wspT_s", [128, 8, 128], BF16)
        bsp = sb("bsp_s", [128, 8], F32)
        msk = sb("msk_s", [128, 3, 384], F32)
        cs = sb("cs_s", [128, 6, 16], F32)
        sn = sb("sn_s", [128, 6, 16], F32)
        stats = [sb(f"stat{i}", [128, 32], F32) for i in range(9)]
        ropes = [sb(f"rope{i}", [128, 8, 2, 16], F32) for i in range(4)]
        bnsts = [sb(f"bnst{i}", [128, 2, 6], F32) for i in range(4)]
        rsA = sb("rsA", [128, 8], F32)
        rsA2 = sb("rsA2", [128, 8], F32)
        rsF = sb("rsF", [128, 4], F32)

        hT = big[:, 0:12288].rearrange("p (c t) -> p c t", t=TE)
        qT = big[:, 12288:16384].rearrange("p (h t) -> p h t", t=T)
        kT = big[:, 16384:17920].rearrange("p (h t) -> p h t", t=TE)
        vv = big[:, 17920:19456].rearrange("p (b n) -> p b n", n=256)
        vgn = big[:, 19456:23552].rearrange("p (b n) -> p b n", n=1024)
        uu = big[:, 23552:31744].bitcast(F32).rearrange("p (b n) -> p b n", n=1024)
        actT = big[:, :].rearrange("p (f t) -> p f t", t=T)
        smS, pbS, pTsS = [], [], []
        for i in range(2):
            o = i * 6144
            smS.append(big[:, o:o + 3072].bitcast(F32).rearrange("p (h k) -> p h k", k=384))
            pbS.append(big[:, o + 3072:o + 4608].rearrange("p (h k) -> p h k", k=384))
            pTsS.append(big[:, o + 4608:o + 6144].rearrange("p (j q) -> p j q", q=128))
        fxb = [xb0[:], big[:, 0:2048], big[:, 2048:4096]]
        mhf = mh[:].rearrange("p c t -> p (c t)")
        ystf = yst[:].rearrange("p a b -> p (a b)")
        xbS = [xb0[:], mhf[:, 0:2048], mhf[:, 2048:4096], ystf[:, 0:1024].bitcast(BF16)]
        tmpfS = [tmpf0[:], mhf[:, 4096:6144].bitcast(F32), mhf[:, 6144:8192].bitcast(F32), ystf[:, 1024:2048]]
        mixtok = [xst[:], ystf]
        xbD = [xb0[:], tmpf0[:].bitcast(BF16)]

        pmm = st.enter_context(nc.psum_tensor("pmm", [128, 8, 512], F32))
        TB = [pmm[:, 4:6, :].bitcast(BF16), pmm[:, 6:8, :].bitcast(BF16)]
        TB16 = [tb.rearrange("p a (c t) -> p (a c) t", t=128) for tb in TB]

        R_xres = [Res(f"xres{b}") for b in range(4)]
        R_xst = Res("xst")
        R_yst = Res("yst")
        R_mix = [R_xst, R_yst]
        R_xbS = [Res(f"xbS{i}") for i in range(4)]
        R_tmpfS = [Res(f"tmpfS{i}") for i in range(4)]
        R_fxb = [R_xbS[0], Res("fxb1"), Res("fxb2")]
        R_stat = [Res(f"stat{i}") for i in range(9)]
        R_rope = [Res(f"rope{i}") for i in range(4)]
        R_hT = [Res(f"hT{b}") for b in range(6)]
        R_qT = [Res(f"qT{b}") for b in range(4)]
        R_kT = [Res(f"kT{b}") for b in range(6)]
        R_v = [Res(f"v{b}") for b in range(6)]
        R_vgn = [Res(f"vgn{b}") for b in range(4)]
        R_u = [Res(f"u{b}") for b in range(4)]
        R_sm = [Res("sm0"), Res("sm1")]
        R_pb = [Res("pb0"), Res("pb1")]
        R_pTs = [Res("pTs0"), Res("pTs1")]
        R_actT = [Res(f"actT{g}") for g in range(16)]
        R_mh = [Res(f"mh{b}") for b in range(4)]
        R_wsl = [Res(f"wsl{i}") for i in range(NSLOT)]
        R_rl = [Res("rl0"), Res("rl1")]
        R_const = Res("const")
        R_task = Res("taskconst")
        R_bns = [Res(f"bn{i}") for i in range(4)]
        R_rsA = [Res(f"rsA{i}") for i in range(6)]
        R_rsF = [Res(f"rsF{i}") for i in range(4)]
        R_junkF = Res("junkF")
        R_bank = [Res(f"bank{i}") for i in range(8)]
        R_TB = [[R_bank[4], R_bank[5]], [R_bank[6], R_bank[7]]]
        arena_phase1 = R_hT + R_qT + R_kT + R_v + R_vgn + R_u
        dscratch_core = R_sm + R_pb + R_pTs
        fxb_res = [R_fxb[1], R_fxb[2], R_junkF]
        all_phase1 = arena_phase1 + dscratch_core + fxb_res
        mh_scratch = [R_xbS[1], R_xbS[2], R_tmpfS[1], R_tmpfS[2]]
        yst_scratch = [R_xbS[3], R_tmpfS[3]]

        d_const = DSem(P, "d_const")
        d_task = DSem(P, "d_task")
        d_xst = DSem(P, "d_xst")
        d_xres = [DSem(P, f"d_xres{b}") for b in range(4)]
        d_w = [DSem(P, f"d_w{i}") for i in range(NSLOT)]
        d_y = DSem(P, "d_y")
        d_yh = DSem(P, "d_yh")
        d_xc = [DSem(P, f"d_xc{i}") for i in range(4)]

        def xdst(eb):
            if eb in (0, 5):
                return xst[:], R_xst, d_xst
            return xres[:, eb - 1, :], R_xres[eb - 1], d_xres[eb - 1]

        NSET = 4
        SSTAT = [0, 1, 2, 8]

        sync, act, dve, pool, pe = nc.sync, nc.scalar, nc.vector, nc.gpsimd, nc.tensor

        for dst, src in ((gmixT, gmixT_d), (gffnT, gffnT_d), (goutT, goutT_d), (gq, gq_d),
                         (gk, gk_d), (gln, gln_d), (bln, bln_d), (sinkb, sink_d), (bsp, bsp_d)):
            P.dma("sp", d_const, lambda dst=dst, src=src: sync.dma_start(out=dst[:], in_=src[:, :]),
                  writes=[Res("c")])
        R_const.w = (d_const.key, d_const.count)
        d_wsp = DSem(P, "d_wsp")
        R_wsp = Res("wsp")
        P.dma("pool", d_wsp, lambda: pool.dma_start(out=wspT[:].rearrange("p h i -> p (h i)"), in_=wspT_d[:, :]),
              writes=[R_wsp])
        R_ident = Res("ident")
        P.op("pool", lambda: pool.memset(identf[:], 0.0), writes=[R_ident])
        P.op("pool", lambda: pool.affine_select(out=identf[:], in_=identf[:], pattern=[[-1, 128]],
                                                compare_op=ALU.not_equal, fill=1.0, base=0,
                                                channel_multiplier=1),
             reads=[R_ident], writes=[R_ident])
        P.op("dve", lambda: dve.tensor_copy(out=ident[:], in_=identf[:]), reads=[R_ident], writes=[R_ident])
        R_nsink = Res("nsink")
        P.op("dve", lambda: dve.tensor_scalar(out=nsink[:], in0=sinkb[:], scalar1=-1.0, scalar2=None,
                                              op0=ALU.mult), reads=[R_const], writes=[R_nsink])

        slabs = []
        for cg in (2, 0, 1, 3, 4, 5, 6):
            slabs.append(w_in_v[:, :, cg * 512:(cg + 1) * 512])
        for cg in range(4):
            slabs.append(w_out_v[:, :, cg * 512:(cg + 1) * 512])
        for fg in range(16):
            slabs.append(w_up_v[:, :, fg * 512:(fg + 1) * 512])
        for cg in range(4):
            for sub in range(4):
                slabs.append(w_down_v[sub][:, :, cg * 512:(cg + 1) * 512])
        n_per_task = len(slabs)
        total_slabs = n_per_task * n_tasks
        wstate = {"issued": 0}

        def ensure_loaded(upto):
            upto = min(upto, total_slabs - 1)
            while wstate["issued"] <= upto:
                n = wstate["issued"]
                s = n % NSLOT
                src = slabs[n % n_per_task]
                P.dma("pool", d_w[s], lambda s=s, src=src: pool.dma_start(out=wsl[s][:], in_=src),
                      writes=[R_wsl[s]])
                wstate["issued"] += 1

        def g_load(upto):
            ensure_loaded(upto)
            yield

        def rstd_chain(stt, rs, col, n, width=1):
            a = stt[:, col:col + width]
            P.op("dve", lambda: dve.tensor_scalar(out=a, in0=a, scalar1=1.0 / n, scalar2=EPS,
                                                  op0=ALU.mult, op1=ALU.add), reads=[rs], writes=[rs])
            yield
            P.op("act", lambda: act.activation(out=a, in_=a, func=AF.Ln), reads=[rs], writes=[rs])
            yield
            P.op("act", lambda: act.activation(out=a, in_=a, func=AF.Exp, scale=-0.5), reads=[rs], writes=[rs])
            yield

        def g_rms_to_T(src_ap, src_res, gT, dst_ap, dst_res, xbt, r_xb, junk, r_junk, stt, rs, tbname,
                       mode, col, out_res, cast_dma=None):
            if cast_dma == "done":
                pass
            elif cast_dma is not None:
                dsem, src_d = cast_dma
                P.dma("pool", dsem, lambda: pool.dma_start(out=xbt, in_=src_d), writes=[r_xb])
            else:
                P.op("act", lambda: act.activation(out=xbt, in_=src_ap, func=AF.Copy),
                     reads=[src_res], writes=[r_xb])
            yield
            P.op("act", lambda: act.activation(out=junk, in_=src_ap, func=AF.Square, accum_out=stt[:, 0:1]),
                 reads=[src_res], writes=[r_junk, rs])
            yield ("mark_src_read",)
            ti = int(tbname[-1])
            yield ("acq", f"bank{4 + 2 * ti}")
            yield ("acq", f"bank{5 + 2 * ti}")
            P.group("pe", [(lambda c=c: pe.transpose(out=TB16[ti][:, c, :], in_=xbt[:, c * 128:(c + 1) * 128],
                                                     identity=ident[:])) for c in range(16)],
                    reads=[r_xb, R_ident], writes=R_TB[ti])
            yield
            a = stt[:, 0:1]
            P.op("dve", lambda: dve.tensor_tensor(out=dst_ap, in0=TB16[ti],
                                                  in1=gT[:].unsqueeze(2).broadcast_to([128, 16, 128]),
                                                  op=ALU.mult),
                 reads=R_TB[ti] + [R_const], writes=[dst_res])
            yield ("rel", f"bank{4 + 2 * ti}")
            yield ("rel", f"bank{5 + 2 * ti}")
            P.op("dve", lambda: dve.tensor_scalar(out=a, in0=a, scalar1=1.0 / D, scalar2=EPS,
                                                  op0=ALU.mult, op1=ALU.add), reads=[rs], writes=[rs])
            yield
            if mode == "A":
                P.op("act", lambda: act.activation(out=a, in_=a, func=AF.Ln), reads=[rs], writes=[rs])
                yield
                P.op("act", lambda: act.activation(out=rsA[:, col:col + 1], in_=a, func=AF.Exp, scale=-0.5),
                     reads=[rs], writes=[out_res])
                yield
                P.op("dve", lambda: dve.tensor_tensor(out=rsA2[:, col:col + 1], in0=rsA[:, col:col + 1],
                                                      in1=rsA[:, col:col + 1], op=ALU.mult),
                     reads=[out_res], writes=[out_res])
                yield
            else:
                P.op("dve", lambda: dve.reciprocal(out=rsF[:, col:col + 1], in_=a), reads=[rs], writes=[out_res])
                yield

        def g_qk_chain(src3, src_res, H, g, eb, dstb, r_dst, tf, r_tf, stt, rs, rp, r_rp):
            tv = tf[:, 0:H * 128]
            tv3 = tv.rearrange("p (h d) -> p h d", d=128)
            P.op("act", lambda: act.activation(out=tv3, in_=src3, func=AF.Square), reads=[src_res], writes=[r_tf])
            yield
            P.op("dve", lambda: dve.tensor_reduce(out=stt[:, 8:8 + H], in_=tv3, axis=AX.X, op=ALU.add),
                 reads=[r_tf], writes=[rs])
            yield
            P.op("dve", lambda: dve.tensor_scalar(out=stt[:, 8:8 + H], in0=stt[:, 8:8 + H],
                                                  scalar1=rsA2[:, eb:eb + 1], scalar2=None, op0=ALU.mult),
                 reads=[rs, R_rsA[eb]], writes=[rs])
            yield
            yield from rstd_chain(stt, rs, 8, 128, H)
            P.op("dve", lambda: dve.tensor_scalar(out=stt[:, 8:8 + H], in0=stt[:, 8:8 + H],
                                                  scalar1=rsA[:, eb:eb + 1], scalar2=None, op0=ALU.mult),
                 reads=[rs, R_rsA[eb]], writes=[rs])
            yield
            P.op("dve", lambda: dve.tensor_tensor(out=tv3, in0=src3,
                                                  in1=stt[:, 8:8 + H].unsqueeze(2).broadcast_to([128, H, 128]),
                                                  op=ALU.mult), reads=[src_res, rs], writes=[r_tf])
            yield ("rel", "srcbank")
            P.op("dve", lambda: dve.tensor_tensor(out=tv3, in0=tv3,
                                                  in1=g[:].unsqueeze(1).broadcast_to([128, H, 128]),
                                                  op=ALU.mult), reads=[r_tf, R_const], writes=[r_tf])
            yield
            cosb = cs[:, eb, :].unsqueeze(1).broadcast_to([128, H, 16])
            sinb = sn[:, eb, :].unsqueeze(1).broadcast_to([128, H, 16])
            x1 = tv3[:, :, 0:16]
            x2 = tv3[:, :, 16:32]
            dst3 = dstb.rearrange("p (h d) -> p h d", d=128)
            r3 = rp[:, 0:H, :, :]
            P.op("act", lambda: act.activation(out=dstb, in_=tv, func=AF.Copy), reads=[r_tf], writes=[r_dst])
            yield
            for (a1, b1, a2, b2, opx, lo) in ((x1, cosb, x2, sinb, ALU.subtract, 0),
                                               (x2, cosb, x1, sinb, ALU.add, 16)):
                P.op("dve", lambda a1=a1, b1=b1: dve.tensor_tensor(out=r3[:, :, 0, :], in0=a1, in1=b1, op=ALU.mult),
                     reads=[r_tf, R_task], writes=[r_rp])
                yield
                P.op("dve", lambda a2=a2, b2=b2: dve.tensor_tensor(out=r3[:, :, 1, :], in0=a2, in1=b2, op=ALU.mult),
                     reads=[r_tf, R_task, r_rp], writes=[r_rp])
                yield
                P.op("dve", lambda opx=opx, lo=lo: dve.tensor_tensor(out=dst3[:, :, lo:lo + 16], in0=r3[:, :, 0, :],
                                                                     in1=r3[:, :, 1, :], op=opx),
                     reads=[r_rp], writes=[r_dst])
                yield

        def g_A_kv(t, eb, si_, slot):
            sset = eb % NSET
            xbt, r_xb = xbS[sset], R_xbS[sset]
            tf, r_tf = tmpfS[sset], R_tmpfS[sset]
            stt, rs = stats[SSTAT[sset]], R_stat[SSTAT[sset]]
            rp, r_rp = ropes[sset], R_rope[sset]
            yield ("acq", f"set{sset}")
            dst_ap, dres, dsem = xdst(eb)
            if eb in (0, 5):
                yield ("acq", "xst")
            if eb == 5:
                P.dma("sp", dsem, lambda: sync.dma_start(out=dst_ap, in_=xe[t, eb * 128:(eb + 1) * 128, :]),
                      writes=[dres])
                yield
            tbname = f"tb{eb % 2}"
            for r in g_rms_to_T(dst_ap, dres, gmixT, hT[:, :, eb * 128:(eb + 1) * 128], R_hT[eb],
                                xbt, r_xb, tf.bitcast(BF16), r_tf, stt, rs, tbname, "A", eb, R_rsA[eb],
                                cast_dma=("done" if (t, eb) in prefetched else
                                          (d_xc[sset], xe[t, eb * 128:(eb + 1) * 128, :]))):
                if r == ("mark_src_read",):
                    if eb in (0, 5):
                        yield ("rel", "xst")
                    else:
                        yield
                else:
                    yield r
            bk = eb % 4
            yield ("acq", f"bank{bk}")
            P.group("pe", [(lambda c=c: pe.matmul(out=pmm[:, bk, :], lhsT=hT[:, c, eb * 128:(eb + 1) * 128],
                                                  rhs=wsl[slot][:, c, :], start=(c == 0), stop=(c == 15)))
                           for c in range(16)],
                    reads=[R_hT[eb], R_wsl[slot]], writes=[R_bank[bk]])
            yield
            P.op("act", lambda: act.activation(out=vv[:, eb, :], in_=pmm[:, bk, 256:512], func=AF.Copy,
                                               scale=rsA[:, eb:eb + 1]),
                 reads=[R_bank[bk], R_rsA[eb]], writes=[R_v[eb]])
            yield
            src3 = pmm[:, bk, 0:256].rearrange("p (h d) -> p h d", d=128)
            for r in g_qk_chain(src3, R_bank[bk], 2, gk, eb, xbt[:, 0:256], r_xb, tf, r_tf, stt, rs, rp, r_rp):
                if r == ("rel", "srcbank"):
                    yield ("rel", f"bank{bk}")
                else:
                    yield r
            ti = eb % 2
            yield ("acq", f"bank{4 + 2 * ti}")
            P.group("pe", [(lambda h=h: pe.transpose(out=TB[ti][:, 0, h * 128:(h + 1) * 128],
                                                     in_=xbt[:, h * 128:(h + 1) * 128], identity=ident[:]))
                           for h in range(2)],
                    reads=[r_xb, R_ident], writes=[R_TB[ti][0]])
            yield
            P.op("dve", lambda: dve.tensor_copy(out=kT[:, :, eb * 128:(eb + 1) * 128],
                                                in_=TB[ti][:, 0, 0:256].rearrange("p (h t) -> p h t", t=128)),
                 reads=[R_TB[ti][0]], writes=[R_kT[eb]])
            yield ("rel", f"bank{4 + 2 * ti}")
            yield ("rel", f"set{sset}")

        def g_q(t, b, slot0, slot1, idx):
            eb = b + 1
            sset = (idx + 2) % NSET
            xbt, r_xb = xbS[sset], R_xbS[sset]
            tf, r_tf = tmpfS[sset], R_tmpfS[sset]
            stt, rs = stats[SSTAT[sset]], R_stat[SSTAT[sset]]
            rp, r_rp = ropes[sset], R_rope[sset]
            yield ("acq", f"set{sset}")
            pr = (idx % 2) * 2
            yield ("acq", f"bank{pr}")
            yield ("acq", f"bank{pr + 1}")
            for j, slot in ((0, slot0), (1, slot1)):
                P.group("pe", [(lambda c=c, j=j, slot=slot: pe.matmul(
                    out=pmm[:, pr + j, :], lhsT=hT[:, c, eb * 128:(eb + 1) * 128], rhs=wsl[slot][:, c, :],
                    start=(c == 0), stop=(c == 15))) for c in range(16)],
                    reads=[R_hT[eb], R_wsl[slot]], writes=[R_bank[pr + j]])
                yield
            src3 = pmm[:, pr:pr + 2, :].rearrange("p a (h d) -> p (a h) d", d=128)

            for r in g_qk_chain2(src3, [R_bank[pr], R_bank[pr + 1]], 8, gq, eb, xbt[:, 0:1024], r_xb,
                                 tf, r_tf, stt, rs, rp, r_rp):
                if r == ("rel", "srcbank"):
                    yield ("rel", f"bank{pr}")
                    yield ("rel", f"bank{pr + 1}")
                else:
                    yield r
            ti = idx % 2
            yield ("acq", f"bank{4 + 2 * ti}")
            P.group("pe", [(lambda h=h: pe.transpose(out=TB[ti][:, 0, h * 128:(h + 1) * 128],
                                                     in_=xbt[:, h * 128:(h + 1) * 128], identity=ident[:]))
                           for h in range(8)],
                    reads=[r_xb, R_ident], writes=[R_TB[ti][0]])
            yield
            P.op("dve", lambda: dve.tensor_copy(out=qT[:, :, b * 128:(b + 1) * 128],
                                                in_=TB[ti][:, 0, :].rearrange("p (h t) -> p h t", t=128)),
                 reads=[R_TB[ti][0]], writes=[R_qT[b]])
            yield ("rel", f"bank{4 + 2 * ti}")
            yield ("rel", f"set{sset}")

        def g_qk_chain2(src3, src_rl, H, g, eb, dstb, r_dst, tf, r_tf, stt, rs, rp, r_rp):
            tv = tf[:, 0:H * 128]
            tv3 = tv.rearrange("p (h d) -> p h d", d=128)
            P.op("act", lambda: act.activation(out=tv3, in_=src3, func=AF.Square), reads=src_rl, writes=[r_tf])
            yield
            P.op("dve", lambda: dve.tensor_reduce(out=stt[:, 8:8 + H], in_=tv3, axis=AX.X, op=ALU.add),
                 reads=[r_tf], writes=[rs])
            yield
            P.op("dve", lambda: dve.tensor_scalar(out=stt[:, 8:8 + H], in0=stt[:, 8:8 + H],
                                                  scalar1=rsA2[:, eb:eb + 1], scalar2=None, op0=ALU.mult),
                 reads=[rs, R_rsA[eb]], writes=[rs])
            yield
            yield from rstd_chain(stt, rs, 8, 128, H)
            P.op("dve", lambda: dve.tensor_scalar(out=stt[:, 8:8 + H], in0=stt[:, 8:8 + H],
                                                  scalar1=rsA[:, eb:eb + 1], scalar2=None, op0=ALU.mult),
                 reads=[rs, R_rsA[eb]], writes=[rs])
            yield
            P.op("dve", lambda: dve.tensor_tensor(out=tv3, in0=src3,
                                                  in1=stt[:, 8:8 + H].unsqueeze(2).broadcast_to([128, H, 128]),
                                                  op=ALU.mult), reads=src_rl + [rs], writes=[r_tf])
            yield ("rel", "srcbank")
            P.op("dve", lambda: dve.tensor_tensor(out=tv3, in0=tv3,
                                                  in1=g[:].unsqueeze(1).broadcast_to([128, H, 128]),
                                                  op=ALU.mult), reads=[r_tf, R_const], writes=[r_tf])
            yield
            cosb = cs[:, eb, :].unsqueeze(1).broadcast_to([128, H, 16])
            sinb = sn[:, eb, :].unsqueeze(1).broadcast_to([128, H, 16])
            x1 = tv3[:, :, 0:16]
            x2 = tv3[:, :, 16:32]
            dst3 = dstb.rearrange("p (h d) -> p h d", d=128)
            r3 = rp[:, 0:H, :, :]
            P.op("act", lambda: act.activation(out=dstb, in_=tv, func=AF.Copy), reads=[r_tf], writes=[r_dst])
            yield
            for (a1, b1, a2, b2, opx, lo) in ((x1, cosb, x2, sinb, ALU.subtract, 0),
                                               (x2, cosb, x1, sinb, ALU.add, 16)):
                P.op("dve", lambda a1=a1, b1=b1: dve.tensor_tensor(out=r3[:, :, 0, :], in0=a1, in1=b1, op=ALU.mult),
                     reads=[r_tf, R_task], writes=[r_rp])
                yield
                P.op("dve", lambda a2=a2, b2=b2: dve.tensor_tensor(out=r3[:, :, 1, :], in0=a2, in1=b2, op=ALU.mult),
                     reads=[r_tf, R_task, r_rp], writes=[r_rp])
                yield
                P.op("dve", lambda opx=opx, lo=lo: dve.tensor_tensor(out=dst3[:, :, lo:lo + 16], in0=r3[:, :, 0, :],
                                                                     in1=r3[:, :, 1, :], op=opx),
                     reads=[r_rp], writes=[r_dst])
                yield

        def g_u(b, j, slot, i):
            eb = b + 1
            bk5 = i % 4
            yield ("acq", f"bank{bk5}")
            P.group("pe", [(lambda c=c: pe.matmul(out=pmm[:, bk5, :], lhsT=hT[:, c, eb * 128:(eb + 1) * 128],
                                                  rhs=wsl[slot][:, c, :], start=(c == 0), stop=(c == 15)))
                           for c in range(16)],
                    reads=[R_hT[eb], R_wsl[slot]], writes=[R_bank[bk5]])
            yield
            P.op("act", lambda: act.activation(out=uu[:, b, j * 512:(j + 1) * 512], in_=pmm[:, bk5, :],
                                               func=AF.Gelu_apprx_tanh, scale=rsA[:, eb:eb + 1]),
                 reads=[R_bank[bk5], R_rsA[eb]], writes=[R_u[b]])
            yield ("rel", f"bank{bk5}")

        def g_vg(b, slot0, slot1, i):
            eb = b + 1
            sset = i % NSET
            stt, rs = stats[SSTAT[sset]], R_stat[SSTAT[sset]]
            tf, r_tf = tmpfS[sset], R_tmpfS[sset]
            bnst, R_bn = bnsts[sset], R_bns[sset]
            yield ("acq", f"set{sset}")
            for j, slot in ((0, slot0), (1, slot1)):
                bk5 = (2 * i + j) % 4
                yield ("acq", f"bank{bk5}")
                P.group("pe", [(lambda c=c, slot=slot, bk5=bk5: pe.matmul(
                    out=pmm[:, bk5, :], lhsT=hT[:, c, eb * 128:(eb + 1) * 128], rhs=wsl[slot][:, c, :],
                    start=(c == 0), stop=(c == 15))) for c in range(16)],
                    reads=[R_hT[eb], R_wsl[slot]], writes=[R_bank[bk5]])
                yield
                P.op("act", lambda j=j, bk5=bk5: act.activation(out=tf[:, j * 512:(j + 1) * 512], in_=pmm[:, bk5, :],
                                                       func=AF.Gelu_apprx_tanh, scale=rsA[:, eb:eb + 1]),
                     reads=[R_bank[bk5], R_rsA[eb]], writes=[r_tf])
                yield ("rel", f"bank{bk5}")
                P.op("dve", lambda j=j: dve.bn_stats(out=bnst[:, j, :], in_=tf[:, j * 512:(j + 1) * 512]),
                     reads=[r_tf], writes=[R_bn])
                yield
            P.op("dve", lambda: dve.bn_aggr(out=stt[:, 16:18], in_=bnst[:].rearrange("p a b -> p (a b)")),
                 reads=[R_bn], writes=[rs])
            yield
            P.op("dve", lambda: dve.tensor_scalar(out=stt[:, 17:18], in0=stt[:, 17:18], scalar1=EPS, scalar2=None,
                                                  op0=ALU.add), reads=[rs], writes=[rs])
            yield
            P.op("act", lambda: act.activation(out=stt[:, 17:18], in_=stt[:, 17:18], func=AF.Ln),
                 reads=[rs], writes=[rs])
            yield
            P.op("act", lambda: act.activation(out=stt[:, 17:18], in_=stt[:, 17:18], func=AF.Exp, scale=-0.5),
                 reads=[rs], writes=[rs])
            yield
            P.op("dve", lambda: dve.tensor_scalar(out=tf[:], in0=tf[:], scalar1=stt[:, 16:17],
                                                  scalar2=stt[:, 17:18], op0=ALU.subtract, op1=ALU.mult),
                 reads=[r_tf, rs], writes=[r_tf])
            yield
            P.op("dve", lambda: dve.tensor_tensor(out=tf[:], in0=tf[:], in1=gln[:], op=ALU.mult),
                 reads=[r_tf, R_const], writes=[r_tf])
            yield
            P.op("dve", lambda: dve.tensor_tensor(out=vgn[:, b, :], in0=tf[:], in1=bln[:], op=ALU.add),
                 reads=[r_tf, R_const], writes=[R_vgn[b]])
            yield ("rel", f"set{sset}")

        def g_att(b, kvg, unit):
            eb = b + 1
            ds = unit % 2
            sm_, pb_, pT_ = smS[ds], pbS[ds], pTsS[ds]
            sti = 3 + unit % 4
            stt, rs = stats[sti], R_stat[sti]
            kind = 0 if b == 0 else (2 if b == 3 else 1)
            mt, r_mt = mixtok[b % 2], R_mix[b % 2]
            yield ("acq", f"st{sti}")
            yield ("acq", f"sm{ds}")
            for k_ in range(4):
                yield ("acq", f"bank{k_}")
            P.group("pe", [(lambda hh=hh: pe.matmul(
                out=pmm[:, hh, 0:384], lhsT=qT[:, 4 * kvg + hh, b * 128:(b + 1) * 128],
                rhs=kT[:, kvg, (eb - 1) * 128:(eb + 2) * 128], start=True, stop=True)) for hh in range(4)],
                reads=[R_qT[b], R_kT[eb - 1], R_kT[eb], R_kT[eb + 1]], writes=R_bank[0:4])
            yield
            P.op("dve", lambda: dve.scalar_tensor_tensor(
                out=sm_, in0=pmm[:, 0:4, 0:384], scalar=SCALE,
                in1=msk[:, kind, :].unsqueeze(1).broadcast_to([128, 4, 384]), op0=ALU.mult, op1=ALU.add),
                reads=R_bank[0:4] + [R_task], writes=[R_sm[ds]])
            for k_ in range(4):
                yield ("rel", f"bank{k_}")
            P.op("dve", lambda: dve.tensor_reduce(out=stt[:, 0:4], in_=sm_, axis=AX.X, op=ALU.max),
                 reads=[R_sm[ds]], writes=[rs])
            yield
            P.op("dve", lambda: dve.scalar_tensor_tensor(out=stt[:, 4:8], in0=stt[:, 0:4], scalar=-1.0,
                                                         in1=nsink[:, 4 * kvg:4 * kvg + 4], op0=ALU.mult,
                                                         op1=ALU.min),
                 reads=[rs, R_nsink], writes=[rs])
            yield
            yield ("acq", f"pb{ds}")
            for hh in range(4):
                P.op("act", lambda hh=hh: act.activation(out=pb_[:, hh, :], in_=sm_[:, hh, :], func=AF.Exp,
                                                         bias=stt[:, 4 + hh:5 + hh], scale=1.0,
                                                         accum_out=stt[:, 8 + hh:9 + hh]),
                     reads=[R_sm[ds], rs], writes=[R_pb[ds], rs])
                yield
            yield ("rel", f"sm{ds}")
            P.op("dve", lambda: dve.tensor_tensor(out=stt[:, 12:16], in0=sinkb[:, 4 * kvg:4 * kvg + 4],
                                                  in1=stt[:, 4:8], op=ALU.add), reads=[rs, R_const], writes=[rs])
            yield
            P.op("act", lambda: act.activation(out=stt[:, 12:16], in_=stt[:, 12:16], func=AF.Exp),
                 reads=[rs], writes=[rs])
            yield
            P.op("dve", lambda: dve.tensor_tensor(out=stt[:, 16:20], in0=stt[:, 8:12], in1=stt[:, 12:16],
                                                  op=ALU.add), reads=[rs], writes=[rs])
            yield
            P.op("dve", lambda: dve.reciprocal(out=stt[:, 20:24], in_=stt[:, 16:20]), reads=[rs], writes=[rs])
            yield
            yield ("acq", "bank6")
            yield ("acq", "bank7")
            tbf = TB16[1]
            P.group("pe", [(lambda hh=hh, kb=kb: pe.transpose(
                out=tbf[:, hh * 3 + kb, :], in_=pb_[:, hh, kb * 128:(kb + 1) * 128], identity=ident[:]))
                for hh in range(4) for kb in range(3)],
                reads=[R_pb[ds], R_ident], writes=R_TB[1])
            yield ("rel", f"pb{ds}")
            yield ("acq", f"pT{ds}")
            P.op("act", lambda: act.activation(out=pT_, in_=tbf[:, 0:12, :], func=AF.Copy),
                 reads=R_TB[1], writes=[R_pTs[ds]])
            yield ("rel", "bank6")
            yield ("rel", "bank7")
            yield ("acq", "bank4")
            P.group("pe", [(lambda hh=hh, kb=kb: pe.matmul(
                out=pmm[:, 4, hh * 128:(hh + 1) * 128], lhsT=pT_[:, hh * 3 + kb, :],
                rhs=vv[:, eb - 1 + kb, kvg * 128:(kvg + 1) * 128], start=(kb == 0), stop=(kb == 2)))
                for hh in range(4) for kb in range(3)],
                reads=[R_pTs[ds], R_v[eb - 1], R_v[eb], R_v[eb + 1]], writes=[R_bank[4]])
            yield ("rel", f"pT{ds}")
            P.op("dve", lambda: dve.tensor_tensor(
                out=mt[:, kvg * 512:(kvg + 1) * 512].rearrange("p (h d) -> p h d", d=128),
                in0=pmm[:, 4, :].rearrange("p (h d) -> p h d", d=128),
                in1=stt[:, 20:24].unsqueeze(2).broadcast_to([128, 4, 128]), op=ALU.mult),
                reads=[R_bank[4], rs], writes=[r_mt])
            yield ("rel", "bank4")
            yield ("rel", f"st{sti}")

        def g_gmn(b):
            mt, r_mt = mixtok[b % 2], R_mix[b % 2]
            stt, rs = stats[7], R_stat[7]
            xbt, r_xb = xbD[b % 2], (R_xbS[0] if b % 2 == 0 else R_tmpfS[0])
            yield ("acq", "bank0")
            yield ("acq", "bank1")
            P.group("pe", [(lambda h=h: pe.matmul(
                out=pmm[:, h // 4, (h % 4) * 128:(h % 4 + 1) * 128], lhsT=wspT[:, h, :],
                rhs=vgn[:, b, h * 128:(h + 1) * 128], start=True, stop=True)) for h in range(8)],
                reads=[R_wsp, R_vgn[b]], writes=[R_bank[0], R_bank[1]])
            yield
            for h in range(8):
                P.op("dve", lambda h=h: dve.scalar_tensor_tensor(
                    out=mt[:, 1024 + h * 128:1024 + (h + 1) * 128],
                    in0=pmm[:, h // 4, (h % 4) * 128:(h % 4 + 1) * 128], scalar=bsp[:, h:h + 1],
                    in1=uu[:, b, h * 128:(h + 1) * 128], op0=ALU.add, op1=ALU.mult),
                    reads=[R_bank[h // 4], R_const, R_u[b]], writes=[r_mt])
                yield
            yield ("rel", "bank0")
            yield ("rel", "bank1")
            yield ("acq", f"xbD{b % 2}")
            yield ("acq", "st7")
            for half in range(2):
                src = mt[:, half * 1024:(half + 1) * 1024]
                dstb = xbt[:, half * 1024:(half + 1) * 1024]
                P.op("act", lambda src=src, dstb=dstb: act.activation(out=dstb, in_=src, func=AF.Square,
                                                                      accum_out=stt[:, half:half + 1]),
                     reads=[r_mt], writes=[r_xb, rs])
                yield
            yield from rstd_chain(stt, rs, 0, 1024, 2)
            for half in range(2):
                src = mt[:, half * 1024:(half + 1) * 1024]
                dstb = xbt[:, half * 1024:(half + 1) * 1024]
                P.op("act", lambda src=src, dstb=dstb, half=half: act.activation(
                    out=dstb, in_=src, func=AF.Copy, scale=stt[:, half:half + 1]),
                    reads=[r_mt, rs], writes=[r_xb])
                yield
            yield ("rel", "st7")
            yield ("mark", f"m{b}")
            yield ("acq", "bank4")
            yield ("acq", "bank5")
            P.group("pe", [(lambda c=c: pe.transpose(out=TB16[0][:, c, :], in_=xbt[:, c * 128:(c + 1) * 128],
                                                     identity=ident[:])) for c in range(16)],
                    reads=[r_xb, R_ident], writes=R_TB[0])
            yield ("rel", f"xbD{b % 2}")
            P.op("dve", lambda: dve.tensor_tensor(out=mh[:, :, b * 128:(b + 1) * 128], in0=TB16[0],
                                                  in1=goutT[:].unsqueeze(2).broadcast_to([128, 16, 128]),
                                                  op=ALU.mult),
                 reads=R_TB[0] + [R_const], writes=[R_mh[b]])
            yield ("rel", "bank4")
            yield ("rel", "bank5")

        def g_F(b, i):
            xbt, r_xb = fxb[i % 3], R_fxb[i % 3]
            yield ("acq", f"fxb{i % 3}")
            for r in g_rms_to_T(xres[:, b, :], R_xres[b], gffnT, mh[:, :, b * 128:(b + 1) * 128], R_mh[b],
                                xbt, r_xb, big[:, 4096:6144], R_junkF, stats[i % 3], R_stat[i % 3],
                                f"tb{i % 2}", "F", b, R_rsF[b]):
                yield (None if r == ("mark_src_read",) else r)
            yield ("rel", f"fxb{i % 3}")

        prefetched = set()

        def issue_casts(tt, ebs):
            for eb in ebs:
                sset = eb % NSET
                P.dma("pool", d_xc[sset], lambda sset=sset, eb=eb: pool.dma_start(
                    out=xbS[sset], in_=xe[tt, eb * 128:(eb + 1) * 128, :]), writes=[R_xbS[sset]])
                prefetched.add((tt, eb))

        bank_rot = {"i": 0}

        def next_bank(n=4):
            b = bank_rot["i"]
            bank_rot["i"] = (b + 1) % n
            return b

        try:
          ckpt("const")
          for t in range(n_tasks):
            base = t * n_per_task
            alias_edges(R_mh, mh_scratch)
            alias_edges([R_yst] + R_rl, yst_scratch)
            issue_casts(t, [0, 1, 2, 3])
            ensure_loaded(base + 2)

            for i_, (dst, src) in enumerate(((msk, msk_d), (cs, cs_d), (sn, sn_d))):
                P.dma("sp", d_task, lambda dst=dst, src=src: sync.dma_start(
                    out=dst[:].rearrange("p a b -> p (a b)"), in_=src[t]),
                    writes=[R_task if i_ == 0 else Res("tc")])
            R_task.w = (d_task.key, d_task.count)
            for eb in range(5):
                dst_ap, dres, dsem = xdst(eb)
                P.dma("sp", dsem, lambda dst_ap=dst_ap, eb=eb: sync.dma_start(
                    out=dst_ap, in_=xe[t, eb * 128:(eb + 1) * 128, :]), writes=[dres])

            alias_edges(R_mh, mh_scratch)
            alias_edges([R_yst] + R_rl, yst_scratch)
            alias_edges(R_actT, arena_phase1)

            s_kv, s_q0, s_q1 = base % NSLOT, (base + 1) % NSLOT, (base + 2) % NSLOT
            s_u = [(base + 3) % NSLOT, (base + 4) % NSLOT]
            s_vg = [(base + 5) % NSLOT, (base + 6) % NSLOT]
            allA = [f"A{eb}" for eb in range(6)]
            allq = [f"q{b}" for b in range(4)]
            items = [Item(f"A{eb}", g_A_kv(t, eb, 0, s_kv)) for eb in range(6)]
            items += [Item(f"q{b}", g_q(t, b, s_q0, s_q1, b), after=[f"A{b + 1}"]) for b in range(4)]
            items.append(Item("ld_u0", g_load(base + 3), after=allA))
            items += [Item(f"u0{b}", g_u(b, 0, s_u[0], b)) for b in range(4)]
            items.append(Item("ld_u1", g_load(base + 5), after=allq))
            items += [Item(f"u1{b}", g_u(b, 1, s_u[1], b)) for b in range(4)]
            items.append(Item("ld_vg1", g_load(base + 6), after=[f"u0{b}" for b in range(4)]))
            items += [Item(f"vg{b}", g_vg(b, s_vg[0], s_vg[1], b)) for b in range(4)]
            run_pipeline(items, width=4, stagger=7)
            ckpt("AB")

            ensure_loaded(base + 7)
            alias_edges(R_hT, dscratch_core)
            alias_edges(mh_scratch, R_mh)
            alias_edges(yst_scratch, [R_yst])
            items = []
            unit = 0
            for b in range(4):
                for kvg in range(2):
                    items.append(Item(f"a{b}{kvg}", g_att(b, kvg, unit), after=([f"m{b - 2}"] if b >= 2 else [])))
                    unit += 1
                if b >= 1:
                    items.append(Item(f"g{b - 1}", g_gmn(b - 1), after=[f"a{b - 1}0", f"a{b - 1}1"]))
            items.append(Item("g3", g_gmn(3), after=["a30", "a31"]))
            run_pipeline(items, width=6, stagger=3)
            ckpt("D")
            si = 7

            bank_rot["i"] = 0
            for cg in range(4):
                n = base + si
                ensure_loaded(n + 2)
                s = n % NSLOT
                for b in range(4):
                    bk = next_bank()
                    P.group("pe", [(lambda c=c, b=b, s=s, bk=bk: pe.matmul(
                        out=pmm[:, bk, :], lhsT=mh[:, c, b * 128:(b + 1) * 128], rhs=wsl[s][:, c, :],
                        start=(c == 0), stop=(c == 15))) for c in range(16)],
                        reads=[R_mh[b], R_wsl[s]], writes=[R_bank[bk]])
                    P.op("dve", lambda b=b, cg=cg, bk=bk: dve.tensor_tensor(
                        out=xres[:, b, cg * 512:(cg + 1) * 512], in0=pmm[:, bk, :],
                        in1=xres[:, b, cg * 512:(cg + 1) * 512], op=ALU.add),
                        reads=[R_bank[bk], R_xres[b]], writes=[R_xres[b]])
                si += 1
            ckpt("E")

            alias_edges(dscratch_core, fxb_res)
            items = [Item(f"F{b}", g_F(b, b)) for b in range(4)]
            run_pipeline(items, width=3, stagger=4)
            ckpt("F")

            bank_rot["i"] = 0
            for fg in range(16):
                n = base + si
                ensure_loaded(n + 2)
                s = n % NSLOT
                for fi in range(4):
                    fc = fg * 4 + fi
                    bk = next_bank(2)
                    P.group("pe", [(lambda c=c, fi=fi, s=s, bk=bk: pe.matmul(
                        out=pmm[:, bk, :], lhsT=wsl[s][:, c, fi * 128:(fi + 1) * 128], rhs=mh[:, c, :],
                        start=(c == 0), stop=(c == 15))) for c in range(16)],
                        reads=R_mh + [R_wsl[s]], writes=[R_bank[bk]])
                    rl = yst[:, bk, :]
                    P.op("act", lambda bk=bk, rl=rl: act.activation(out=rl, in_=pmm[:, bk, :], func=AF.Relu),
                         reads=[R_bank[bk]], writes=[R_rl[bk], R_yst])
                    extra = all_phase1 if (fg == 0 and fi == 0) else []
                    P.op("dve", lambda fc=fc, rl=rl: dve.tensor_tensor(out=actT[:, fc, :], in0=rl, in1=rl,
                                                                       op=ALU.mult),
                         reads=[R_rl[bk]], writes=[R_actT[fg]] + extra)
                si += 1
            ckpt("G")

            for cg in range(4):
                for sub in range(4):
                    n = base + si
                    ensure_loaded(n + 2)
                    s = n % NSLOT
                    for b in range(4):
                        P.group("pe", [(lambda fc=fc, b=b, s=s, sub=sub: pe.matmul(
                            out=pmm[:, 2 + b, :], lhsT=actT[:, sub * 16 + fc, b * 128:(b + 1) * 128],
                            rhs=wsl[s][:, fc, :], start=(sub == 0 and fc == 0),
                            stop=(sub == 3 and fc == 15))) for fc in range(16)],
                            reads=[R_actT[sub * 4 + k] for k in range(4)] + [R_wsl[s]],
                            writes=[R_bank[2 + b]])
                        if sub == 3:
                            P.op("dve", lambda b=b, cg=cg: dve.scalar_tensor_tensor(
                                out=yst[:, b, :], in0=pmm[:, 2 + b, :], scalar=rsF[:, b:b + 1],
                                in1=xres[:, b, cg * 512:(cg + 1) * 512], op0=ALU.mult, op1=ALU.add),
                                reads=[R_bank[2 + b], R_xres[b], R_rsF[b]],
                                writes=[R_yst] + ([R_rl[b]] if b < 2 else []))
                    si += 1
                P.dma("sp", d_y, lambda cg=cg: sync.dma_start(
                    out=y[t].rearrange("(b p) n -> p b n", p=128)[:, :, cg * 512:(cg + 1) * 512],
                    in_=yst[:]), reads=[R_yst])
            assert si == n_per_task
        except _Stop:
            pass
        for e in ("pe", "act", "dve", "pool"):
            if P.cnt[e]:
                sync.wait_ge(P.semh[P.sem[e]], P.cnt[e])
        P.wait_all("sp", [d_const, d_task, d_xst, d_yh, d_wsp] + d_xres + d_w + d_xc + [d_y])
    return nc


def _task_table():
    tasks = []
    for s in range(2):
        for j in range(8):
            tasks.append(("p", s, j * T))
    for s in range(4):
        for j in range(4):
            tasks.append(("s", s, j * T))
    return tasks


def _rope_tables(pos):
    inv_freq = 500000.0 ** (-np.arange(0, 32, 2, dtype=np.float64) / 32.0)
    ang = pos.astype(np.float64)[:, None] * inv_freq[None, :]
    return np.cos(ang).astype(np.float32), np.sin(ang).astype(np.float32)


_NC_CACHE = {}


def kernel(x_prompt, x_sample, g_mix, w_in, g_q, g_k, sink, g_v_ln, b_v_ln, w_spatial,
           b_spatial, g_attn_out, g_gmlp_out, w_out, g_ffn, w_up, w_down):
    f32 = np.float32
    x_prompt = np.asarray(x_prompt, f32)
    x_sample = np.asarray(x_sample, f32)
    tasks = _task_table()

    def colT(v):
        return np.ascontiguousarray(np.asarray(v, f32).reshape(16, 128).T)

    def bcast(v, n):
        return np.ascontiguousarray(np.broadcast_to(np.asarray(v, f32).reshape(1, n), (128, n)))

    shared = {
        "w_in": np.ascontiguousarray(np.asarray(w_in, f32)[0]),
        "w_out": np.ascontiguousarray(np.asarray(w_out, f32)[0]),
        "w_up": np.ascontiguousarray(np.asarray(w_up, f32)[0]),
        "w_down": np.ascontiguousarray(np.asarray(w_down, f32)[0]),
        "gmixT": colT(g_mix[0]),
        "gffnT": colT(g_ffn[0]),
        "goutT": colT(np.concatenate([np.asarray(g_attn_out, f32)[0], np.asarray(g_gmlp_out, f32)[0]])),
        "gq": bcast(g_q[0], 128),
        "gk": bcast(g_k[0], 128),
        "gln": bcast(g_v_ln[0], 1024),
        "bln": bcast(b_v_ln[0], 1024),
        "sinkb": bcast(sink[0], 8),
        "wspT": np.ascontiguousarray(np.transpose(np.asarray(w_spatial, f32)[0], (2, 0, 1)).reshape(128, 1024)),
        "bsp": np.ascontiguousarray(np.asarray(b_spatial, f32)[0].T),
    }

    qi = np.arange(128)[:, None]
    kj = np.arange(384)[None, :]
    band = (kj >= qi) & (kj <= qi + 256)

    in_maps = []
    for c in range(N_CORES):
        xe = np.zeros((NT, TE, D), f32)
        msk = np.empty((NT, 128, 3, 384), f32)
        cs = np.empty((NT, 128, 6, 16), f32)
        sn = np.empty((NT, 128, 6, 16), f32)
        for i in range(NT):
            grp, s, start = tasks[c * NT + i]
            xs = x_prompt[s] if grp == "p" else x_sample[s]
            S = xs.shape[0]
            lo, hi = start - 128, start + T + 128
            a, b = max(lo, 0), min(hi, S)
            xe[i, a - lo:b - lo] = xs[a:b]
            left_ok = lo >= 0
            right_ok = hi <= S
            m0 = band if left_ok else (band & (kj >= 128))
            m2 = band if right_ok else (band & (kj < 256))
            for kind, m in enumerate((m0, band, m2)):
                msk[i, :, kind, :] = np.where(m, 0.0, NEG).astype(f32)
            pos = np.arange(lo, hi)
            co, si_ = _rope_tables(pos)
            cs[i] = co.reshape(6, 128, 16).transpose(1, 0, 2)
            sn[i] = si_.reshape(6, 128, 16).transpose(1, 0, 2)
        m = dict(shared)
        m["xe"] = xe
        m["msk"] = msk.reshape(NT, 128, 3 * 384)
        m["cs"] = cs.reshape(NT, 128, 96)
        m["sn"] = sn.reshape(NT, 128, 96)
        in_maps.append(m)

    if "nc" not in _NC_CACHE:
        _NC_CACHE["nc"] = build_program()
    nc = _NC_CACHE["nc"]
    res = run_bass_kernel_spmd(nc, in_maps, core_ids=list(range(N_CORES)))

    y_prompt = np.empty((2, 4096, D), f32)
    y_sample = np.empty((4, 2048, D), f32)
    for c in range(N_CORES):
        yc = np.asarray(res.results[c]["y"], f32)
        for i in range(NT):
            grp, s, start = tasks[c * NT + i]
            if grp == "p":
                y_prompt[s, start:start + T] = yc[i]
            else:
                y_sample[s, start:start + T] = yc[i]
    return (y_prompt, y_sample)
```

```python
import contextlib
import math
import numpy as np
import concourse.bass as bass
import concourse.mybir as mybir
from concourse.bass_utils import run_bass_kernel_spmd

F32 = mybir.dt.float32
BF16 = mybir.dt.bfloat16
AF = mybir.ActivationFunctionType
ALU = mybir.AluOpType
AX = mybir.AxisListType

N_CORES = 8
NT = 4
T = 512
TE = 768
D = 2048
DFF = 8192
EPS = 1e-6
NEG = -1e30
SCALE = 1.0 / math.sqrt(128.0)
NSLOT = 3


class Res:
    __slots__ = ("name", "w", "r")

    def __init__(self, name):
        self.name = name
        self.w = None
        self.r = {}


class DSem:
    def __init__(self, prog, name):
        self.h = prog.new_sem(name)
        self.key = name
        self.count = 0


class Prog:
    ENGS = ("pe", "act", "dve", "pool", "sp")

    def __init__(self, nc, stack):
        self.nc = nc
        self.stack = stack
        self.eng = {"pe": nc.tensor, "act": nc.scalar, "dve": nc.vector,
                    "pool": nc.gpsimd, "sp": nc.sync}
        self.semh = {}
        self.sem = {}
        self.cnt = {}
        self.waited = {e: {} for e in self.ENGS}
        for e in self.ENGS:
            self.sem[e] = "E_" + e
            self.semh["E_" + e] = stack.enter_context(nc.semaphore("E_" + e))
            self.cnt[e] = 0

    def new_sem(self, name):
        h = self.stack.enter_context(self.nc.semaphore(name))
        self.semh[name] = h
        return h

    def _emit_waits(self, e, reads, writes):
        need = {}

        def add(pt):
            if pt is None:
                return
            k, v = pt
            if need.get(k, 0) < v:
                need[k] = v
        for r in reads:
            add(r.w)
        for w in writes:
            add(w.w)
            for k, v in w.r.items():
                add((k, v))
        own = self.sem[e]
        for k, v in need.items():
            if e == "pe" and k == own:
                continue
            if self.waited[e].get(k, 0) >= v:
                continue
            self.eng[e].wait_ge(self.semh[k], v)
            self.waited[e][k] = v

    def _commit(self, pt, reads, writes):
        k, v = pt
        for r in reads:
            if r.r.get(k, 0) < v:
                r.r[k] = v
        for w in writes:
            w.w = pt
            w.r = {}

    def op(self, e, emit, reads=(), writes=()):
        self._emit_waits(e, reads, writes)
        ins = emit()
        self.cnt[e] += 1
        assert self.cnt[e] < 60000
        ins.then_inc(self.semh[self.sem[e]], 1)
        self._commit((self.sem[e], self.cnt[e]), reads, writes)

    def group(self, e, emits, reads=(), writes=()):
        self._emit_waits(e, reads, writes)
        ins = None
        for f in emits:
            ins = f()
        self.cnt[e] += 1
        ins.then_inc(self.semh[self.sem[e]], 1)
        self._commit((self.sem[e], self.cnt[e]), reads, writes)

    def dma(self, q, dsem, emit, reads=(), writes=()):
        self._emit_waits(q, reads, writes)
        ins = emit()
        dsem.count += 16
        ins.then_inc(dsem.h, 16)
        self._commit((dsem.key, dsem.count), reads, writes)

    def wait_all(self, e, dsems):
        for d in dsems:
            if d.count:
                self.eng[e].wait_ge(d.h, d.count)


class _Stop(Exception):
    pass


def alias_edges(src, dst):
    for d in dst:
        for s_ in src:
            if s_.w is not None:
                k, v = s_.w
                if d.r.get(k, 0) < v:
                    d.r[k] = v
            for k, v in s_.r.items():
                if d.r.get(k, 0) < v:
                    d.r[k] = v


class Item:
    def __init__(self, name, gen, after=()):
        self.name = name
        self.gen = gen
        self.after = tuple(after)


def run_pipeline(items, width, stagger):
    held = {}
    done = set()
    active = []
    nxt = 0
    rnd = 0
    last_start = -10 ** 9
    while active or nxt < len(items):
        if nxt < len(items) and len(active) < width and (rnd - last_start >= stagger or not active):
            it = items[nxt]
            if all(a in done for a in it.after):
                active.append([it, None])
                nxt += 1
                last_start = rnd
            else:
                assert active, f"pipeline stuck at {it.name}"
        progressed = False
        for ent in list(active):
            it, pending = ent
            if pending is not None:
                if pending in held:
                    continue
                held[pending] = it.name
                ent[1] = None
                progressed = True
                continue
            try:
                r = next(it.gen)
            except StopIteration:
                active.remove(ent)
                done.add(it.name)
                for k in [k for k, v in held.items() if v == it.name]:
                    del held[k]
                progressed = True
                continue
            progressed = True
            if isinstance(r, tuple):
                if r[0] == "acq":
                    if r[1] in held:
                        ent[1] = r[1]
                    else:
                        held[r[1]] = it.name
                elif r[0] == "rel":
                    assert held.get(r[1]) == it.name, (r, held)
                    del held[r[1]]
                elif r[0] == "mark":
                    done.add(r[1])
        assert progressed or nxt < len(items), "pipeline deadlock"
        rnd += 1


def build_program(n_tasks=NT, stop_at=None):
    nc = bass.Bass("TRN2", target_bir_lowering=False)

    def ckpt(name):
        if stop_at is not None and name == stop_at:
            raise _Stop()

    def din(name, shape):
        return nc.dram_tensor(name, shape, F32, kind="ExternalInput").ap()

    xe = din("xe", [NT, TE, D])
    y = nc.dram_tensor("y", [NT, T, D], F32, kind="ExternalOutput").ap()
    w_in = din("w_in", [7, 128, 8192])
    w_out = din("w_out", [4, 128, 8192])
    w_up = din("w_up", [16, 128, 8192])
    w_down = din("w_down", [16, 128, 8192])
    msk_d = din("msk", [NT, 128, 3 * 384])
    cs_d = din("cs", [NT, 128, 96])
    sn_d = din("sn", [NT, 128, 96])
    gmixT_d = din("gmixT", [128, 16])
    gffnT_d = din("gffnT", [128, 16])
    goutT_d = din("goutT", [128, 16])
    gq_d = din("gq", [128, 128])
    gk_d = din("gk", [128, 128])
    gln_d = din("gln", [128, 1024])
    bln_d = din("bln", [128, 1024])
    sink_d = din("sinkb", [128, 8])
    wspT_d = din("wspT", [128, 1024])
    bsp_d = din("bsp", [128, 8])

    with contextlib.ExitStack() as st:
        P = Prog(nc, st)

        def sb(name, shape, dt):
            return st.enter_context(nc.sbuf_tensor(name, shape, dt))

        xres = sb("xres", [128, 4, D], F32)
        xst = sb("xst", [128, D], F32)
        xb0 = sb("xb", [128, D], BF16)
        big = sb("big", [128, 32768], BF16)
        mh = sb("mh", [128, 16, T], BF16)
        wsl = [sb(f"wsl{i}", [128, 16, 512], BF16) for i in range(NSLOT)]
        yst = sb("yst", [128, 4, 512], F32)
        tmpf0 = sb("tmpf", [128, 1024], F32)
        ident = sb("ident", [128, 128], BF16)
        identf = sb("identf", [128, 128], F32)
        gmixT = sb("gmixT_s", [128, 16], F32)
        gffnT = sb("gffnT_s", [128, 16], F32)
        goutT = sb("goutT_s", [128, 16], F32)
        gq = sb("gq_s", [128, 128], F32)
        gk = sb("gk_s", [128, 128], F32)
        gln = sb("gln_s", [128, 1024], F32)
        bln = sb("bln_s", [128, 1024], F32)
        sinkb = sb("sink_s", [128, 8], F32)
        nsink = sb("nsink_s", [128, 8], F32)
        wspT = sb("wspT_s", [128, 8, 128], BF16)
        bsp = sb("bsp_s", [128, 8], F32)
        msk = sb("msk_s", [128, 3, 384], F32)
        cs = sb("cs_s", [128, 6, 16], F32)
        sn = sb("sn_s", [128, 6, 16], F32)
        stats = [sb(f"stat{i}", [128, 32], F32) for i in range(9)]
        ropes = [sb(f"rope{i}", [128, 8, 2, 16], F32) for i in range(4)]
        bnsts = [sb(f"bnst{i}", [128, 2, 6], F32) for i in range(4)]
        rsA = sb("rsA", [128, 8], F32)
        rsA2 = sb("rsA2", [128, 8], F32)
        rsF = sb("rsF", [128, 4], F32)

        hT = big[:, 0:12288].rearrange("p (c t) -> p c t", t=TE)
        qT = big[:, 12288:16384].rearrange("p (h t) -> p h t", t=T)
        kT = big[:, 16384:17920].rearrange("p (h t) -> p h t", t=TE)
        vv = big[:, 17920:19456].rearrange("p (b n) -> p b n", n=256)
        vgn = big[:, 19456:23552].rearrange("p (b n) -> p b n", n=1024)
        uu = big[:, 23552:31744].bitcast(F32).rearrange("p (b n) -> p b n", n=1024)
        actT = big[:, :].rearrange("p (f t) -> p f t", t=T)
        smS, pbS, pTsS = [], [], []
        for i in range(2):
            o = i * 6144
            smS.append(big[:, o:o + 3072].bitcast(F32).rearrange("p (h k) -> p h k", k=384))
            pbS.append(big[:, o + 3072:o + 4608].rearrange("p (h k) -> p h k", k=384))
            pTsS.append(big[:, o + 4608:o + 6144].rearrange("p (j q) -> p j q", q=128))
        fxb = [xb0[:], big[:, 0:2048], big[:, 2048:4096]]
        mhf = mh[:].rearrange("p c t -> p (c t)")
        ystf = yst[:].rearrange("p a b -> p (a b)")
        xbS = [xb0[:], mhf[:, 0:2048], mhf[:, 2048:4096], ystf[:, 0:1024].bitcast(BF16)]
        tmpfS = [tmpf0[:], mhf[:, 4096:6144].bitcast(F32), mhf[:, 6144:8192].bitcast(F32), ystf[:, 1024:2048]]
        mixtok = [xst[:], ystf]
        xbD = [xb0[:], tmpf0[:].bitcast(BF16)]

        pmm = st.enter_context(nc.psum_tensor("pmm", [128, 8, 512], F32))
        TB = [pmm[:, 4:6, :].bitcast(BF16), pmm[:, 6:8, :].bitcast(BF16)]
        TB16 = [tb.rearrange("p a (c t) -> p (a c) t", t=128) for tb in TB]

        R_xres = [Res(f"xres{b}") for b in range(4)]
        R_xst = Res("xst")
        R_yst = Res("yst")
        R_mix = [R_xst, R_yst]
        R_xbS = [Res(f"xbS{i}") for i in range(4)]
        R_tmpfS = [Res(f"tmpfS{i}") for i in range(4)]
        R_fxb = [R_xbS[0], Res("fxb1"), Res("fxb2")]
        R_stat = [Res(f"stat{i}") for i in range(9)]
        R_rope = [Res(f"rope{i}") for i in range(4)]
        R_hT = [Res(f"hT{b}") for b in range(6)]
        R_qT = [Res(f"qT{b}") for b in range(4)]
        R_kT = [Res(f"kT{b}") for b in range(6)]
        R_v = [Res(f"v{b}") for b in range(6)]
        R_vgn = [Res(f"vgn{b}") for b in range(4)]
        R_u = [Res(f"u{b}") for b in range(4)]
        R_sm = [Res("sm0"), Res("sm1")]
        R_pb = [Res("pb0"), Res("pb1")]
        R_pTs = [Res("pTs0"), Res("pTs1")]
        R_actT = [Res(f"actT{g}") for g in range(16)]
        R_mh = [Res(f"mh{b}") for b in range(4)]
        R_wsl = [Res(f"wsl{i}") for i in range(NSLOT)]
        R_rl = [Res("rl0"), Res("rl1")]
        R_const = Res("const")
        R_task = Res("taskconst")
        R_bns = [Res(f"bn{i}") for i in range(4)]
        R_rsA = [Res(f"rsA{i}") for i in range(6)]
        R_rsF = [Res(f"rsF{i}") for i in range(4)]
        R_junkF = Res("junkF")
        R_bank = [Res(f"bank{i}") for i in range(8)]
        R_TB = [[R_bank[4], R_bank[5]], [R_bank[6], R_bank[7]]]
        arena_phase1 = R_hT + R_qT + R_kT + R_v + R_vgn + R_u
        dscratch_core = R_sm + R_pb + R_pTs
        fxb_res = [R_fxb[1], R_fxb[2], R_junkF]
        all_phase1 = arena_phase1 + dscratch_core + fxb_res
        mh_scratch = [R_xbS[1], R_xbS[2], R_tmpfS[1], R_tmpfS[2]]
        yst_scratch = [R_xbS[3], R_tmpfS[3]]

        d_const = DSem(P, "d_const")
        d_task = DSem(P, "d_task")
        d_xst = DSem(P, "d_xst")
        d_xres = [DSem(P, f"d_xres{b}") for b in range(4)]
        d_w = [DSem(P, f"d_w{i}") for i in range(NSLOT)]
        d_y = DSem(P, "d_y")
        d_yh = DSem(P, "d_yh")
        d_xc = [DSem(P, f"d_xc{i}") for i in range(4)]

        def xdst(eb):
            if eb in (0, 5):
                return xst[:], R_xst, d_xst
            return xres[:, eb - 1, :], R_xres[eb - 1], d_xres[eb - 1]

        NSET = 4
        SSTAT = [0, 1, 2, 8]

        sync, act, dve, pool, pe = nc.sync, nc.scalar, nc.vector, nc.gpsimd, nc.tensor

        for dst, src in ((gmixT, gmixT_d), (gffnT, gffnT_d), (goutT, goutT_d), (gq, gq_d),
                         (gk, gk_d), (gln, gln_d), (bln, bln_d), (sinkb, sink_d), (bsp, bsp_d)):
            P.dma("sp", d_const, lambda dst=dst, src=src: sync.dma_start(out=dst[:], in_=src[:, :]),
                  writes=[Res("c")])
        R_const.w = (d_const.key, d_const.count)
        d_wsp = DSem(P, "d_wsp")
        R_wsp = Res("wsp")
        P.dma("pool", d_wsp, lambda: pool.dma_start(out=wspT[:].rearrange("p h i -> p (h i)"), in_=wspT_d[:, :]),
              writes=[R_wsp])
        R_ident = Res("ident")
        P.op("pool", lambda: pool.memset(identf[:], 0.0), writes=[R_ident])
        P.op("pool", lambda: pool.affine_select(out=identf[:], in_=identf[:], pattern=[[-1, 128]],
                                                compare_op=ALU.not_equal, fill=1.0, base=0,
                                                channel_multiplier=1),
             reads=[R_ident], writes=[R_ident])
        P.op("dve", lambda: dve.tensor_copy(out=ident[:], in_=identf[:]), reads=[R_ident], writes=[R_ident])
        R_nsink = Res("nsink")
        P.op("dve", lambda: dve.tensor_scalar(out=nsink[:], in0=sinkb[:], scalar1=-1.0, scalar2=None,
                                              op0=ALU.mult), reads=[R_const], writes=[R_nsink])

        slabs = []
        for cg in (2, 0, 1, 3, 4, 5, 6):
            slabs.append(w_in[cg])
        for cg in range(4):
            slabs.append(w_out[cg])
        for fg in range(16):
            slabs.append(w_up[fg])
        for cg in range(4):
            for sub in range(4):
                slabs.append(w_down[cg * 4 + sub])
        n_per_task = len(slabs)
        total_slabs = n_per_task * n_tasks
        wstate = {"issued": 0}

        def ensure_loaded(upto):
            upto = min(upto, total_slabs - 1)
            while wstate["issued"] <= upto:
                n = wstate["issued"]
                s = n % NSLOT
                src = slabs[n % n_per_task]
                P.dma("pool", d_w[s], lambda s=s, src=src: pool.dma_start(
                    out=wsl[s][:].rearrange("p c n -> p (c n)"), in_=src), writes=[R_wsl[s]])
                wstate["issued"] += 1

        def g_load(upto):
            ensure_loaded(upto)
            yield

        def rstd_chain(stt, rs, col, n, width=1):
            a = stt[:, col:col + width]
            P.op("dve", lambda: dve.tensor_scalar(out=a, in0=a, scalar1=1.0 / n, scalar2=EPS,
                                                  op0=ALU.mult, op1=ALU.add), reads=[rs], writes=[rs])
            yield
            P.op("act", lambda: act.activation(out=a, in_=a, func=AF.Ln), reads=[rs], writes=[rs])
            yield
            P.op("act", lambda: act.activation(out=a, in_=a, func=AF.Exp, scale=-0.5), reads=[rs], writes=[rs])
            yield

        def g_rms_to_T(src_ap, src_res, gT, dst_ap, dst_res, xbt, r_xb, junk, r_junk, stt, rs, tbname,
                       mode, col, out_res, cast_dma=None):
            if cast_dma == "done":
                pass
            elif cast_dma is not None:
                dsem, src_d = cast_dma
                P.dma("pool", dsem, lambda: pool.dma_start(out=xbt, in_=src_d), writes=[r_xb])
            else:
                P.op("act", lambda: act.activation(out=xbt, in_=src_ap, func=AF.Copy),
                     reads=[src_res], writes=[r_xb])
            yield
            P.op("act", lambda: act.activation(out=junk, in_=src_ap, func=AF.Square, accum_out=stt[:, 0:1]),
                 reads=[src_res], writes=[r_junk, rs])
            yield ("mark_src_read",)
            ti = int(tbname[-1])
            yield ("acq", f"bank{4 + 2 * ti}")
            yield ("acq", f"bank{5 + 2 * ti}")
            P.group("pe", [(lambda c=c: pe.transpose(out=TB16[ti][:, c, :], in_=xbt[:, c * 128:(c + 1) * 128],
                                                     identity=ident[:])) for c in range(16)],
                    reads=[r_xb, R_ident], writes=R_TB[ti])
            yield
            a = stt[:, 0:1]
            P.op("dve", lambda: dve.tensor_tensor(out=dst_ap, in0=TB16[ti],
                                                  in1=gT[:].unsqueeze(2).broadcast_to([128, 16, 128]),
                                                  op=ALU.mult),
                 reads=R_TB[ti] + [R_const], writes=[dst_res])
            yield ("rel", f"bank{4 + 2 * ti}")
            yield ("rel", f"bank{5 + 2 * ti}")
            P.op("dve", lambda: dve.tensor_scalar(out=a, in0=a, scalar1=1.0 / D, scalar2=EPS,
                                                  op0=ALU.mult, op1=ALU.add), reads=[rs], writes=[rs])
            yield
            if mode == "A":
                P.op("act", lambda: act.activation(out=a, in_=a, func=AF.Ln), reads=[rs], writes=[rs])
                yield
                P.op("act", lambda: act.activation(out=rsA[:, col:col + 1], in_=a, func=AF.Exp, scale=-0.5),
                     reads=[rs], writes=[out_res])
                yield
                P.op("dve", lambda: dve.tensor_tensor(out=rsA2[:, col:col + 1], in0=rsA[:, col:col + 1],
                                                      in1=rsA[:, col:col + 1], op=ALU.mult),
                     reads=[out_res], writes=[out_res])
                yield
            else:
                P.op("dve", lambda: dve.reciprocal(out=rsF[:, col:col + 1], in_=a), reads=[rs], writes=[out_res])
                yield

        def g_qk_chain(src3, src_res, H, g, eb, dstb, r_dst, tf, r_tf, stt, rs, rp, r_rp):
            tv = tf[:, 0:H * 128]
            tv3 = tv.rearrange("p (h d) -> p h d", d=128)
            P.op("act", lambda: act.activation(out=tv3, in_=src3, func=AF.Square), reads=[src_res], writes=[r_tf])
            yield
            P.op("dve", lambda: dve.tensor_reduce(out=stt[:, 8:8 + H], in_=tv3, axis=AX.X, op=ALU.add),
                 reads=[r_tf], writes=[rs])
            yield
            P.op("dve", lambda: dve.tensor_scalar(out=stt[:, 8:8 + H], in0=stt[:, 8:8 + H],
                                                  scalar1=rsA2[:, eb:eb + 1], scalar2=None, op0=ALU.mult),
                 reads=[rs, R_rsA[eb]], writes=[rs])
            yield
            yield from rstd_chain(stt, rs, 8, 128, H)
            P.op("dve", lambda: dve.tensor_scalar(out=stt[:, 8:8 + H], in0=stt[:, 8:8 + H],
                                                  scalar1=rsA[:, eb:eb + 1], scalar2=None, op0=ALU.mult),
                 reads=[rs, R_rsA[eb]], writes=[rs])
            yield
            P.op("dve", lambda: dve.tensor_tensor(out=tv3, in0=src3,
                                                  in1=stt[:, 8:8 + H].unsqueeze(2).broadcast_to([128, H, 128]),
                                                  op=ALU.mult), reads=[src_res, rs], writes=[r_tf])
            yield ("rel", "srcbank")
            P.op("dve", lambda: dve.tensor_tensor(out=tv3, in0=tv3,
                                                  in1=g[:].unsqueeze(1).broadcast_to([128, H, 128]),
                                                  op=ALU.mult), reads=[r_tf, R_const], writes=[r_tf])
            yield
            cosb = cs[:, eb, :].unsqueeze(1).broadcast_to([128, H, 16])
            sinb = sn[:, eb, :].unsqueeze(1).broadcast_to([128, H, 16])
            x1 = tv3[:, :, 0:16]
            x2 = tv3[:, :, 16:32]
            dst3 = dstb.rearrange("p (h d) -> p h d", d=128)
            r3 = rp[:, 0:H, :, :]
            P.op("act", lambda: act.activation(out=dstb, in_=tv, func=AF.Copy), reads=[r_tf], writes=[r_dst])
            yield
            for (a1, b1, a2, b2, opx, lo) in ((x1, cosb, x2, sinb, ALU.subtract, 0),
                                               (x2, cosb, x1, sinb, ALU.add, 16)):
                P.op("dve", lambda a1=a1, b1=b1: dve.tensor_tensor(out=r3[:, :, 0, :], in0=a1, in1=b1, op=ALU.mult),
                     reads=[r_tf, R_task], writes=[r_rp])
                yield
                P.op("dve", lambda a2=a2, b2=b2: dve.tensor_tensor(out=r3[:, :, 1, :], in0=a2, in1=b2, op=ALU.mult),
                     reads=[r_tf, R_task, r_rp], writes=[r_rp])
                yield
                P.op("dve", lambda opx=opx, lo=lo: dve.tensor_tensor(out=dst3[:, :, lo:lo + 16], in0=r3[:, :, 0, :],
                                                                     in1=r3[:, :, 1, :], op=opx),
                     reads=[r_rp], writes=[r_dst])
                yield

        def g_A_kv(t, eb, si_, slot):
            sset = eb % NSET
            xbt, r_xb = xbS[sset], R_xbS[sset]
            tf, r_tf = tmpfS[sset], R_tmpfS[sset]
            stt, rs = stats[SSTAT[sset]], R_stat[SSTAT[sset]]
            rp, r_rp = ropes[sset], R_rope[sset]
            yield ("acq", f"set{sset}")
            dst_ap, dres, dsem = xdst(eb)
            if eb in (0, 5):
                yield ("acq", "xst")
            if eb == 5:
                P.dma("sp", dsem, lambda: sync.dma_start(out=dst_ap, in_=xe[t, eb * 128:(eb + 1) * 128, :]),
                      writes=[dres])
                yield
            tbname = f"tb{eb % 2}"
            for r in g_rms_to_T(dst_ap, dres, gmixT, hT[:, :, eb * 128:(eb + 1) * 128], R_hT[eb],
                                xbt, r_xb, tf.bitcast(BF16), r_tf, stt, rs, tbname, "A", eb, R_rsA[eb],
                                cast_dma=("done" if (t, eb) in prefetched else
                                          (d_xc[sset], xe[t, eb * 128:(eb + 1) * 128, :]))):
                if r == ("mark_src_read",):
                    if eb in (0, 5):
                        yield ("rel", "xst")
                    else:
                        yield
                else:
                    yield r
            bk = eb % 4
            yield ("acq", f"bank{bk}")
            P.group("pe", [(lambda c=c: pe.matmul(out=pmm[:, bk, :], lhsT=hT[:, c, eb * 128:(eb + 1) * 128],
                                                  rhs=wsl[slot][:, c, :], start=(c == 0), stop=(c == 15)))
                           for c in range(16)],
                    reads=[R_hT[eb], R_wsl[slot]], writes=[R_bank[bk]])
            yield
            P.op("act", lambda: act.activation(out=vv[:, eb, :], in_=pmm[:, bk, 256:512], func=AF.Copy,
                                               scale=rsA[:, eb:eb + 1]),
                 reads=[R_bank[bk], R_rsA[eb]], writes=[R_v[eb]])
            yield
            src3 = pmm[:, bk, 0:256].rearrange("p (h d) -> p h d", d=128)
            for r in g_qk_chain(src3, R_bank[bk], 2, gk, eb, xbt[:, 0:256], r_xb, tf, r_tf, stt, rs, rp, r_rp):
                if r == ("rel", "srcbank"):
                    yield ("rel", f"bank{bk}")
                else:
                    yield r
            ti = eb % 2
            yield ("acq", f"bank{4 + 2 * ti}")
            P.group("pe", [(lambda h=h: pe.transpose(out=TB[ti][:, 0, h * 128:(h + 1) * 128],
                                                     in_=xbt[:, h * 128:(h + 1) * 128], identity=ident[:]))
                           for h in range(2)],
                    reads=[r_xb, R_ident], writes=[R_TB[ti][0]])
            yield
            P.op("dve", lambda: dve.tensor_copy(out=kT[:, :, eb * 128:(eb + 1) * 128],
                                                in_=TB[ti][:, 0, 0:256].rearrange("p (h t) -> p h t", t=128)),
                 reads=[R_TB[ti][0]], writes=[R_kT[eb]])
            yield ("rel", f"bank{4 + 2 * ti}")
            yield ("rel", f"set{sset}")

        def g_q(t, b, slot0, slot1, idx):
            eb = b + 1
            sset = (idx + 2) % NSET
            xbt, r_xb = xbS[sset], R_xbS[sset]
            tf, r_tf = tmpfS[sset], R_tmpfS[sset]
            stt, rs = stats[SSTAT[sset]], R_stat[SSTAT[sset]]
            rp, r_rp = ropes[sset], R_rope[sset]
            yield ("acq", f"set{sset}")
            pr = (idx % 2) * 2
            yield ("acq", f"bank{pr}")
            yield ("acq", f"bank{pr + 1}")
            for j, slot in ((0, slot0), (1, slot1)):
                P.group("pe", [(lambda c=c, j=j, slot=slot: pe.matmul(
                    out=pmm[:, pr + j, :], lhsT=hT[:, c, eb * 128:(eb + 1) * 128], rhs=wsl[slot][:, c, :],
                    start=(c == 0), stop=(c == 15))) for c in range(16)],
                    reads=[R_hT[eb], R_wsl[slot]], writes=[R_bank[pr + j]])
                yield
            src3 = pmm[:, pr:pr + 2, :].rearrange("p a (h d) -> p (a h) d", d=128)

            for r in g_qk_chain2(src3, [R_bank[pr], R_bank[pr + 1]], 8, gq, eb, xbt[:, 0:1024], r_xb,
                                 tf, r_tf, stt, rs, rp, r_rp):
                if r == ("rel", "srcbank"):
                    yield ("rel", f"bank{pr}")
                    yield ("rel", f"bank{pr + 1}")
                else:
                    yield r
            ti = idx % 2
            yield ("acq", f"bank{4 + 2 * ti}")
            P.group("pe", [(lambda h=h: pe.transpose(out=TB[ti][:, 0, h * 128:(h + 1) * 128],
                                                     in_=xbt[:, h * 128:(h + 1) * 128], identity=ident[:]))
                           for h in range(8)],
                    reads=[r_xb, R_ident], writes=[R_TB[ti][0]])
            yield
            P.op("dve", lambda: dve.tensor_copy(out=qT[:, :, b * 128:(b + 1) * 128],
                                                in_=TB[ti][:, 0, :].rearrange("p (h t) -> p h t", t=128)),
                 reads=[R_TB[ti][0]], writes=[R_qT[b]])
            yield ("rel", f"bank{4 + 2 * ti}")
            yield ("rel", f"set{sset}")

        def g_qk_chain2(src3, src_rl, H, g, eb, dstb, r_dst, tf, r_tf, stt, rs, rp, r_rp):
            tv = tf[:, 0:H * 128]
            tv3 = tv.rearrange("p (h d) -> p h d", d=128)
            P.op("act", lambda: act.activation(out=tv3, in_=src3, func=AF.Square), reads=src_rl, writes=[r_tf])
            yield
            P.op("dve", lambda: dve.tensor_reduce(out=stt[:, 8:8 + H], in_=tv3, axis=AX.X, op=ALU.add),
                 reads=[r_tf], writes=[rs])
            yield
            P.op("dve", lambda: dve.tensor_scalar(out=stt[:, 8:8 + H], in0=stt[:, 8:8 + H],
                                                  scalar1=rsA2[:, eb:eb + 1], scalar2=None, op0=ALU.mult),
                 reads=[rs, R_rsA[eb]], writes=[rs])
            yield
            yield from rstd_chain(stt, rs, 8, 128, H)
            P.op("dve", lambda: dve.tensor_scalar(out=stt[:, 8:8 + H], in0=stt[:, 8:8 + H],
                                                  scalar1=rsA[:, eb:eb + 1], scalar2=None, op0=ALU.mult),
                 reads=[rs, R_rsA[eb]], writes=[rs])
            yield
            P.op("dve", lambda: dve.tensor_tensor(out=tv3, in0=src3,
                                                  in1=stt[:, 8:8 + H].unsqueeze(2).broadcast_to([128, H, 128]),
                                                  op=ALU.mult), reads=src_rl + [rs], writes=[r_tf])
            yield ("rel", "srcbank")
            P.op("dve", lambda: dve.tensor_tensor(out=tv3, in0=tv3,
                                                  in1=g[:].unsqueeze(1).broadcast_to([128, H, 128]),
                                                  op=ALU.mult), reads=[r_tf, R_const], writes=[r_tf])
            yield
            cosb = cs[:, eb, :].unsqueeze(1).broadcast_to([128, H, 16])
            sinb = sn[:, eb, :].unsqueeze(1).broadcast_to([128, H, 16])
            x1 = tv3[:, :, 0:16]
            x2 = tv3[:, :, 16:32]
            dst3 = dstb.rearrange("p (h d) -> p h d", d=128)
            r3 = rp[:, 0:H, :, :]
            P.op("act", lambda: act.activation(out=dstb, in_=tv, func=AF.Copy), reads=[r_tf], writes=[r_dst])
            yield
            for (a1, b1, a2, b2, opx, lo) in ((x1, cosb, x2, sinb, ALU.subtract, 0),
                                               (x2, cosb, x1, sinb, ALU.add, 16)):
                P.op("dve", lambda a1=a1, b1=b1: dve.tensor_tensor(out=r3[:, :, 0, :], in0=a1, in1=b1, op=ALU.mult),
                     reads=[r_tf, R_task], writes=[r_rp])
                yield
                P.op("dve", lambda a2=a2, b2=b2: dve.tensor_tensor(out=r3[:, :, 1, :], in0=a2, in1=b2, op=ALU.mult),
                     reads=[r_tf, R_task, r_rp], writes=[r_rp])
                yield
                P.op("dve", lambda opx=opx, lo=lo: dve.tensor_tensor(out=dst3[:, :, lo:lo + 16], in0=r3[:, :, 0, :],
                                                                     in1=r3[:, :, 1, :], op=opx),
                     reads=[r_rp], writes=[r_dst])
                yield

        def g_u(b, j, slot, i):
            eb = b + 1
            bk5 = i % 4
            yield ("acq", f"bank{bk5}")
            P.group("pe", [(lambda c=c: pe.matmul(out=pmm[:, bk5, :], lhsT=hT[:, c, eb * 128:(eb + 1) * 128],
                                                  rhs=wsl[slot][:, c, :], start=(c == 0), stop=(c == 15)))
                           for c in range(16)],
                    reads=[R_hT[eb], R_wsl[slot]], writes=[R_bank[bk5]])
            yield
            P.op("act", lambda: act.activation(out=uu[:, b, j * 512:(j + 1) * 512], in_=pmm[:, bk5, :],
                                               func=AF.Gelu_apprx_tanh, scale=rsA[:, eb:eb + 1]),
                 reads=[R_bank[bk5], R_rsA[eb]], writes=[R_u[b]])
            yield ("rel", f"bank{bk5}")

        def g_vg(b, slot0, slot1, i):
            eb = b + 1
            sset = i % NSET
            stt, rs = stats[SSTAT[sset]], R_stat[SSTAT[sset]]
            tf, r_tf = tmpfS[sset], R_tmpfS[sset]
            bnst, R_bn = bnsts[sset], R_bns[sset]
            yield ("acq", f"set{sset}")
            for j, slot in ((0, slot0), (1, slot1)):
                bk5 = (2 * i + j) % 4
                yield ("acq", f"bank{bk5}")
                P.group("pe", [(lambda c=c, slot=slot, bk5=bk5: pe.matmul(
                    out=pmm[:, bk5, :], lhsT=hT[:, c, eb * 128:(eb + 1) * 128], rhs=wsl[slot][:, c, :],
                    start=(c == 0), stop=(c == 15))) for c in range(16)],
                    reads=[R_hT[eb], R_wsl[slot]], writes=[R_bank[bk5]])
                yield
                P.op("act", lambda j=j, bk5=bk5: act.activation(out=tf[:, j * 512:(j + 1) * 512], in_=pmm[:, bk5, :],
                                                       func=AF.Gelu_apprx_tanh, scale=rsA[:, eb:eb + 1]),
                     reads=[R_bank[bk5], R_rsA[eb]], writes=[r_tf])
                yield ("rel", f"bank{bk5}")
                P.op("dve", lambda j=j: dve.bn_stats(out=bnst[:, j, :], in_=tf[:, j * 512:(j + 1) * 512]),
                     reads=[r_tf], writes=[R_bn])
                yield
            P.op("dve", lambda: dve.bn_aggr(out=stt[:, 16:18], in_=bnst[:].rearrange("p a b -> p (a b)")),
                 reads=[R_bn], writes=[rs])
            yield
            P.op("dve", lambda: dve.tensor_scalar(out=stt[:, 17:18], in0=stt[:, 17:18], scalar1=EPS, scalar2=None,
                                                  op0=ALU.add), reads=[rs], writes=[rs])
            yield
            P.op("act", lambda: act.activation(out=stt[:, 17:18], in_=stt[:, 17:18], func=AF.Ln),
                 reads=[rs], writes=[rs])
            yield
            P.op("act", lambda: act.activation(out=stt[:, 17:18], in_=stt[:, 17:18], func=AF.Exp, scale=-0.5),
                 reads=[rs], writes=[rs])
            yield
            P.op("dve", lambda: dve.tensor_scalar(out=tf[:], in0=tf[:], scalar1=stt[:, 16:17],
                                                  scalar2=stt[:, 17:18], op0=ALU.subtract, op1=ALU.mult),
                 reads=[r_tf, rs], writes=[r_tf])
            yield
            P.op("dve", lambda: dve.tensor_tensor(out=tf[:], in0=tf[:], in1=gln[:], op=ALU.mult),
                 reads=[r_tf, R_const], writes=[r_tf])
            yield
            P.op("dve", lambda: dve.tensor_tensor(out=vgn[:, b, :], in0=tf[:], in1=bln[:], op=ALU.add),
                 reads=[r_tf, R_const], writes=[R_vgn[b]])
            yield ("rel", f"set{sset}")

        def g_att(b, kvg, unit):
            eb = b + 1
            ds = unit % 2
            sm_, pb_, pT_ = smS[ds], pbS[ds], pTsS[ds]
            sti = 3 + unit % 4
            stt, rs = stats[sti], R_stat[sti]
            kind = 0 if b == 0 else (2 if b == 3 else 1)
            mt, r_mt = mixtok[b % 2], R_mix[b % 2]
            yield ("acq", f"st{sti}")
            yield ("acq", f"sm{ds}")
            for k_ in range(4):
                yield ("acq", f"bank{k_}")
            P.group("pe", [(lambda hh=hh: pe.matmul(
                out=pmm[:, hh, 0:384], lhsT=qT[:, 4 * kvg + hh, b * 128:(b + 1) * 128],
                rhs=kT[:, kvg, (eb - 1) * 128:(eb + 2) * 128], start=True, stop=True)) for hh in range(4)],
                reads=[R_qT[b], R_kT[eb - 1], R_kT[eb], R_kT[eb + 1]], writes=R_bank[0:4])
            yield
            P.op("dve", lambda: dve.scalar_tensor_tensor(
                out=sm_, in0=pmm[:, 0:4, 0:384], scalar=SCALE,
                in1=msk[:, kind, :].unsqueeze(1).broadcast_to([128, 4, 384]), op0=ALU.mult, op1=ALU.add),
                reads=R_bank[0:4] + [R_task], writes=[R_sm[ds]])
            for k_ in range(4):
                yield ("rel", f"bank{k_}")
            P.op("dve", lambda: dve.tensor_reduce(out=stt[:, 0:4], in_=sm_, axis=AX.X, op=ALU.max),
                 reads=[R_sm[ds]], writes=[rs])
            yield
            P.op("dve", lambda: dve.scalar_tensor_tensor(out=stt[:, 4:8], in0=stt[:, 0:4], scalar=-1.0,
                                                         in1=nsink[:, 4 * kvg:4 * kvg + 4], op0=ALU.mult,
                                                         op1=ALU.min),
                 reads=[rs, R_nsink], writes=[rs])
            yield
            yield ("acq", f"pb{ds}")
            for hh in range(4):
                P.op("act", lambda hh=hh: act.activation(out=pb_[:, hh, :], in_=sm_[:, hh, :], func=AF.Exp,
                                                         bias=stt[:, 4 + hh:5 + hh], scale=1.0,
                                                         accum_out=stt[:, 8 + hh:9 + hh]),
                     reads=[R_sm[ds], rs], writes=[R_pb[ds], rs])
                yield
            yield ("rel", f"sm{ds}")
            P.op("dve", lambda: dve.tensor_tensor(out=stt[:, 12:16], in0=sinkb[:, 4 * kvg:4 * kvg + 4],
                                                  in1=stt[:, 4:8], op=ALU.add), reads=[rs, R_const], writes=[rs])
            yield
            P.op("act", lambda: act.activation(out=stt[:, 12:16], in_=stt[:, 12:16], func=AF.Exp),
                 reads=[rs], writes=[rs])
            yield
            P.op("dve", lambda: dve.tensor_tensor(out=stt[:, 16:20], in0=stt[:, 8:12], in1=stt[:, 12:16],
                                                  op=ALU.add), reads=[rs], writes=[rs])
            yield
            P.op("dve", lambda: dve.reciprocal(out=stt[:, 20:24], in_=stt[:, 16:20]), reads=[rs], writes=[rs])
            yield
            yield ("acq", "bank6")
            yield ("acq", "bank7")
            tbf = TB16[1]
            P.group("pe", [(lambda hh=hh, kb=kb: pe.transpose(
                out=tbf[:, hh * 3 + kb, :], in_=pb_[:, hh, kb * 128:(kb + 1) * 128], identity=ident[:]))
                for hh in range(4) for kb in range(3)],
                reads=[R_pb[ds], R_ident], writes=R_TB[1])
            yield ("rel", f"pb{ds}")
            yield ("acq", f"pT{ds}")
            P.op("act", lambda: act.activation(out=pT_, in_=tbf[:, 0:12, :], func=AF.Copy),
                 reads=R_TB[1], writes=[R_pTs[ds]])
            yield ("rel", "bank6")
            yield ("rel", "bank7")
            yield ("acq", "bank4")
            P.group("pe", [(lambda hh=hh, kb=kb: pe.matmul(
                out=pmm[:, 4, hh * 128:(hh + 1) * 128], lhsT=pT_[:, hh * 3 + kb, :],
                rhs=vv[:, eb - 1 + kb, kvg * 128:(kvg + 1) * 128], start=(kb == 0), stop=(kb == 2)))
                for hh in range(4) for kb in range(3)],
                reads=[R_pTs[ds], R_v[eb - 1], R_v[eb], R_v[eb + 1]], writes=[R_bank[4]])
            yield ("rel", f"pT{ds}")
            P.op("dve", lambda: dve.tensor_tensor(
                out=mt[:, kvg * 512:(kvg + 1) * 512].rearrange("p (h d) -> p h d", d=128),
                in0=pmm[:, 4, :].rearrange("p (h d) -> p h d", d=128),
                in1=stt[:, 20:24].unsqueeze(2).broadcast_to([128, 4, 128]), op=ALU.mult),
                reads=[R_bank[4], rs], writes=[r_mt])
            yield ("rel", "bank4")
            yield ("rel", f"st{sti}")

        def g_gmn(b):
            mt, r_mt = mixtok[b % 2], R_mix[b % 2]
            stt, rs = stats[7], R_stat[7]
            xbt, r_xb = xbD[b % 2], (R_xbS[0] if b % 2 == 0 else R_tmpfS[0])
            yield ("acq", "bank0")
            yield ("acq", "bank1")
            P.group("pe", [(lambda h=h: pe.matmul(
                out=pmm[:, h // 4, (h % 4) * 128:(h % 4 + 1) * 128], lhsT=wspT[:, h, :],
                rhs=vgn[:, b, h * 128:(h + 1) * 128], start=True, stop=True)) for h in range(8)],
                reads=[R_wsp, R_vgn[b]], writes=[R_bank[0], R_bank[1]])
            yield
            for h in range(8):
                P.op("dve", lambda h=h: dve.scalar_tensor_tensor(
                    out=mt[:, 1024 + h * 128:1024 + (h + 1) * 128],
                    in0=pmm[:, h // 4, (h % 4) * 128:(h % 4 + 1) * 128], scalar=bsp[:, h:h + 1],
                    in1=uu[:, b, h * 128:(h + 1) * 128], op0=ALU.add, op1=ALU.mult),
                    reads=[R_bank[h // 4], R_const, R_u[b]], writes=[r_mt])
                yield
            yield ("rel", "bank0")
            yield ("rel", "bank1")
            yield ("acq", f"xbD{b % 2}")
            yield ("acq", "st7")
            for half in range(2):
                src = mt[:, half * 1024:(half + 1) * 1024]
                dstb = xbt[:, half * 1024:(half + 1) * 1024]
                P.op("act", lambda src=src, dstb=dstb: act.activation(out=dstb, in_=src, func=AF.Square,
                                                                      accum_out=stt[:, half:half + 1]),
                     reads=[r_mt], writes=[r_xb, rs])
                yield
            yield from rstd_chain(stt, rs, 0, 1024, 2)
            for half in range(2):
                src = mt[:, half * 1024:(half + 1) * 1024]
                dstb = xbt[:, half * 1024:(half + 1) * 1024]
                P.op("act", lambda src=src, dstb=dstb, half=half: act.activation(
                    out=dstb, in_=src, func=AF.Copy, scale=stt[:, half:half + 1]),
                    reads=[r_mt, rs], writes=[r_xb])
                yield
            yield ("rel", "st7")
            yield ("mark", f"m{b}")
            yield ("acq", "bank4")
            yield ("acq", "bank5")
            P.group("pe", [(lambda c=c: pe.transpose(out=TB16[0][:, c, :], in_=xbt[:, c * 128:(c + 1) * 128],
                                                     identity=ident[:])) for c in range(16)],
                    reads=[r_xb, R_ident], writes=R_TB[0])
            yield ("rel", f"xbD{b % 2}")
            P.op("dve", lambda: dve.tensor_tensor(out=mh[:, :, b * 128:(b + 1) * 128], in0=TB16[0],
                                                  in1=goutT[:].unsqueeze(2).broadcast_to([128, 16, 128]),
                                                  op=ALU.mult),
                 reads=R_TB[0] + [R_const], writes=[R_mh[b]])
            yield ("rel", "bank4")
            yield ("rel", "bank5")

        def g_F(b, i):
            xbt, r_xb = fxb[i % 3], R_fxb[i % 3]
            yield ("acq", f"fxb{i % 3}")
            for r in g_rms_to_T(xres[:, b, :], R_xres[b], gffnT, mh[:, :, b * 128:(b + 1) * 128], R_mh[b],
                                xbt, r_xb, big[:, 4096:6144], R_junkF, stats[i % 3], R_stat[i % 3],
                                f"tb{i % 2}", "F", b, R_rsF[b]):
                yield (None if r == ("mark_src_read",) else r)
            yield ("rel", f"fxb{i % 3}")

        prefetched = set()

        def issue_casts(tt, ebs):
            for eb in ebs:
                sset = eb % NSET
                P.dma("pool", d_xc[sset], lambda sset=sset, eb=eb: pool.dma_start(
                    out=xbS[sset], in_=xe[tt, eb * 128:(eb + 1) * 128, :]), writes=[R_xbS[sset]])
                prefetched.add((tt, eb))

        bank_rot = {"i": 0}

        def next_bank(n=4):
            b = bank_rot["i"]
            bank_rot["i"] = (b + 1) % n
            return b

        try:
          ckpt("const")
          for t in range(n_tasks):
            base = t * n_per_task
            alias_edges(R_mh, mh_scratch)
            alias_edges([R_yst] + R_rl, yst_scratch)
            issue_casts(t, [0, 1, 2, 3])
            ensure_loaded(base + 2)

            for i_, (dst, src) in enumerate(((msk, msk_d), (cs, cs_d), (sn, sn_d))):
                P.dma("sp", d_task, lambda dst=dst, src=src: sync.dma_start(
                    out=dst[:].rearrange("p a b -> p (a b)"), in_=src[t]),
                    writes=[R_task if i_ == 0 else Res("tc")])
            R_task.w = (d_task.key, d_task.count)
            for eb in range(5):
                dst_ap, dres, dsem = xdst(eb)
                P.dma("sp", dsem, lambda dst_ap=dst_ap, eb=eb: sync.dma_start(
                    out=dst_ap, in_=xe[t, eb * 128:(eb + 1) * 128, :]), writes=[dres])

            alias_edges(R_mh, mh_scratch)
            alias_edges([R_yst] + R_rl, yst_scratch)
            alias_edges(R_actT, arena_phase1)

            s_kv, s_q0, s_q1 = base % NSLOT, (base + 1) % NSLOT, (base + 2) % NSLOT
            s_u = [(base + 3) % NSLOT, (base + 4) % NSLOT]
            s_vg = [(base + 5) % NSLOT, (base + 6) % NSLOT]
            allA = [f"A{eb}" for eb in range(6)]
            allq = [f"q{b}" for b in range(4)]
            items = [Item(f"A{eb}", g_A_kv(t, eb, 0, s_kv)) for eb in range(6)]
            items += [Item(f"q{b}", g_q(t, b, s_q0, s_q1, b), after=[f"A{b + 1}"]) for b in range(4)]
            items.append(Item("ld_u0", g_load(base + 3), after=allA))
            items += [Item(f"u0{b}", g_u(b, 0, s_u[0], b)) for b in range(4)]
            items.append(Item("ld_u1", g_load(base + 5), after=allq))
            items += [Item(f"u1{b}", g_u(b, 1, s_u[1], b)) for b in range(4)]
            items.append(Item("ld_vg1", g_load(base + 6), after=[f"u0{b}" for b in range(4)]))
            items += [Item(f"vg{b}", g_vg(b, s_vg[0], s_vg[1], b)) for b in range(4)]
            run_pipeline(items, width=4, stagger=7)
            ckpt("AB")

            ensure_loaded(base + 7)
            alias_edges(R_hT, dscratch_core)
            alias_edges(mh_scratch, R_mh)
            alias_edges(yst_scratch, [R_yst])
            items = []
            unit = 0
            for b in range(4):
                for kvg in range(2):
                    items.append(Item(f"a{b}{kvg}", g_att(b, kvg, unit), after=([f"m{b - 2}"] if b >= 2 else [])))
                    unit += 1
                if b >= 1:
                    items.append(Item(f"g{b - 1}", g_gmn(b - 1), after=[f"a{b - 1}0", f"a{b - 1}1"]))
            items.append(Item("g3", g_gmn(3), after=["a30", "a31"]))
            run_pipeline(items, width=6, stagger=3)
            ckpt("D")
            si = 7

            bank_rot["i"] = 0
            for cg in range(4):
                n = base + si
                ensure_loaded(n + 2)
                s = n % NSLOT
                for b in range(4):
                    bk = next_bank()
                    P.group("pe", [(lambda c=c, b=b, s=s, bk=bk: pe.matmul(
                        out=pmm[:, bk, :], lhsT=mh[:, c, b * 128:(b + 1) * 128], rhs=wsl[s][:, c, :],
                        start=(c == 0), stop=(c == 15))) for c in range(16)],
                        reads=[R_mh[b], R_wsl[s]], writes=[R_bank[bk]])
                    P.op("dve", lambda b=b, cg=cg, bk=bk: dve.tensor_tensor(
                        out=xres[:, b, cg * 512:(cg + 1) * 512], in0=pmm[:, bk, :],
                        in1=xres[:, b, cg * 512:(cg + 1) * 512], op=ALU.add),
                        reads=[R_bank[bk], R_xres[b]], writes=[R_xres[b]])
                si += 1
            ckpt("E")

            alias_edges(dscratch_core, fxb_res)
            items = [Item(f"F{b}", g_F(b, b)) for b in range(4)]
            run_pipeline(items, width=3, stagger=4)
            ckpt("F")

            bank_rot["i"] = 0
            for fg in range(16):
                n = base + si
                ensure_loaded(n + 2)
                s = n % NSLOT
                for fi in range(4):
                    fc = fg * 4 + fi
                    bk = next_bank(2)
                    P.group("pe", [(lambda c=c, fi=fi, s=s, bk=bk: pe.matmul(
                        out=pmm[:, bk, :], lhsT=wsl[s][:, c, fi * 128:(fi + 1) * 128], rhs=mh[:, c, :],
                        start=(c == 0), stop=(c == 15))) for c in range(16)],
                        reads=R_mh + [R_wsl[s]], writes=[R_bank[bk]])
                    rl = yst[:, bk, :]
                    P.op("act", lambda bk=bk, rl=rl: act.activation(out=rl, in_=pmm[:, bk, :], func=AF.Relu),
                         reads=[R_bank[bk]], writes=[R_rl[bk], R_yst])
                    extra = all_phase1 if (fg == 0 and fi == 0) else []
                    P.op("dve", lambda fc=fc, rl=rl: dve.tensor_tensor(out=actT[:, fc, :], in0=rl, in1=rl,
                                                                       op=ALU.mult),
                         reads=[R_rl[bk]], writes=[R_actT[fg]] + extra)
                si += 1
            ckpt("G")

            for cg in range(4):
                for sub in range(4):
                    n = base + si
                    ensure_loaded(n + 2)
                    s = n % NSLOT
                    for b in range(4):
                        P.group("pe", [(lambda fc=fc, b=b, s=s, sub=sub: pe.matmul(
                            out=pmm[:, 2 + b, :], lhsT=actT[:, sub * 16 + fc, b * 128:(b + 1) * 128],
                            rhs=wsl[s][:, fc, :], start=(sub == 0 and fc == 0),
                            stop=(sub == 3 and fc == 15))) for fc in range(16)],
                            reads=[R_actT[sub * 4 + k] for k in range(4)] + [R_wsl[s]],
                            writes=[R_bank[2 + b]])
                        if sub == 3:
                            P.op("dve", lambda b=b, cg=cg: dve.scalar_tensor_tensor(
                                out=yst[:, b, :], in0=pmm[:, 2 + b, :], scalar=rsF[:, b:b + 1],
                                in1=xres[:, b, cg * 512:(cg + 1) * 512], op0=ALU.mult, op1=ALU.add),
                                reads=[R_bank[2 + b], R_xres[b], R_rsF[b]],
                                writes=[R_yst] + ([R_rl[b]] if b < 2 else []))
                    si += 1
                P.dma("sp", d_y, lambda cg=cg: sync.dma_start(
                    out=y[t].rearrange("(b p) n -> p b n", p=128)[:, :, cg * 512:(cg + 1) * 512],
                    in_=yst[:]), reads=[R_yst])
            assert si == n_per_task
        except _Stop:
            pass
        for e in ("pe", "act", "dve", "pool"):
            if P.cnt[e]:
                sync.wait_ge(P.semh[P.sem[e]], P.cnt[e])
        P.wait_all("sp", [d_const, d_task, d_xst, d_yh, d_wsp] + d_xres + d_w + d_xc + [d_y])
    return nc


def _task_table():
    tasks = []
    for s in range(2):
        for j in range(8):
            tasks.append(("p", s, j * T))
    for s in range(4):
        for j in range(4):
            tasks.append(("s", s, j * T))
    return tasks


def _rope_tables(pos):
    inv_freq = 500000.0 ** (-np.arange(0, 32, 2, dtype=np.float64) / 32.0)
    ang = pos.astype(np.float64)[:, None] * inv_freq[None, :]
    return np.cos(ang).astype(np.float32), np.sin(ang).astype(np.float32)


_NC_CACHE = {}


def kernel(x_prompt, x_sample, g_mix, w_in, g_q, g_k, sink, g_v_ln, b_v_ln, w_spatial,
           b_spatial, g_attn_out, g_gmlp_out, w_out, g_ffn, w_up, w_down):
    f32 = np.float32
    x_prompt = np.asarray(x_prompt, f32)
    x_sample = np.asarray(x_sample, f32)
    tasks = _task_table()

    def colT(v):
        return np.ascontiguousarray(np.asarray(v, f32).reshape(16, 128).T)

    def bcast(v, n):
        return np.ascontiguousarray(np.broadcast_to(np.asarray(v, f32).reshape(1, n), (128, n)))

    shared = {
        "w_in": np.ascontiguousarray(
            np.asarray(w_in, f32)[0].reshape(16, 128, 7, 512).transpose(2, 1, 0, 3)).reshape(7, 128, 8192),
        "w_out": np.ascontiguousarray(
            np.asarray(w_out, f32)[0].reshape(16, 128, 4, 512).transpose(2, 1, 0, 3)).reshape(4, 128, 8192),
        "w_up": np.ascontiguousarray(
            np.asarray(w_up, f32)[0].reshape(16, 128, 16, 512).transpose(2, 1, 0, 3)).reshape(16, 128, 8192),
        "w_down": np.ascontiguousarray(
            np.asarray(w_down, f32)[0].reshape(4, 16, 128, 4, 512).transpose(3, 0, 2, 1, 4)).reshape(16, 128, 8192),
        "gmixT": colT(g_mix[0]),
        "gffnT": colT(g_ffn[0]),
        "goutT": colT(np.concatenate([np.asarray(g_attn_out, f32)[0], np.asarray(g_gmlp_out, f32)[0]])),
        "gq": bcast(g_q[0], 128),
        "gk": bcast(g_k[0], 128),
        "gln": bcast(g_v_ln[0], 1024),
        "bln": bcast(b_v_ln[0], 1024),
        "sinkb": bcast(sink[0], 8),
        "wspT": np.ascontiguousarray(np.transpose(np.asarray(w_spatial, f32)[0], (2, 0, 1)).reshape(128, 1024)),
        "bsp": np.ascontiguousarray(np.asarray(b_spatial, f32)[0].T),
    }

    qi = np.arange(128)[:, None]
    kj = np.arange(384)[None, :]
    band = (kj >= qi) & (kj <= qi + 256)

    in_maps = []
    for c in range(N_CORES):
        xe = np.zeros((NT, TE, D), f32)
        msk = np.empty((NT, 128, 3, 384), f32)
        cs = np.empty((NT, 128, 6, 16), f32)
        sn = np.empty((NT, 128, 6, 16), f32)
        for i in range(NT):
            grp, s, start = tasks[c * NT + i]
            xs = x_prompt[s] if grp == "p" else x_sample[s]
            S = xs.shape[0]
            lo, hi = start - 128, start + T + 128
            a, b = max(lo, 0), min(hi, S)
            xe[i, a - lo:b - lo] = xs[a:b]
            left_ok = lo >= 0
            right_ok = hi <= S
            m0 = band if left_ok else (band & (kj >= 128))
            m2 = band if right_ok else (band & (kj < 256))
            for kind, m in enumerate((m0, band, m2)):
                msk[i, :, kind, :] = np.where(m, 0.0, NEG).astype(f32)
            pos = np.arange(lo, hi)
            co, si_ = _rope_tables(pos)
            cs[i] = co.reshape(6, 128, 16).transpose(1, 0, 2)
            sn[i] = si_.reshape(6, 128, 16).transpose(1, 0, 2)
        m = dict(shared)
        m["xe"] = xe
        m["msk"] = msk.reshape(NT, 128, 3 * 384)
        m["cs"] = cs.reshape(NT, 128, 96)
        m["sn"] = sn.reshape(NT, 128, 96)
        in_maps.append(m)

    if "nc" not in _NC_CACHE:
        _NC_CACHE["nc"] = build_program()
    nc = _NC_CACHE["nc"]
    res = run_bass_kernel_spmd(nc, in_maps, core_ids=list(range(N_CORES)))

    y_prompt = np.empty((2, 4096, D), f32)
    y_sample = np.empty((4, 2048, D), f32)
    for c in range(N_CORES):
        yc = np.asarray(res.results[c]["y"], f32)
        for i in range(NT):
            grp, s, start = tasks[c * NT + i]
            if grp == "p":
                y_prompt[s, start:start + T] = yc[i]
            else:
                y_sample[s, start:start + T] = yc[i]
    return (y_prompt, y_sample)
```

```python
import contextlib
import math
import numpy as np
import concourse.bass as bass
import concourse.mybir as mybir
from concourse.bass_utils import run_bass_kernel_spmd

F32 = mybir.dt.float32
BF16 = mybir.dt.bfloat16
AF = mybir.ActivationFunctionType
ALU = mybir.AluOpType
AX = mybir.AxisListType

N_CORES = 8
NT = 4
T = 512
TE = 768
D = 2048
DFF = 8192
EPS = 1e-6
NEG = -1e30
SCALE = 1.0 / math.sqrt(128.0)
NSLOT = 3


class Res:
    __slots__ = ("name", "w", "r")

    def __init__(self, name):
        self.name = name
        self.w = None
        self.r = {}


class DSem:
    def __init__(self, prog, name):
        self.h = prog.new_sem(name)
        self.key = name
        self.count = 0


class Prog:
    ENGS = ("pe", "act", "dve", "pool", "sp")

    def __init__(self, nc, stack):
        self.nc = nc
        self.stack = stack
        self.eng = {"pe": nc.tensor, "act": nc.scalar, "dve": nc.vector,
                    "pool": nc.gpsimd, "sp": nc.sync}
        self.semh = {}
        self.sem = {}
        self.cnt = {}
        self.waited = {e: {} for e in self.ENGS}
        for e in self.ENGS:
            self.sem[e] = "E_" + e
            self.semh["E_" + e] = stack.enter_context(nc.semaphore("E_" + e))
            self.cnt[e] = 0

    def new_sem(self, name):
        h = self.stack.enter_context(self.nc.semaphore(name))
        self.semh[name] = h
        return h

    def _emit_waits(self, e, reads, writes):
        need = {}

        def add(pt):
            if pt is None:
                return
            k, v = pt
            if need.get(k, 0) < v:
                need[k] = v
        for r in reads:
            add(r.w)
        for w in writes:
            add(w.w)
            for k, v in w.r.items():
                add((k, v))
        own = self.sem[e]
        for k, v in need.items():
            if e == "pe" and k == own:
                continue
            if self.waited[e].get(k, 0) >= v:
                continue
            self.eng[e].wait_ge(self.semh[k], v)
            self.waited[e][k] = v

    def _commit(self, pt, reads, writes):
        k, v = pt
        for r in reads:
            if r.r.get(k, 0) < v:
                r.r[k] = v
        for w in writes:
            w.w = pt
            w.r = {}

    def op(self, e, emit, reads=(), writes=()):
        self._emit_waits(e, reads, writes)
        ins = emit()
        self.cnt[e] += 1
        assert self.cnt[e] < 60000
        ins.then_inc(self.semh[self.sem[e]], 1)
        self._commit((self.sem[e], self.cnt[e]), reads, writes)

    def group(self, e, emits, reads=(), writes=()):
        self._emit_waits(e, reads, writes)
        ins = None
        for f in emits:
            ins = f()
        self.cnt[e] += 1
        ins.then_inc(self.semh[self.sem[e]], 1)
        self._commit((self.sem[e], self.cnt[e]), reads, writes)

    def dma(self, q, dsem, emit, reads=(), writes=()):
        self._emit_waits(q, reads, writes)
        ins = emit()
        dsem.count += 16
        ins.then_inc(dsem.h, 16)
        self._commit((dsem.key, dsem.count), reads, writes)

    def wait_all(self, e, dsems):
        for d in dsems:
            if d.count:
                self.eng[e].wait_ge(d.h, d.count)


class _Stop(Exception):
    pass


def alias_edges(src, dst):
    for d in dst:
        for s_ in src:
            if s_.w is not None:
                k, v = s_.w
                if d.r.get(k, 0) < v:
                    d.r[k] = v
            for k, v in s_.r.items():
                if d.r.get(k, 0) < v:
                    d.r[k] = v


class Item:
    def __init__(self, name, gen, after=()):
        self.name = name
        self.gen = gen
        self.after = tuple(after)


def run_pipeline(items, width, stagger, lookahead=1):
    held = {}
    done = set()
    active = []
    pending_items = list(items)
    rnd = 0
    last_start = -10 ** 9
    while active or pending_items:
        if pending_items and len(active) < width and (rnd - last_start >= stagger or not active):
            for it in pending_items[:lookahead]:
                if all(a in done for a in it.after):
                    pending_items.remove(it)
                    active.append([it, None])
                    last_start = rnd
                    break
            else:
                assert active, f"pipeline stuck at {pending_items[0].name}"
        progressed = False
        for ent in list(active):
            it, pending = ent
            if pending is not None:
                if pending in held:
                    continue
                held[pending] = it.name
                ent[1] = None
                progressed = True
                continue
            try:
                r = next(it.gen)
            except StopIteration:
                active.remove(ent)
                done.add(it.name)
                for k in [k for k, v in held.items() if v == it.name]:
                    del held[k]
                progressed = True
                continue
            progressed = True
            if isinstance(r, tuple):
                if r[0] == "acq":
                    if r[1] in held:
                        ent[1] = r[1]
                    else:
                        held[r[1]] = it.name
                elif r[0] == "rel":
                    assert held.get(r[1]) == it.name, (r, held)
                    del held[r[1]]
                elif r[0] == "mark":
                    done.add(r[1])
        assert progressed or pending_items, "pipeline deadlock"
        rnd += 1


def build_program(n_tasks=NT, stop_at=None):
    nc = bass.Bass("TRN2", target_bir_lowering=False)

    def ckpt(name):
        if stop_at is not None and name == stop_at:
            raise _Stop()

    def din(name, shape):
        return nc.dram_tensor(name, shape, F32, kind="ExternalInput").ap()

    xe = din("xe", [NT, TE, D])
    y = nc.dram_tensor("y", [NT, T, D], F32, kind="ExternalOutput").ap()
    w_in = din("w_in", [7, 128, 8192])
    w_out = din("w_out", [4, 128, 8192])
    w_up = din("w_up", [16, 128, 8192])
    w_down = din("w_down", [16, 128, 8192])
    msk_d = din("msk", [NT, 128, 3 * 384])
    cs_d = din("cs", [NT, 128, 96])
    sn_d = din("sn", [NT, 128, 96])
    gmixT_d = din("gmixT", [128, 16])
    gffnT_d = din("gffnT", [128, 16])
    goutT_d = din("goutT", [128, 16])
    gq_d = din("gq", [128, 128])
    gk_d = din("gk", [128, 128])
    gln_d = din("gln", [128, 1024])
    bln_d = din("bln", [128, 1024])
    sink_d = din("sinkb", [128, 8])
    wspT_d = din("wspT", [128, 1024])
    bsp_d = din("bsp", [128, 8])

    with contextlib.ExitStack() as st:
        P = Prog(nc, st)

        def sb(name, shape, dt):
            return st.enter_context(nc.sbuf_tensor(name, shape, dt))

        xres = sb("xres", [128, 4, D], F32)
        xst = sb("xst", [128, D], F32)
        xb0 = sb("xb", [128, D], BF16)
        big = sb("big", [128, 32768], BF16)
        mh = sb("mh", [128, 16, T], BF16)
        wsl = [sb(f"wsl{i}", [128, 16, 512], BF16) for i in range(NSLOT)]
        yst = sb("yst", [128, 4, 512], F32)
        tmpf0 = sb("tmpf", [128, 1024], F32)
        ident = sb("ident", [128, 128], BF16)
        identf = sb("identf", [128, 128], F32)
        gmixT = sb("gmixT_s", [128, 16], F32)
        gffnT = sb("gffnT_s", [128, 16], F32)
        goutT = sb("goutT_s", [128, 16], F32)
        gq = sb("gq_s", [128, 128], F32)
        gk = sb("gk_s", [128, 128], F32)
        gln = sb("gln_s", [128, 1024], F32)
        bln = sb("bln_s", [128, 1024], F32)
        sinkb = sb("sink_s", [128, 8], F32)
        nsink = sb("nsink_s", [128, 8], F32)
        wspT = sb("wspT_s", [128, 8, 128], BF16)
        bsp = sb("bsp_s", [128, 8], F32)
        msk = sb("msk_s", [128, 3, 384], F32)
        cs = sb("cs_s", [128, 6, 16], F32)
        sn = sb("sn_s", [128, 6, 16], F32)
        stats = [sb(f"stat{i}", [128, 32], F32) for i in range(9)]
        ropes = [sb(f"rope{i}", [128, 8, 2, 16], F32) for i in range(4)]
        bnsts = [sb(f"bnst{i}", [128, 2, 6], F32) for i in range(4)]
        rsA = sb("rsA", [128, 8], F32)
        rsA2 = sb("rsA2", [128, 8], F32)
        rsF = sb("rsF", [128, 4], F32)

        hT = big[:, 0:12288].rearrange("p (c t) -> p c t", t=TE)
        qT = big[:, 12288:16384].rearrange("p (h t) -> p h t", t=T)
        kT = big[:, 16384:17920].rearrange("p (h t) -> p h t", t=TE)
        vv = big[:, 17920:19456].rearrange("p (b n) -> p b n", n=256)
        vgn = big[:, 19456:23552].rearrange("p (b n) -> p b n", n=1024)
        uu = big[:, 23552:31744].bitcast(F32).rearrange("p (b n) -> p b n", n=1024)
        actT = big[:, :].rearrange("p (f t) -> p f t", t=T)
        smS, pbS, pTsS = [], [], []
        for i in range(2):
            o = i * 6144
            smS.append(big[:, o:o + 3072].bitcast(F32).rearrange("p (h k) -> p h k", k=384))
            pbS.append(big[:, o + 3072:o + 4608].rearrange("p (h k) -> p h k", k=384))
            pTsS.append(big[:, o + 4608:o + 6144].rearrange("p (j q) -> p j q", q=128))
        fxb = [xb0[:], big[:, 0:2048], big[:, 2048:4096]]
        mhf = mh[:].rearrange("p c t -> p (c t)")
        ystf = yst[:].rearrange("p a b -> p (a b)")
        xbS = [xb0[:], mhf[:, 0:2048], mhf[:, 2048:4096], ystf[:, 0:1024].bitcast(BF16)]
        tmpfS = [tmpf0[:], mhf[:, 4096:6144].bitcast(F32), mhf[:, 6144:8192].bitcast(F32), ystf[:, 1024:2048]]
        mixtok = [xst[:], ystf]
        xbD = [xb0[:], tmpf0[:].bitcast(BF16)]

        pmm = st.enter_context(nc.psum_tensor("pmm", [128, 8, 512], F32))
        TB = [pmm[:, 4:6, :].bitcast(BF16), pmm[:, 6:8, :].bitcast(BF16)]
        TB16 = [tb.rearrange("p a (c t) -> p (a c) t", t=128) for tb in TB]

        R_xres = [Res(f"xres{b}") for b in range(4)]
        R_xst = Res("xst")
        R_yst = Res("yst")
        R_mix = [R_xst, R_yst]
        R_xbS = [Res(f"xbS{i}") for i in range(4)]
        R_tmpfS = [Res(f"tmpfS{i}") for i in range(4)]
        R_fxb = [R_xbS[0], Res("fxb1"), Res("fxb2")]
        R_stat = [Res(f"stat{i}") for i in range(9)]
        R_rope = [Res(f"rope{i}") for i in range(4)]
        R_hT = [Res(f"hT{b}") for b in range(6)]
        R_qT = [Res(f"qT{b}") for b in range(4)]
        R_kT = [Res(f"kT{b}") for b in range(6)]
        R_v = [Res(f"v{b}") for b in range(6)]
        R_vgn = [Res(f"vgn{b}") for b in range(4)]
        R_u = [Res(f"u{b}") for b in range(4)]
        R_sm = [Res("sm0"), Res("sm1")]
        R_pb = [Res("pb0"), Res("pb1")]
        R_pTs = [Res("pTs0"), Res("pTs1")]
        R_actT = [Res(f"actT{g}") for g in range(16)]
        R_mh = [Res(f"mh{b}") for b in range(4)]
        R_wsl = [Res(f"wsl{i}") for i in range(NSLOT)]
        R_rl = [Res("rl0"), Res("rl1")]
        R_const = Res("const")
        R_task = Res("taskconst")
        R_bns = [Res(f"bn{i}") for i in range(4)]
        R_rsA = [Res(f"rsA{i}") for i in range(6)]
        R_rsF = [Res(f"rsF{i}") for i in range(4)]
        R_junkF = Res("junkF")
        R_bank = [Res(f"bank{i}") for i in range(8)]
        R_TB = [[R_bank[4], R_bank[5]], [R_bank[6], R_bank[7]]]
        arena_phase1 = R_hT + R_qT + R_kT + R_v + R_vgn + R_u
        dscratch_core = R_sm + R_pb + R_pTs
        fxb_res = [R_fxb[1], R_fxb[2], R_junkF]
        all_phase1 = arena_phase1 + dscratch_core + fxb_res
        mh_scratch = [R_xbS[1], R_xbS[2], R_tmpfS[1], R_tmpfS[2]]
        yst_scratch = [R_xbS[3], R_tmpfS[3]]

        d_const = DSem(P, "d_const")
        d_task = DSem(P, "d_task")
        d_xst = DSem(P, "d_xst")
        d_xres = [DSem(P, f"d_xres{b}") for b in range(4)]
        d_w = [DSem(P, f"d_w{i}") for i in range(NSLOT)]
        d_y = DSem(P, "d_y")
        d_yh = DSem(P, "d_yh")
        d_xc = [DSem(P, f"d_xc{i}") for i in range(4)]

        def xdst(eb):
            if eb in (0, 5):
                return xst[:], R_xst, d_xst
            return xres[:, eb - 1, :], R_xres[eb - 1], d_xres[eb - 1]

        NSET = 4
        SSTAT = [0, 1, 2, 8]

        sync, act, dve, pool, pe = nc.sync, nc.scalar, nc.vector, nc.gpsimd, nc.tensor

        for dst, src in ((gmixT, gmixT_d), (gffnT, gffnT_d), (goutT, goutT_d), (gq, gq_d),
                         (gk, gk_d), (gln, gln_d), (bln, bln_d), (sinkb, sink_d), (bsp, bsp_d)):
            P.dma("sp", d_const, lambda dst=dst, src=src: sync.dma_start(out=dst[:], in_=src[:, :]),
                  writes=[Res("c")])
        R_const.w = (d_const.key, d_const.count)
        d_wsp = DSem(P, "d_wsp")
        R_wsp = Res("wsp")
        P.dma("pool", d_wsp, lambda: pool.dma_start(out=wspT[:].rearrange("p h i -> p (h i)"), in_=wspT_d[:, :]),
              writes=[R_wsp])
        R_ident = Res("ident")
        P.op("pool", lambda: pool.memset(identf[:], 0.0), writes=[R_ident])
        P.op("pool", lambda: pool.affine_select(out=identf[:], in_=identf[:], pattern=[[-1, 128]],
                                                compare_op=ALU.not_equal, fill=1.0, base=0,
                                                channel_multiplier=1),
             reads=[R_ident], writes=[R_ident])
        P.op("dve", lambda: dve.tensor_copy(out=ident[:], in_=identf[:]), reads=[R_ident], writes=[R_ident])
        R_nsink = Res("nsink")
        P.op("dve", lambda: dve.tensor_scalar(out=nsink[:], in0=sinkb[:], scalar1=-1.0, scalar2=None,
                                              op0=ALU.mult), reads=[R_const], writes=[R_nsink])

        slabs = []
        for cg in (2, 0, 1, 3, 4, 5, 6):
            slabs.append(w_in[cg])
        for cg in range(4):
            slabs.append(w_out[cg])
        for fg in range(16):
            slabs.append(w_up[fg])
        for cg in range(4):
            for sub in range(4):
                slabs.append(w_down[cg * 4 + sub])
        n_per_task = len(slabs)
        total_slabs = n_per_task * n_tasks
        wstate = {"issued": 0}

        def ensure_loaded(upto):
            upto = min(upto, total_slabs - 1)
            while wstate["issued"] <= upto:
                n = wstate["issued"]
                s = n % NSLOT
                src = slabs[n % n_per_task]
                P.dma("pool", d_w[s], lambda s=s, src=src: pool.dma_start(
                    out=wsl[s][:].rearrange("p c n -> p (c n)"), in_=src), writes=[R_wsl[s]])
                wstate["issued"] += 1

        def g_load(upto):
            ensure_loaded(upto)
            yield

        def rstd_chain(stt, rs, col, n, width=1):
            a = stt[:, col:col + width]
            P.op("dve", lambda: dve.tensor_scalar(out=a, in0=a, scalar1=1.0 / n, scalar2=EPS,
                                                  op0=ALU.mult, op1=ALU.add), reads=[rs], writes=[rs])
            yield
            P.op("act", lambda: act.activation(out=a, in_=a, func=AF.Ln), reads=[rs], writes=[rs])
            yield
            P.op("act", lambda: act.activation(out=a, in_=a, func=AF.Exp, scale=-0.5), reads=[rs], writes=[rs])
            yield

        def g_rms_to_T(src_ap, src_res, gT, dst_ap, dst_res, xbt, r_xb, junk, r_junk, stt, rs, tbname,
                       mode, col, out_res, cast_dma=None):
            if cast_dma == "done":
                pass
            elif cast_dma is not None:
                dsem, src_d = cast_dma
                P.dma("pool", dsem, lambda: pool.dma_start(out=xbt, in_=src_d), writes=[r_xb])
            else:
                P.op("act", lambda: act.activation(out=xbt, in_=src_ap, func=AF.Copy),
                     reads=[src_res], writes=[r_xb])
            yield
            P.op("act", lambda: act.activation(out=junk, in_=src_ap, func=AF.Square, accum_out=stt[:, 0:1]),
                 reads=[src_res], writes=[r_junk, rs])
            yield ("mark_src_read",)
            ti = int(tbname[-1])
            yield ("acq", f"bank{4 + 2 * ti}")
            yield ("acq", f"bank{5 + 2 * ti}")
            P.group("pe", [(lambda c=c: pe.transpose(out=TB16[ti][:, c, :], in_=xbt[:, c * 128:(c + 1) * 128],
                                                     identity=ident[:])) for c in range(16)],
                    reads=[r_xb, R_ident], writes=R_TB[ti])
            yield
            a = stt[:, 0:1]
            P.op("dve", lambda: dve.tensor_tensor(out=dst_ap, in0=TB16[ti],
                                                  in1=gT[:].unsqueeze(2).broadcast_to([128, 16, 128]),
                                                  op=ALU.mult),
                 reads=R_TB[ti] + [R_const], writes=[dst_res])
            yield ("rel", f"bank{4 + 2 * ti}")
            yield ("rel", f"bank{5 + 2 * ti}")
            P.op("dve", lambda: dve.tensor_scalar(out=a, in0=a, scalar1=1.0 / D, scalar2=EPS,
                                                  op0=ALU.mult, op1=ALU.add), reads=[rs], writes=[rs])
            yield
            if mode == "A":
                P.op("act", lambda: act.activation(out=a, in_=a, func=AF.Ln), reads=[rs], writes=[rs])
                yield
                P.op("act", lambda: act.activation(out=rsA[:, col:col + 1], in_=a, func=AF.Exp, scale=-0.5),
                     reads=[rs], writes=[out_res])
                yield
                P.op("dve", lambda: dve.tensor_tensor(out=rsA2[:, col:col + 1], in0=rsA[:, col:col + 1],
                                                      in1=rsA[:, col:col + 1], op=ALU.mult),
                     reads=[out_res], writes=[out_res])
                yield
            else:
                P.op("dve", lambda: dve.reciprocal(out=rsF[:, col:col + 1], in_=a), reads=[rs], writes=[out_res])
                yield

        def g_qk_chain(src3, src_res, H, g, eb, dstb, r_dst, tf, r_tf, stt, rs, rp, r_rp):
            tv = tf[:, 0:H * 128]
            tv3 = tv.rearrange("p (h d) -> p h d", d=128)
            P.op("act", lambda: act.activation(out=tv3, in_=src3, func=AF.Square), reads=[src_res], writes=[r_tf])
            yield
            P.op("dve", lambda: dve.tensor_reduce(out=stt[:, 8:8 + H], in_=tv3, axis=AX.X, op=ALU.add),
                 reads=[r_tf], writes=[rs])
            yield
            P.op("dve", lambda: dve.tensor_scalar(out=stt[:, 8:8 + H], in0=stt[:, 8:8 + H],
                                                  scalar1=rsA2[:, eb:eb + 1], scalar2=None, op0=ALU.mult),
                 reads=[rs, R_rsA[eb]], writes=[rs])
            yield
            yield from rstd_chain(stt, rs, 8, 128, H)
            P.op("dve", lambda: dve.tensor_scalar(out=stt[:, 8:8 + H], in0=stt[:, 8:8 + H],
                                                  scalar1=rsA[:, eb:eb + 1], scalar2=None, op0=ALU.mult),
                 reads=[rs, R_rsA[eb]], writes=[rs])
            yield
            P.op("dve", lambda: dve.tensor_tensor(out=tv3, in0=src3,
                                                  in1=stt[:, 8:8 + H].unsqueeze(2).broadcast_to([128, H, 128]),
                                                  op=ALU.mult), reads=[src_res, rs], writes=[r_tf])
            yield ("rel", "srcbank")
            P.op("dve", lambda: dve.tensor_tensor(out=tv3, in0=tv3,
                                                  in1=g[:].unsqueeze(1).broadcast_to([128, H, 128]),
                                                  op=ALU.mult), reads=[r_tf, R_const], writes=[r_tf])
            yield
            cosb = cs[:, eb, :].unsqueeze(1).broadcast_to([128, H, 16])
            sinb = sn[:, eb, :].unsqueeze(1).broadcast_to([128, H, 16])
            x1 = tv3[:, :, 0:16]
            x2 = tv3[:, :, 16:32]
            dst3 = dstb.rearrange("p (h d) -> p h d", d=128)
            r3 = rp[:, 0:H, :, :]
            P.op("act", lambda: act.activation(out=dstb, in_=tv, func=AF.Copy), reads=[r_tf], writes=[r_dst])
            yield
            for (a1, b1, a2, b2, opx, lo) in ((x1, cosb, x2, sinb, ALU.subtract, 0),
                                               (x2, cosb, x1, sinb, ALU.add, 16)):
                P.op("dve", lambda a1=a1, b1=b1: dve.tensor_tensor(out=r3[:, :, 0, :], in0=a1, in1=b1, op=ALU.mult),
                     reads=[r_tf, R_task], writes=[r_rp])
                yield
                P.op("dve", lambda a2=a2, b2=b2: dve.tensor_tensor(out=r3[:, :, 1, :], in0=a2, in1=b2, op=ALU.mult),
                     reads=[r_tf, R_task, r_rp], writes=[r_rp])
                yield
                P.op("dve", lambda opx=opx, lo=lo: dve.tensor_tensor(out=dst3[:, :, lo:lo + 16], in0=r3[:, :, 0, :],
                                                                     in1=r3[:, :, 1, :], op=opx),
                     reads=[r_rp], writes=[r_dst])
                yield

        def g_A_kv(t, eb, si_, slot):
            sset = eb % NSET
            xbt, r_xb = xbS[sset], R_xbS[sset]
            tf, r_tf = tmpfS[sset], R_tmpfS[sset]
            stt, rs = stats[SSTAT[sset]], R_stat[SSTAT[sset]]
            rp, r_rp = ropes[sset], R_rope[sset]
            yield ("acq", f"set{sset}")
            dst_ap, dres, dsem = xdst(eb)
            if eb in (0, 5):
                yield ("acq", "xst")
            if eb == 5:
                P.dma("sp", dsem, lambda: sync.dma_start(out=dst_ap, in_=xe[t, eb * 128:(eb + 1) * 128, :]),
                      writes=[dres])
                yield
            tbname = f"tb{eb % 2}"
            for r in g_rms_to_T(dst_ap, dres, gmixT, hT[:, :, eb * 128:(eb + 1) * 128], R_hT[eb],
                                xbt, r_xb, tf.bitcast(BF16), r_tf, stt, rs, tbname, "A", eb, R_rsA[eb],
                                cast_dma=("done" if (t, eb) in prefetched else
                                          (d_xc[sset], xe[t, eb * 128:(eb + 1) * 128, :]))):
                if r == ("mark_src_read",):
                    if eb in (0, 5):
                        yield ("rel", "xst")
                    else:
                        yield
                else:
                    yield r
            bk = eb % 4
            yield ("acq", f"bank{bk}")
            P.group("pe", [(lambda c=c: pe.matmul(out=pmm[:, bk, :], lhsT=hT[:, c, eb * 128:(eb + 1) * 128],
                                                  rhs=wsl[slot][:, c, :], start=(c == 0), stop=(c == 15)))
                           for c in range(16)],
                    reads=[R_hT[eb], R_wsl[slot]], writes=[R_bank[bk]])
            yield
            P.op("act", lambda: act.activation(out=vv[:, eb, :], in_=pmm[:, bk, 256:512], func=AF.Copy,
                                               scale=rsA[:, eb:eb + 1]),
                 reads=[R_bank[bk], R_rsA[eb]], writes=[R_v[eb]])
            yield
            src3 = pmm[:, bk, 0:256].rearrange("p (h d) -> p h d", d=128)
            for r in g_qk_chain(src3, R_bank[bk], 2, gk, eb, xbt[:, 0:256], r_xb, tf, r_tf, stt, rs, rp, r_rp):
                if r == ("rel", "srcbank"):
                    yield ("rel", f"bank{bk}")
                else:
                    yield r
            ti = eb % 2
            yield ("acq", f"bank{4 + 2 * ti}")
            P.group("pe", [(lambda h=h: pe.transpose(out=TB[ti][:, 0, h * 128:(h + 1) * 128],
                                                     in_=xbt[:, h * 128:(h + 1) * 128], identity=ident[:]))
                           for h in range(2)],
                    reads=[r_xb, R_ident], writes=[R_TB[ti][0]])
            yield
            P.op("dve", lambda: dve.tensor_copy(out=kT[:, :, eb * 128:(eb + 1) * 128],
                                                in_=TB[ti][:, 0, 0:256].rearrange("p (h t) -> p h t", t=128)),
                 reads=[R_TB[ti][0]], writes=[R_kT[eb]])
            yield ("rel", f"bank{4 + 2 * ti}")
            yield ("rel", f"set{sset}")

        def g_q(t, b, slot0, slot1, idx):
            eb = b + 1
            sset = (idx + 2) % NSET
            xbt, r_xb = xbS[sset], R_xbS[sset]
            tf, r_tf = tmpfS[sset], R_tmpfS[sset]
            stt, rs = stats[SSTAT[sset]], R_stat[SSTAT[sset]]
            rp, r_rp = ropes[sset], R_rope[sset]
            yield ("acq", f"set{sset}")
            pr = (idx % 2) * 2
            yield ("acq", f"bank{pr}")
            yield ("acq", f"bank{pr + 1}")
            for j, slot in ((0, slot0), (1, slot1)):
                P.group("pe", [(lambda c=c, j=j, slot=slot: pe.matmul(
                    out=pmm[:, pr + j, :], lhsT=hT[:, c, eb * 128:(eb + 1) * 128], rhs=wsl[slot][:, c, :],
                    start=(c == 0), stop=(c == 15))) for c in range(16)],
                    reads=[R_hT[eb], R_wsl[slot]], writes=[R_bank[pr + j]])
                yield
            src3 = pmm[:, pr:pr + 2, :].rearrange("p a (h d) -> p (a h) d", d=128)

            for r in g_qk_chain2(src3, [R_bank[pr], R_bank[pr + 1]], 8, gq, eb, xbt[:, 0:1024], r_xb,
                                 tf, r_tf, stt, rs, rp, r_rp):
                if r == ("rel", "srcbank"):
                    yield ("rel", f"bank{pr}")
                    yield ("rel", f"bank{pr + 1}")
                else:
                    yield r
            ti = idx % 2
            yield ("acq", f"bank{4 + 2 * ti}")
            P.group("pe", [(lambda h=h: pe.transpose(out=TB[ti][:, 0, h * 128:(h + 1) * 128],
                                                     in_=xbt[:, h * 128:(h + 1) * 128], identity=ident[:]))
                           for h in range(8)],
                    reads=[r_xb, R_ident], writes=[R_TB[ti][0]])
            yield
            P.op("dve", lambda: dve.tensor_copy(out=qT[:, :, b * 128:(b + 1) * 128],
                                                in_=TB[ti][:, 0, :].rearrange("p (h t) -> p h t", t=128)),
                 reads=[R_TB[ti][0]], writes=[R_qT[b]])
            yield ("rel", f"bank{4 + 2 * ti}")
            yield ("rel", f"set{sset}")

        def g_qk_chain2(src3, src_rl, H, g, eb, dstb, r_dst, tf, r_tf, stt, rs, rp, r_rp):
            tv = tf[:, 0:H * 128]
            tv3 = tv.rearrange("p (h d) -> p h d", d=128)
            P.op("act", lambda: act.activation(out=tv3, in_=src3, func=AF.Square), reads=src_rl, writes=[r_tf])
            yield
            P.op("dve", lambda: dve.tensor_reduce(out=stt[:, 8:8 + H], in_=tv3, axis=AX.X, op=ALU.add),
                 reads=[r_tf], writes=[rs])
            yield
            P.op("dve", lambda: dve.tensor_scalar(out=stt[:, 8:8 + H], in0=stt[:, 8:8 + H],
                                                  scalar1=rsA2[:, eb:eb + 1], scalar2=None, op0=ALU.mult),
                 reads=[rs, R_rsA[eb]], writes=[rs])
            yield
            yield from rstd_chain(stt, rs, 8, 128, H)
            P.op("dve", lambda: dve.tensor_scalar(out=stt[:, 8:8 + H], in0=stt[:, 8:8 + H],
                                                  scalar1=rsA[:, eb:eb + 1], scalar2=None, op0=ALU.mult),
                 reads=[rs, R_rsA[eb]], writes=[rs])
            yield
            P.op("dve", lambda: dve.tensor_tensor(out=tv3, in0=src3,
                                                  in1=stt[:, 8:8 + H].unsqueeze(2).broadcast_to([128, H, 128]),
                                                  op=ALU.mult), reads=src_rl + [rs], writes=[r_tf])
            yield ("rel", "srcbank")
            P.op("dve", lambda: dve.tensor_tensor(out=tv3, in0=tv3,
                                                  in1=g[:].unsqueeze(1).broadcast_to([128, H, 128]),
                                                  op=ALU.mult), reads=[r_tf, R_const], writes=[r_tf])
            yield
            cosb = cs[:, eb, :].unsqueeze(1).broadcast_to([128, H, 16])
            sinb = sn[:, eb, :].unsqueeze(1).broadcast_to([128, H, 16])
            x1 = tv3[:, :, 0:16]
            x2 = tv3[:, :, 16:32]
            dst3 = dstb.rearrange("p (h d) -> p h d", d=128)
            r3 = rp[:, 0:H, :, :]
            P.op("act", lambda: act.activation(out=dstb, in_=tv, func=AF.Copy), reads=[r_tf], writes=[r_dst])
            yield
            for (a1, b1, a2, b2, opx, lo) in ((x1, cosb, x2, sinb, ALU.subtract, 0),
                                               (x2, cosb, x1, sinb, ALU.add, 16)):
                P.op("dve", lambda a1=a1, b1=b1: dve.tensor_tensor(out=r3[:, :, 0, :], in0=a1, in1=b1, op=ALU.mult),
                     reads=[r_tf, R_task], writes=[r_rp])
                yield
                P.op("dve", lambda a2=a2, b2=b2: dve.tensor_tensor(out=r3[:, :, 1, :], in0=a2, in1=b2, op=ALU.mult),
                     reads=[r_tf, R_task, r_rp], writes=[r_rp])
                yield
                P.op("dve", lambda opx=opx, lo=lo: dve.tensor_tensor(out=dst3[:, :, lo:lo + 16], in0=r3[:, :, 0, :],
                                                                     in1=r3[:, :, 1, :], op=opx),
                     reads=[r_rp], writes=[r_dst])
                yield

        def g_u(b, j, slot, i):
            eb = b + 1
            bk5 = i % 4
            yield ("acq", f"bank{bk5}")
            P.group("pe", [(lambda c=c: pe.matmul(out=pmm[:, bk5, :], lhsT=hT[:, c, eb * 128:(eb + 1) * 128],
                                                  rhs=wsl[slot][:, c, :], start=(c == 0), stop=(c == 15)))
                           for c in range(16)],
                    reads=[R_hT[eb], R_wsl[slot]], writes=[R_bank[bk5]])
            yield
            P.op("act", lambda: act.activation(out=uu[:, b, j * 512:(j + 1) * 512], in_=pmm[:, bk5, :],
                                               func=AF.Gelu_apprx_tanh, scale=rsA[:, eb:eb + 1]),
                 reads=[R_bank[bk5], R_rsA[eb]], writes=[R_u[b]])
            yield ("rel", f"bank{bk5}")

        def g_vg(b, slot0, slot1, i):
            eb = b + 1
            sset = i % NSET
            stt, rs = stats[SSTAT[sset]], R_stat[SSTAT[sset]]
            tf, r_tf = tmpfS[sset], R_tmpfS[sset]
            bnst, R_bn = bnsts[sset], R_bns[sset]
            yield ("acq", f"set{sset}")
            for j, slot in ((0, slot0), (1, slot1)):
                bk5 = (2 * i + j) % 4
                yield ("acq", f"bank{bk5}")
                P.group("pe", [(lambda c=c, slot=slot, bk5=bk5: pe.matmul(
                    out=pmm[:, bk5, :], lhsT=hT[:, c, eb * 128:(eb + 1) * 128], rhs=wsl[slot][:, c, :],
                    start=(c == 0), stop=(c == 15))) for c in range(16)],
                    reads=[R_hT[eb], R_wsl[slot]], writes=[R_bank[bk5]])
                yield
                P.op("act", lambda j=j, bk5=bk5: act.activation(out=tf[:, j * 512:(j + 1) * 512], in_=pmm[:, bk5, :],
                                                       func=AF.Gelu_apprx_tanh, scale=rsA[:, eb:eb + 1]),
                     reads=[R_bank[bk5], R_rsA[eb]], writes=[r_tf])
                yield ("rel", f"bank{bk5}")
                P.op("dve", lambda j=j: dve.bn_stats(out=bnst[:, j, :], in_=tf[:, j * 512:(j + 1) * 512]),
                     reads=[r_tf], writes=[R_bn])
                yield
            P.op("dve", lambda: dve.bn_aggr(out=stt[:, 16:18], in_=bnst[:].rearrange("p a b -> p (a b)")),
                 reads=[R_bn], writes=[rs])
            yield
            P.op("dve", lambda: dve.tensor_scalar(out=stt[:, 17:18], in0=stt[:, 17:18], scalar1=EPS, scalar2=None,
                                                  op0=ALU.add), reads=[rs], writes=[rs])
            yield
            P.op("act", lambda: act.activation(out=stt[:, 17:18], in_=stt[:, 17:18], func=AF.Ln),
                 reads=[rs], writes=[rs])
            yield
            P.op("act", lambda: act.activation(out=stt[:, 17:18], in_=stt[:, 17:18], func=AF.Exp, scale=-0.5),
                 reads=[rs], writes=[rs])
            yield
            P.op("dve", lambda: dve.tensor_scalar(out=tf[:], in0=tf[:], scalar1=stt[:, 16:17],
                                                  scalar2=stt[:, 17:18], op0=ALU.subtract, op1=ALU.mult),
                 reads=[r_tf, rs], writes=[r_tf])
            yield
            P.op("dve", lambda: dve.tensor_tensor(out=tf[:], in0=tf[:], in1=gln[:], op=ALU.mult),
                 reads=[r_tf, R_const], writes=[r_tf])
            yield
            P.op("dve", lambda: dve.tensor_tensor(out=vgn[:, b, :], in0=tf[:], in1=bln[:], op=ALU.add),
                 reads=[r_tf, R_const], writes=[R_vgn[b]])
            yield ("rel", f"set{sset}")

        def g_att(b, kvg, unit):
            eb = b + 1
            ds = unit % 2
            sm_, pb_, pT_ = smS[ds], pbS[ds], pTsS[ds]
            sti = 3 + unit % 4
            stt, rs = stats[sti], R_stat[sti]
            kind = 0 if b == 0 else (2 if b == 3 else 1)
            mt, r_mt = mixtok[b % 2], R_mix[b % 2]
            yield ("acq", f"st{sti}")
            yield ("acq", f"sm{ds}")
            for k_ in range(4):
                yield ("acq", f"bank{k_}")
            P.group("pe", [(lambda hh=hh: pe.matmul(
                out=pmm[:, hh, 0:384], lhsT=qT[:, 4 * kvg + hh, b * 128:(b + 1) * 128],
                rhs=kT[:, kvg, (eb - 1) * 128:(eb + 2) * 128], start=True, stop=True)) for hh in range(4)],
                reads=[R_qT[b], R_kT[eb - 1], R_kT[eb], R_kT[eb + 1]], writes=R_bank[0:4])
            yield
            P.op("dve", lambda: dve.scalar_tensor_tensor(
                out=sm_, in0=pmm[:, 0:4, 0:384], scalar=SCALE,
                in1=msk[:, kind, :].unsqueeze(1).broadcast_to([128, 4, 384]), op0=ALU.mult, op1=ALU.add),
                reads=R_bank[0:4] + [R_task], writes=[R_sm[ds]])
            for k_ in range(4):
                yield ("rel", f"bank{k_}")
            P.op("dve", lambda: dve.tensor_reduce(out=stt[:, 0:4], in_=sm_, axis=AX.X, op=ALU.max),
                 reads=[R_sm[ds]], writes=[rs])
            yield
            P.op("dve", lambda: dve.scalar_tensor_tensor(out=stt[:, 4:8], in0=stt[:, 0:4], scalar=-1.0,
                                                         in1=nsink[:, 4 * kvg:4 * kvg + 4], op0=ALU.mult,
                                                         op1=ALU.min),
                 reads=[rs, R_nsink], writes=[rs])
            yield
            yield ("acq", f"pb{ds}")
            for hh in range(4):
                P.op("act", lambda hh=hh: act.activation(out=pb_[:, hh, :], in_=sm_[:, hh, :], func=AF.Exp,
                                                         bias=stt[:, 4 + hh:5 + hh], scale=1.0,
                                                         accum_out=stt[:, 8 + hh:9 + hh]),
                     reads=[R_sm[ds], rs], writes=[R_pb[ds], rs])
                yield
            yield ("rel", f"sm{ds}")
            P.op("dve", lambda: dve.tensor_tensor(out=stt[:, 12:16], in0=sinkb[:, 4 * kvg:4 * kvg + 4],
                                                  in1=stt[:, 4:8], op=ALU.add), reads=[rs, R_const], writes=[rs])
            yield
            P.op("act", lambda: act.activation(out=stt[:, 12:16], in_=stt[:, 12:16], func=AF.Exp),
                 reads=[rs], writes=[rs])
            yield
            P.op("dve", lambda: dve.tensor_tensor(out=stt[:, 16:20], in0=stt[:, 8:12], in1=stt[:, 12:16],
                                                  op=ALU.add), reads=[rs], writes=[rs])
            yield
            P.op("dve", lambda: dve.reciprocal(out=stt[:, 20:24], in_=stt[:, 16:20]), reads=[rs], writes=[rs])
            yield
            yield ("acq", "bank6")
            yield ("acq", "bank7")
            tbf = TB16[1]
            P.group("pe", [(lambda hh=hh, kb=kb: pe.transpose(
                out=tbf[:, hh * 3 + kb, :], in_=pb_[:, hh, kb * 128:(kb + 1) * 128], identity=ident[:]))
                for hh in range(4) for kb in range(3)],
                reads=[R_pb[ds], R_ident], writes=R_TB[1])
            yield ("rel", f"pb{ds}")
            yield ("acq", f"pT{ds}")
            P.op("act", lambda: act.activation(out=pT_, in_=tbf[:, 0:12, :], func=AF.Copy),
                 reads=R_TB[1], writes=[R_pTs[ds]])
            yield ("rel", "bank6")
            yield ("rel", "bank7")
            yield ("acq", "bank4")
            P.group("pe", [(lambda hh=hh, kb=kb: pe.matmul(
                out=pmm[:, 4, hh * 128:(hh + 1) * 128], lhsT=pT_[:, hh * 3 + kb, :],
                rhs=vv[:, eb - 1 + kb, kvg * 128:(kvg + 1) * 128], start=(kb == 0), stop=(kb == 2)))
                for hh in range(4) for kb in range(3)],
                reads=[R_pTs[ds], R_v[eb - 1], R_v[eb], R_v[eb + 1]], writes=[R_bank[4]])
            yield ("rel", f"pT{ds}")
            P.op("dve", lambda: dve.tensor_tensor(
                out=mt[:, kvg * 512:(kvg + 1) * 512].rearrange("p (h d) -> p h d", d=128),
                in0=pmm[:, 4, :].rearrange("p (h d) -> p h d", d=128),
                in1=stt[:, 20:24].unsqueeze(2).broadcast_to([128, 4, 128]), op=ALU.mult),
                reads=[R_bank[4], rs], writes=[r_mt])
            yield ("rel", "bank4")
            yield ("rel", f"st{sti}")

        def g_gmn(b):
            mt, r_mt = mixtok[b % 2], R_mix[b % 2]
            stt, rs = stats[7], R_stat[7]
            xbt, r_xb = xbD[b % 2], (R_xbS[0] if b % 2 == 0 else R_tmpfS[0])
            yield ("acq", "bank0")
            yield ("acq", "bank1")
            P.group("pe", [(lambda h=h: pe.matmul(
                out=pmm[:, h // 4, (h % 4) * 128:(h % 4 + 1) * 128], lhsT=wspT[:, h, :],
                rhs=vgn[:, b, h * 128:(h + 1) * 128], start=True, stop=True)) for h in range(8)],
                reads=[R_wsp, R_vgn[b]], writes=[R_bank[0], R_bank[1]])
            yield
            for h in range(8):
                P.op("dve", lambda h=h: dve.scalar_tensor_tensor(
                    out=mt[:, 1024 + h * 128:1024 + (h + 1) * 128],
                    in0=pmm[:, h // 4, (h % 4) * 128:(h % 4 + 1) * 128], scalar=bsp[:, h:h + 1],
                    in1=uu[:, b, h * 128:(h + 1) * 128], op0=ALU.add, op1=ALU.mult),
                    reads=[R_bank[h // 4], R_const, R_u[b]], writes=[r_mt])
                yield
            yield ("rel", "bank0")
            yield ("rel", "bank1")
            yield ("acq", f"xbD{b % 2}")
            yield ("acq", "st7")
            for half in range(2):
                src = mt[:, half * 1024:(half + 1) * 1024]
                dstb = xbt[:, half * 1024:(half + 1) * 1024]
                P.op("act", lambda src=src, dstb=dstb: act.activation(out=dstb, in_=src, func=AF.Square,
                                                                      accum_out=stt[:, half:half + 1]),
                     reads=[r_mt], writes=[r_xb, rs])
                yield
            yield from rstd_chain(stt, rs, 0, 1024, 2)
            for half in range(2):
                src = mt[:, half * 1024:(half + 1) * 1024]
                dstb = xbt[:, half * 1024:(half + 1) * 1024]
                P.op("act", lambda src=src, dstb=dstb, half=half: act.activation(
                    out=dstb, in_=src, func=AF.Copy, scale=stt[:, half:half + 1]),
                    reads=[r_mt, rs], writes=[r_xb])
                yield
            yield ("rel", "st7")
            yield ("mark", f"m{b}")
            yield ("acq", "bank4")
            yield ("acq", "bank5")
            P.group("pe", [(lambda c=c: pe.transpose(out=TB16[0][:, c, :], in_=xbt[:, c * 128:(c + 1) * 128],
                                                     identity=ident[:])) for c in range(16)],
                    reads=[r_xb, R_ident], writes=R_TB[0])
            yield ("rel", f"xbD{b % 2}")
            P.op("dve", lambda: dve.tensor_tensor(out=mh[:, :, b * 128:(b + 1) * 128], in0=TB16[0],
                                                  in1=goutT[:].unsqueeze(2).broadcast_to([128, 16, 128]),
                                                  op=ALU.mult),
                 reads=R_TB[0] + [R_const], writes=[R_mh[b]])
            yield ("rel", "bank4")
            yield ("rel", "bank5")

        def g_E(b, cgs, base_):
            for cg in cgs:
                slot = (base_ + 7 + cg) % NSLOT
                yield ("acq", "bank5")
                P.group("pe", [(lambda c=c, slot=slot: pe.matmul(
                    out=pmm[:, 5, :], lhsT=mh[:, c, b * 128:(b + 1) * 128], rhs=wsl[slot][:, c, :],
                    start=(c == 0), stop=(c == 15))) for c in range(16)],
                    reads=[R_mh[b], R_wsl[slot]], writes=[R_bank[5]])
                yield ("mark", f"e{b}c{cg}")
                P.op("dve", lambda cg=cg: dve.tensor_tensor(
                    out=xres[:, b, cg * 512:(cg + 1) * 512], in0=pmm[:, 5, :],
                    in1=xres[:, b, cg * 512:(cg + 1) * 512], op=ALU.add),
                    reads=[R_bank[5], R_xres[b]], writes=[R_xres[b]])
                yield ("rel", "bank5")

        def g_F(b, i):
            xbt, r_xb = fxb[i % 3], R_fxb[i % 3]
            yield ("acq", f"fxb{i % 3}")
            for r in g_rms_to_T(xres[:, b, :], R_xres[b], gffnT, mh[:, :, b * 128:(b + 1) * 128], R_mh[b],
                                xbt, r_xb, big[:, 4096:6144], R_junkF, stats[i % 3], R_stat[i % 3],
                                f"tb{i % 2}", "F", b, R_rsF[b]):
                yield (None if r == ("mark_src_read",) else r)
            yield ("rel", f"fxb{i % 3}")

        prefetched = set()

        def issue_casts(tt, ebs):
            for eb in ebs:
                sset = eb % NSET
                P.dma("pool", d_xc[sset], lambda sset=sset, eb=eb: pool.dma_start(
                    out=xbS[sset], in_=xe[tt, eb * 128:(eb + 1) * 128, :]), writes=[R_xbS[sset]])
                prefetched.add((tt, eb))

        bank_rot = {"i": 0}

        def next_bank(n=4):
            b = bank_rot["i"]
            bank_rot["i"] = (b + 1) % n
            return b

        try:
          ckpt("const")
          for t in range(n_tasks):
            base = t * n_per_task
            alias_edges(R_mh, mh_scratch)
            alias_edges([R_yst] + R_rl, yst_scratch)
            issue_casts(t, [0, 1, 2, 3])
            ensure_loaded(base + 2)

            for i_, (dst, src) in enumerate(((msk, msk_d), (cs, cs_d), (sn, sn_d))):
                P.dma("sp", d_task, lambda dst=dst, src=src: sync.dma_start(
                    out=dst[:].rearrange("p a b -> p (a b)"), in_=src[t]),
                    writes=[R_task if i_ == 0 else Res("tc")])
            R_task.w = (d_task.key, d_task.count)
            for eb in range(5):
                dst_ap, dres, dsem = xdst(eb)
                P.dma("sp", dsem, lambda dst_ap=dst_ap, eb=eb: sync.dma_start(
                    out=dst_ap, in_=xe[t, eb * 128:(eb + 1) * 128, :]), writes=[dres])

            alias_edges(R_mh, mh_scratch)
            alias_edges([R_yst] + R_rl, yst_scratch)
            alias_edges(R_actT, arena_phase1)

            s_kv, s_q0, s_q1 = base % NSLOT, (base + 1) % NSLOT, (base + 2) % NSLOT
            s_u = [(base + 3) % NSLOT, (base + 4) % NSLOT]
            s_vg = [(base + 5) % NSLOT, (base + 6) % NSLOT]
            allA = [f"A{eb}" for eb in range(6)]
            allq = [f"q{b}" for b in range(4)]
            items = [Item(f"A{eb}", g_A_kv(t, eb, 0, s_kv)) for eb in range(6)]
            items += [Item(f"q{b}", g_q(t, b, s_q0, s_q1, b), after=[f"A{b + 1}"]) for b in range(4)]
            items.append(Item("ld_u0", g_load(base + 3), after=allA))
            items += [Item(f"u0{b}", g_u(b, 0, s_u[0], b)) for b in range(4)]
            items.append(Item("ld_u1", g_load(base + 5), after=allq))
            items += [Item(f"u1{b}", g_u(b, 1, s_u[1], b)) for b in range(4)]
            items.append(Item("ld_vg1", g_load(base + 6), after=[f"u0{b}" for b in range(4)]))
            items += [Item(f"vg{b}", g_vg(b, s_vg[0], s_vg[1], b)) for b in range(4)]
            run_pipeline(items, width=4, stagger=7)
            ckpt("AB")

            ensure_loaded(base + 9)
            alias_edges(R_hT, dscratch_core)
            alias_edges(mh_scratch, R_mh)
            alias_edges(yst_scratch, [R_yst])
            items = []
            unit = 0
            for b in range(4):
                for kvg in range(2):
                    items.append(Item(f"a{b}{kvg}", g_att(b, kvg, unit), after=([f"m{b - 2}"] if b >= 2 else [])))
                    unit += 1
                if b >= 1:
                    items.append(Item(f"g{b - 1}", g_gmn(b - 1), after=[f"a{b - 1}0", f"a{b - 1}1"]))
                    items.append(Item(f"E{b - 1}", g_E(b - 1, (0, 1, 2), base), after=[f"g{b - 1}"]))
            items.append(Item("g3", g_gmn(3), after=["a30", "a31"]))
            items.append(Item("E3", g_E(3, (0, 1, 2), base), after=["g3"]))
            items.append(Item("ld_wo3", g_load(base + 10), after=[f"e{b}c0" for b in range(4)]))
            items.append(Item("ld_up0", g_load(base + 11), after=["ld_wo3"] + [f"e{b}c1" for b in range(4)]))
            items.append(Item("ld_up1", g_load(base + 12), after=["ld_up0"] + [f"e{b}c2" for b in range(4)]))
            run_pipeline(items, width=7, stagger=3, lookahead=4)
            ckpt("D")
            si = 10

            bank_rot["i"] = 0
            for cg in range(3, 4):
                n = base + si
                ensure_loaded(n + 2)
                s = n % NSLOT
                for b in range(4):
                    bk = next_bank()
                    P.group("pe", [(lambda c=c, b=b, s=s, bk=bk: pe.matmul(
                        out=pmm[:, bk, :], lhsT=mh[:, c, b * 128:(b + 1) * 128], rhs=wsl[s][:, c, :],
                        start=(c == 0), stop=(c == 15))) for c in range(16)],
                        reads=[R_mh[b], R_wsl[s]], writes=[R_bank[bk]])
                    P.op("dve", lambda b=b, cg=cg, bk=bk: dve.tensor_tensor(
                        out=xres[:, b, cg * 512:(cg + 1) * 512], in0=pmm[:, bk, :],
                        in1=xres[:, b, cg * 512:(cg + 1) * 512], op=ALU.add),
                        reads=[R_bank[bk], R_xres[b]], writes=[R_xres[b]])
                si += 1
            ckpt("E")

            alias_edges(dscratch_core, fxb_res)
            items = [Item(f"F{b}", g_F(b, b)) for b in range(4)]
            run_pipeline(items, width=3, stagger=4)
            ckpt("F")

            bank_rot["i"] = 0
            for fg in range(16):
                n = base + si
                ensure_loaded(n + 2)
                s = n % NSLOT
                for fi in range(4):
                    fc = fg * 4 + fi
                    bk = next_bank(2)
                    P.group("pe", [(lambda c=c, fi=fi, s=s, bk=bk: pe.matmul(
                        out=pmm[:, bk, :], lhsT=wsl[s][:, c, fi * 128:(fi + 1) * 128], rhs=mh[:, c, :],
                        start=(c == 0), stop=(c == 15))) for c in range(16)],
                        reads=R_mh + [R_wsl[s]], writes=[R_bank[bk]])
                    rl = yst[:, bk, :]
                    P.op("act", lambda bk=bk, rl=rl: act.activation(out=rl, in_=pmm[:, bk, :], func=AF.Relu),
                         reads=[R_bank[bk]], writes=[R_rl[bk], R_yst])
                    extra = all_phase1 if (fg == 0 and fi == 0) else []
                    P.op("dve", lambda fc=fc, rl=rl: dve.tensor_tensor(out=actT[:, fc, :], in0=rl, in1=rl,
                                                                       op=ALU.mult),
                         reads=[R_rl[bk]], writes=[R_actT[fg]] + extra)
                si += 1
            ckpt("G")

            for cg in range(4):
                for sub in range(4):
                    n = base + si
                    ensure_loaded(n + 2)
                    s = n % NSLOT
                    for b in range(4):
                        P.group("pe", [(lambda fc=fc, b=b, s=s, sub=sub: pe.matmul(
                            out=pmm[:, 2 + b, :], lhsT=actT[:, sub * 16 + fc, b * 128:(b + 1) * 128],
                            rhs=wsl[s][:, fc, :], start=(sub == 0 and fc == 0),
                            stop=(sub == 3 and fc == 15))) for fc in range(16)],
                            reads=[R_actT[sub * 4 + k] for k in range(4)] + [R_wsl[s]],
                            writes=[R_bank[2 + b]])
                        if sub == 3:
                            P.op("dve", lambda b=b, cg=cg: dve.scalar_tensor_tensor(
                                out=yst[:, b, :], in0=pmm[:, 2 + b, :], scalar=rsF[:, b:b + 1],
                                in1=xres[:, b, cg * 512:(cg + 1) * 512], op0=ALU.mult, op1=ALU.add),
                                reads=[R_bank[2 + b], R_xres[b], R_rsF[b]],
                                writes=[R_yst] + ([R_rl[b]] if b < 2 else []))
                    si += 1
                P.dma("sp", d_y, lambda cg=cg: sync.dma_start(
                    out=y[t].rearrange("(b p) n -> p b n", p=128)[:, :, cg * 512:(cg + 1) * 512],
                    in_=yst[:]), reads=[R_yst])
            assert si == n_per_task
        except _Stop:
            pass
        for e in ("pe", "act", "dve", "pool"):
            if P.cnt[e]:
                sync.wait_ge(P.semh[P.sem[e]], P.cnt[e])
        P.wait_all("sp", [d_const, d_task, d_xst, d_yh, d_wsp] + d_xres + d_w + d_xc + [d_y])
    return nc


def _task_table():
    tasks = []
    for s in range(2):
        for j in range(8):
            tasks.append(("p", s, j * T))
    for s in range(4):
        for j in range(4):
            tasks.append(("s", s, j * T))
    return tasks


def _rope_tables(pos):
    inv_freq = 500000.0 ** (-np.arange(0, 32, 2, dtype=np.float64) / 32.0)
    ang = pos.astype(np.float64)[:, None] * inv_freq[None, :]
    return np.cos(ang).astype(np.float32), np.sin(ang).astype(np.float32)


_NC_CACHE = {}


def kernel(x_prompt, x_sample, g_mix, w_in, g_q, g_k, sink, g_v_ln, b_v_ln, w_spatial,
           b_spatial, g_attn_out, g_gmlp_out, w_out, g_ffn, w_up, w_down):
    f32 = np.float32
    x_prompt = np.asarray(x_prompt, f32)
    x_sample = np.asarray(x_sample, f32)
    tasks = _task_table()

    def colT(v):
        return np.ascontiguousarray(np.asarray(v, f32).reshape(16, 128).T)

    def bcast(v, n):
        return np.ascontiguousarray(np.broadcast_to(np.asarray(v, f32).reshape(1, n), (128, n)))

    shared = {
        "w_in": np.ascontiguousarray(
            np.asarray(w_in, f32)[0].reshape(16, 128, 7, 512).transpose(2, 1, 0, 3)).reshape(7, 128, 8192),
        "w_out": np.ascontiguousarray(
            np.asarray(w_out, f32)[0].reshape(16, 128, 4, 512).transpose(2, 1, 0, 3)).reshape(4, 128, 8192),
        "w_up": np.ascontiguousarray(
            np.asarray(w_up, f32)[0].reshape(16, 128, 16, 512).transpose(2, 1, 0, 3)).reshape(16, 128, 8192),
        "w_down": np.ascontiguousarray(
            np.asarray(w_down, f32)[0].reshape(4, 16, 128, 4, 512).transpose(3, 0, 2, 1, 4)).reshape(16, 128, 8192),
        "gmixT": colT(g_mix[0]),
        "gffnT": colT(g_ffn[0]),
        "goutT": colT(np.concatenate([np.asarray(g_attn_out, f32)[0], np.asarray(g_gmlp_out, f32)[0]])),
        "gq": bcast(g_q[0], 128),
        "gk": bcast(g_k[0], 128),
        "gln": bcast(g_v_ln[0], 1024),
        "bln": bcast(b_v_ln[0], 1024),
        "sinkb": bcast(sink[0], 8),
        "wspT": np.ascontiguousarray(np.transpose(np.asarray(w_spatial, f32)[0], (2, 0, 1)).reshape(128, 1024)),
        "bsp": np.ascontiguousarray(np.asarray(b_spatial, f32)[0].T),
    }

    qi = np.arange(128)[:, None]
    kj = np.arange(384)[None, :]
    band = (kj >= qi) & (kj <= qi + 256)

    in_maps = []
    for c in range(N_CORES):
        xe = np.zeros((NT, TE, D), f32)
        msk = np.empty((NT, 128, 3, 384), f32)
        cs = np.empty((NT, 128, 6, 16), f32)
        sn = np.empty((NT, 128, 6, 16), f32)
        for i in range(NT):
            grp, s, start = tasks[c * NT + i]
            xs = x_prompt[s] if grp == "p" else x_sample[s]
            S = xs.shape[0]
            lo, hi = start - 128, start + T + 128
            a, b = max(lo, 0), min(hi, S)
            xe[i, a - lo:b - lo] = xs[a:b]
            left_ok = lo >= 0
            right_ok = hi <= S
            m0 = band if left_ok else (band & (kj >= 128))
            m2 = band if right_ok else (band & (kj < 256))
            for kind, m in enumerate((m0, band, m2)):
                msk[i, :, kind, :] = np.where(m, 0.0, NEG).astype(f32)
            pos = np.arange(lo, hi)
            co, si_ = _rope_tables(pos)
            cs[i] = co.reshape(6, 128, 16).transpose(1, 0, 2)
            sn[i] = si_.reshape(6, 128, 16).transpose(1, 0, 2)
        m = dict(shared)
        m["xe"] = xe
        m["msk"] = msk.reshape(NT, 128, 3 * 384)
        m["cs"] = cs.reshape(NT, 128, 96)
        m["sn"] = sn.reshape(NT, 128, 96)
        in_maps.append(m)

    if "nc" not in _NC_CACHE:
        _NC_CACHE["nc"] = build_program()
    nc = _NC_CACHE["nc"]
    res = run_bass_kernel_spmd(nc, in_maps, core_ids=list(range(N_CORES)))

    y_prompt = np.empty((2, 4096, D), f32)
    y_sample = np.empty((4, 2048, D), f32)
    for c in range(N_CORES):
        yc = np.asarray(res.results[c]["y"], f32)
        for i in range(NT):
            grp, s, start = tasks[c * NT + i]
            if grp == "p":
                y_prompt[s, start:start + T] = yc[i]
            else:
                y_sample[s, start:start + T] = yc[i]
    return (y_prompt, y_sample)
```

```python
import contextlib
import math
import numpy as np
import concourse.bass as bass
import concourse.mybir as mybir
from concourse.bass_utils import run_bass_kernel_spmd

F32 = mybir.dt.float32
BF16 = mybir.dt.bfloat16
AF = mybir.ActivationFunctionType
ALU = mybir.AluOpType
AX = mybir.AxisListType

N_CORES = 8
NT = 4
T = 512
TE = 768
D = 2048
DFF = 8192
EPS = 1e-6
NEG = -1e30
SCALE = 1.0 / math.sqrt(128.0)
NSLOT = 3


class Res:
    __slots__ = ("name", "w", "r")

    def __init__(self, name):
        self.name = name
        self.w = None
        self.r = {}


class DSem:
    def __init__(self, prog, name):
        self.h = prog.new_sem(name)
        self.key = name
        self.count = 0


class Prog:
    ENGS = ("pe", "act", "dve", "pool", "sp")

    def __init__(self, nc, stack):
        self.nc = nc
        self.stack = stack
        self.eng = {"pe": nc.tensor, "act": nc.scalar, "dve": nc.vector,
                    "pool": nc.gpsimd, "sp": nc.sync}
        self.semh = {}
        self.sem = {}
        self.cnt = {}
        self.waited = {e: {} for e in self.ENGS}
        for e in self.ENGS:
            self.sem[e] = "E_" + e
            self.semh["E_" + e] = stack.enter_context(nc.semaphore("E_" + e))
            self.cnt[e] = 0

    def new_sem(self, name):
        h = self.stack.enter_context(self.nc.semaphore(name))
        self.semh[name] = h
        return h

    def _emit_waits(self, e, reads, writes):
        need = {}

        def add(pt):
            if pt is None:
                return
            k, v = pt
            if need.get(k, 0) < v:
                need[k] = v
        for r in reads:
            add(r.w)
        for w in writes:
            add(w.w)
            for k, v in w.r.items():
                add((k, v))
        own = self.sem[e]
        for k, v in need.items():
            if e == "pe" and k == own:
                continue
            if self.waited[e].get(k, 0) >= v:
                continue
            self.eng[e].wait_ge(self.semh[k], v)
            self.waited[e][k] = v

    def _commit(self, pt, reads, writes):
        k, v = pt
        for r in reads:
            if r.r.get(k, 0) < v:
                r.r[k] = v
        for w in writes:
            w.w = pt
            w.r = {}

    def op(self, e, emit, reads=(), writes=()):
        self._emit_waits(e, reads, writes)
        ins = emit()
        self.cnt[e] += 1
        assert self.cnt[e] < 60000
        ins.then_inc(self.semh[self.sem[e]], 1)
        self._commit((self.sem[e], self.cnt[e]), reads, writes)

    def group(self, e, emits, reads=(), writes=()):
        self._emit_waits(e, reads, writes)
        ins = None
        for f in emits:
            ins = f()
        self.cnt[e] += 1
        ins.then_inc(self.semh[self.sem[e]], 1)
        self._commit((self.sem[e], self.cnt[e]), reads, writes)

    def dma(self, q, dsem, emit, reads=(), writes=()):
        self._emit_waits(q, reads, writes)
        ins = emit()
        dsem.count += 16
        ins.then_inc(dsem.h, 16)
        self._commit((dsem.key, dsem.count), reads, writes)

    def wait_all(self, e, dsems):
        for d in dsems:
            if d.count:
                self.eng[e].wait_ge(d.h, d.count)


class _Stop(Exception):
    pass


def alias_edges(src, dst):
    for d in dst:
        for s_ in src:
            if s_.w is not None:
                k, v = s_.w
                if d.r.get(k, 0) < v:
                    d.r[k] = v
            for k, v in s_.r.items():
                if d.r.get(k, 0) < v:
                    d.r[k] = v


class Item:
    def __init__(self, name, gen, after=()):
        self.name = name
        self.gen = gen
        self.after = tuple(after)


def run_pipeline(items, width, stagger, lookahead=1):
    held = {}
    done = set()
    active = []
    pending_items = list(items)
    rnd = 0
    last_start = -10 ** 9
    while active or pending_items:
        if pending_items and len(active) < width and (rnd - last_start >= stagger or not active):
            for it in pending_items[:lookahead]:
                if all(a in done for a in it.after):
                    pending_items.remove(it)
                    active.append([it, None])
                    last_start = rnd
                    break
            else:
                assert active, f"pipeline stuck at {pending_items[0].name}"
        progressed = False
        for ent in list(active):
            it, pending = ent
            if pending is not None:
                if pending in held:
                    continue
                held[pending] = it.name
                ent[1] = None
                progressed = True
                continue
            try:
                r = next(it.gen)
            except StopIteration:
                active.remove(ent)
                done.add(it.name)
                for k in [k for k, v in held.items() if v == it.name]:
                    del held[k]
                progressed = True
                continue
            progressed = True
            if isinstance(r, tuple):
                if r[0] == "acq":
                    if r[1] in held:
                        ent[1] = r[1]
                    else:
                        held[r[1]] = it.name
                elif r[0] == "rel":
                    assert held.get(r[1]) == it.name, (r, held)
                    del held[r[1]]
                elif r[0] == "mark":
                    done.add(r[1])
        assert progressed or pending_items, "pipeline deadlock"
        rnd += 1


def build_program(n_tasks=NT, stop_at=None):
    nc = bass.Bass("TRN2", target_bir_lowering=False)

    def ckpt(name):
        if stop_at is not None and name == stop_at:
            raise _Stop()

    def din(name, shape):
        return nc.dram_tensor(name, shape, F32, kind="ExternalInput").ap()

    xe = din("xe", [NT, TE, D])
    y = nc.dram_tensor("y", [NT, T, D], F32, kind="ExternalOutput").ap()
    w_in = din("w_in", [7, 128, 8192])
    w_out = din("w_out", [4, 128, 8192])
    w_up = din("w_up", [16, 128, 8192])
    w_down = din("w_down", [16, 128, 8192])
    msk_d = din("msk", [NT, 128, 3 * 384])
    cs_d = din("cs", [NT, 128, 96])
    sn_d = din("sn", [NT, 128, 96])
    gmixT_d = din("gmixT", [128, 16])
    gffnT_d = din("gffnT", [128, 16])
    goutT_d = din("goutT", [128, 16])
    gq_d = din("gq", [128, 128])
    gk_d = din("gk", [128, 128])
    gln_d = din("gln", [128, 1024])
    bln_d = din("bln", [128, 1024])
    sink_d = din("sinkb", [128, 8])
    wspT_d = din("wspT", [128, 1024])
    bsp_d = din("bsp", [128, 8])

    with contextlib.ExitStack() as st:
        P = Prog(nc, st)

        def sb(name, shape, dt):
            return st.enter_context(nc.sbuf_tensor(name, shape, dt))

        xres = sb("xres", [128, 4, D], F32)
        xst = sb("xst", [128, D], F32)
        xb0 = sb("xb", [128, D], BF16)
        big = sb("big", [128, 32768], BF16)
        mh = sb("mh", [128, 16, T], BF16)
        wsl = [sb(f"wsl{i}", [128, 16, 512], BF16) for i in range(NSLOT)]
        yst = sb("yst", [128, 4, 512], F32)
        tmpf0 = sb("tmpf", [128, 1024], F32)
        ident = sb("ident", [128, 128], BF16)
        identf = sb("identf", [128, 128], F32)
        gmixT = sb("gmixT_s", [128, 16], F32)
        gffnT = sb("gffnT_s", [128, 16], F32)
        goutT = sb("goutT_s", [128, 16], F32)
        gq = sb("gq_s", [128, 128], F32)
        gk = sb("gk_s", [128, 128], F32)
        gln = sb("gln_s", [128, 1024], F32)
        bln = sb("bln_s", [128, 1024], F32)
        sinkb = sb("sink_s", [128, 8], F32)
        nsink = sb("nsink_s", [128, 8], F32)
        wspT = sb("wspT_s", [128, 8, 128], BF16)
        bsp = sb("bsp_s", [128, 8], F32)
        msk = sb("msk_s", [128, 3, 384], F32)
        cs = sb("cs_s", [128, 6, 16], F32)
        sn = sb("sn_s", [128, 6, 16], F32)
        stats = [sb(f"stat{i}", [128, 32], F32) for i in range(9)]
        ropes = [sb(f"rope{i}", [128, 8, 2, 16], F32) for i in range(4)]
        bnsts = [sb(f"bnst{i}", [128, 2, 6], F32) for i in range(4)]
        rsA = sb("rsA", [128, 8], F32)
        rsA2 = sb("rsA2", [128, 8], F32)
        rsF = sb("rsF", [128, 4], F32)

        hT = big[:, 0:12288].rearrange("p (c t) -> p c t", t=TE)
        qT = big[:, 12288:16384].rearrange("p (h t) -> p h t", t=T)
        kT = big[:, 16384:17920].rearrange("p (h t) -> p h t", t=TE)
        vv = big[:, 17920:19456].rearrange("p (b n) -> p b n", n=256)
        vgn = big[:, 19456:23552].rearrange("p (b n) -> p b n", n=1024)
        uu = big[:, 23552:31744].bitcast(F32).rearrange("p (b n) -> p b n", n=1024)
        actT = big[:, :].rearrange("p (f t) -> p f t", t=T)
        smS, pbS, pTsS = [], [], []
        for i in range(2):
            o = i * 6144
            smS.append(big[:, o:o + 3072].bitcast(F32).rearrange("p (h k) -> p h k", k=384))
            pbS.append(big[:, o + 3072:o + 4608].rearrange("p (h k) -> p h k", k=384))
            pTsS.append(big[:, o + 4608:o + 6144].rearrange("p (j q) -> p j q", q=128))
        fxb = [xb0[:], big[:, 0:2048], big[:, 2048:4096]]
        mhf = mh[:].rearrange("p c t -> p (c t)")
        ystf = yst[:].rearrange("p a b -> p (a b)")
        xbS = [xb0[:], mhf[:, 0:2048], mhf[:, 2048:4096], ystf[:, 0:1024].bitcast(BF16)]
        tmpfS = [tmpf0[:], mhf[:, 4096:6144].bitcast(F32), mhf[:, 6144:8192].bitcast(F32), ystf[:, 1024:2048]]
        mixtok = [xst[:], ystf]
        xbD = [xb0[:], tmpf0[:].bitcast(BF16)]

        pmm = st.enter_context(nc.psum_tensor("pmm", [128, 8, 512], F32))
        TB = [pmm[:, 4:6, :].bitcast(BF16), pmm[:, 6:8, :].bitcast(BF16)]
        TB16 = [tb.rearrange("p a (c t) -> p (a c) t", t=128) for tb in TB]

        R_xres = [Res(f"xres{b}") for b in range(4)]
        R_xst = Res("xst")
        R_yst = Res("yst")
        R_mix = [R_xst, R_yst]
        R_xbS = [Res(f"xbS{i}") for i in range(4)]
        R_tmpfS = [Res(f"tmpfS{i}") for i in range(4)]
        R_fxb = [R_xbS[0], Res("fxb1"), Res("fxb2")]
        R_stat = [Res(f"stat{i}") for i in range(9)]
        R_rope = [Res(f"rope{i}") for i in range(4)]
        R_hT = [Res(f"hT{b}") for b in range(6)]
        R_qT = [Res(f"qT{b}") for b in range(4)]
        R_kT = [Res(f"kT{b}") for b in range(6)]
        R_v = [Res(f"v{b}") for b in range(6)]
        R_vgn = [Res(f"vgn{b}") for b in range(4)]
        R_u = [Res(f"u{b}") for b in range(4)]
        R_sm = [Res("sm0"), Res("sm1")]
        R_pb = [Res("pb0"), Res("pb1")]
        R_pTs = [Res("pTs0"), Res("pTs1")]
        R_actT = [Res(f"actT{g}") for g in range(16)]
        R_mh = [Res(f"mh{b}") for b in range(4)]
        R_wsl = [Res(f"wsl{i}") for i in range(NSLOT)]
        R_rl = [Res("rl0"), Res("rl1")]
        R_const = Res("const")
        R_task = Res("taskconst")
        R_bns = [Res(f"bn{i}") for i in range(4)]
        R_rsA = [Res(f"rsA{i}") for i in range(6)]
        R_rsF = [Res(f"rsF{i}") for i in range(4)]
        R_junkF = Res("junkF")
        R_bank = [Res(f"bank{i}") for i in range(8)]
        R_TB = [[R_bank[4], R_bank[5]], [R_bank[6], R_bank[7]]]
        arena_phase1 = R_hT + R_qT + R_kT + R_v + R_vgn + R_u
        dscratch_core = R_sm + R_pb + R_pTs
        fxb_res = [R_fxb[1], R_fxb[2], R_junkF]
        all_phase1 = arena_phase1 + dscratch_core + fxb_res
        mh_scratch = [R_xbS[1], R_xbS[2], R_tmpfS[1], R_tmpfS[2]]
        yst_scratch = [R_xbS[3], R_tmpfS[3]]

        d_const = DSem(P, "d_const")
        d_task = DSem(P, "d_task")
        d_xst = DSem(P, "d_xst")
        d_xres = [DSem(P, f"d_xres{b}") for b in range(4)]
        d_w = [DSem(P, f"d_w{i}") for i in range(NSLOT)]
        d_y = DSem(P, "d_y")
        d_yh = DSem(P, "d_yh")
        d_xc = [DSem(P, f"d_xc{i}") for i in range(4)]

        def xdst(eb):
            if eb in (0, 5):
                return xst[:], R_xst, d_xst
            return xres[:, eb - 1, :], R_xres[eb - 1], d_xres[eb - 1]

        NSET = 4
        SSTAT = [0, 1, 2, 8]

        sync, act, dve, pool, pe = nc.sync, nc.scalar, nc.vector, nc.gpsimd, nc.tensor

        for dst, src in ((gmixT, gmixT_d), (gffnT, gffnT_d), (goutT, goutT_d), (gq, gq_d),
                         (gk, gk_d), (gln, gln_d), (bln, bln_d), (sinkb, sink_d), (bsp, bsp_d)):
            P.dma("sp", d_const, lambda dst=dst, src=src: sync.dma_start(out=dst[:], in_=src[:, :]),
                  writes=[Res("c")])
        R_const.w = (d_const.key, d_const.count)
        d_wsp = DSem(P, "d_wsp")
        R_wsp = Res("wsp")
        P.dma("pool", d_wsp, lambda: pool.dma_start(out=wspT[:].rearrange("p h i -> p (h i)"), in_=wspT_d[:, :]),
              writes=[R_wsp])
        R_ident = Res("ident")
        P.op("pool", lambda: pool.memset(identf[:], 0.0), writes=[R_ident])
        P.op("pool", lambda: pool.affine_select(out=identf[:], in_=identf[:], pattern=[[-1, 128]],
                                                compare_op=ALU.not_equal, fill=1.0, base=0,
                                                channel_multiplier=1),
             reads=[R_ident], writes=[R_ident])
        P.op("dve", lambda: dve.tensor_copy(out=ident[:], in_=identf[:]), reads=[R_ident], writes=[R_ident])
        R_nsink = Res("nsink")
        P.op("dve", lambda: dve.tensor_scalar(out=nsink[:], in0=sinkb[:], scalar1=-1.0, scalar2=None,
                                              op0=ALU.mult), reads=[R_const], writes=[R_nsink])

        slabs = []
        for cg in (2, 0, 1, 3, 4, 5, 6):
            slabs.append(w_in[cg])
        for cg in range(4):
            slabs.append(w_out[cg])
        for fg in range(16):
            slabs.append(w_up[fg])
        for cg in range(4):
            for sub in range(4):
                slabs.append(w_down[cg * 4 + sub])
        n_per_task = len(slabs)
        total_slabs = n_per_task * n_tasks
        wstate = {"issued": 0}

        def ensure_loaded(upto):
            upto = min(upto, total_slabs - 1)
            while wstate["issued"] <= upto:
                n = wstate["issued"]
                s = n % NSLOT
                src = slabs[n % n_per_task]
                P.dma("pool", d_w[s], lambda s=s, src=src: pool.dma_start(
                    out=wsl[s][:].rearrange("p c n -> p (c n)"), in_=src), writes=[R_wsl[s]])
                wstate["issued"] += 1

        def g_load(upto):
            ensure_loaded(upto)
            yield

        def rstd_chain(stt, rs, col, n, width=1):
            a = stt[:, col:col + width]
            P.op("dve", lambda: dve.tensor_scalar(out=a, in0=a, scalar1=1.0 / n, scalar2=EPS,
                                                  op0=ALU.mult, op1=ALU.add), reads=[rs], writes=[rs])
            yield
            P.op("act", lambda: act.activation(out=a, in_=a, func=AF.Ln), reads=[rs], writes=[rs])
            yield
            P.op("act", lambda: act.activation(out=a, in_=a, func=AF.Exp, scale=-0.5), reads=[rs], writes=[rs])
            yield

        def g_rms_to_T(src_ap, src_res, gT, dst_ap, dst_res, xbt, r_xb, junk, r_junk, stt, rs, tbname,
                       mode, col, out_res, cast_dma=None):
            if cast_dma == "done":
                pass
            elif cast_dma is not None:
                dsem, src_d = cast_dma
                P.dma("pool", dsem, lambda: pool.dma_start(out=xbt, in_=src_d), writes=[r_xb])
            else:
                P.op("act", lambda: act.activation(out=xbt, in_=src_ap, func=AF.Copy),
                     reads=[src_res], writes=[r_xb])
            yield
            P.op("act", lambda: act.activation(out=junk, in_=src_ap, func=AF.Square, accum_out=stt[:, 0:1]),
                 reads=[src_res], writes=[r_junk, rs])
            yield ("mark_src_read",)
            ti = int(tbname[-1])
            yield ("acq", f"bank{4 + 2 * ti}")
            yield ("acq", f"bank{5 + 2 * ti}")
            P.group("pe", [(lambda c=c: pe.transpose(out=TB16[ti][:, c, :], in_=xbt[:, c * 128:(c + 1) * 128],
                                                     identity=ident[:])) for c in range(16)],
                    reads=[r_xb, R_ident], writes=R_TB[ti])
            yield
            a = stt[:, 0:1]
            P.op("dve", lambda: dve.tensor_tensor(out=dst_ap, in0=TB16[ti],
                                                  in1=gT[:].unsqueeze(2).broadcast_to([128, 16, 128]),
                                                  op=ALU.mult),
                 reads=R_TB[ti] + [R_const], writes=[dst_res])
            yield ("rel", f"bank{4 + 2 * ti}")
            yield ("rel", f"bank{5 + 2 * ti}")
            P.op("dve", lambda: dve.tensor_scalar(out=a, in0=a, scalar1=1.0 / D, scalar2=EPS,
                                                  op0=ALU.mult, op1=ALU.add), reads=[rs], writes=[rs])
            yield
            if mode == "A":
                P.op("act", lambda: act.activation(out=a, in_=a, func=AF.Ln), reads=[rs], writes=[rs])
                yield
                P.op("act", lambda: act.activation(out=rsA[:, col:col + 1], in_=a, func=AF.Exp, scale=-0.5),
                     reads=[rs], writes=[out_res])
                yield
                P.op("dve", lambda: dve.tensor_tensor(out=rsA2[:, col:col + 1], in0=rsA[:, col:col + 1],
                                                      in1=rsA[:, col:col + 1], op=ALU.mult),
                     reads=[out_res], writes=[out_res])
                yield
            else:
                P.op("dve", lambda: dve.reciprocal(out=rsF[:, col:col + 1], in_=a), reads=[rs], writes=[out_res])
                yield

        def g_qk_chain(src3, src_res, H, g, eb, dstb, r_dst, tf, r_tf, stt, rs, rp, r_rp):
            tv = tf[:, 0:H * 128]
            tv3 = tv.rearrange("p (h d) -> p h d", d=128)
            P.op("act", lambda: act.activation(out=tv3, in_=src3, func=AF.Square), reads=[src_res], writes=[r_tf])
            yield
            P.op("dve", lambda: dve.tensor_reduce(out=stt[:, 8:8 + H], in_=tv3, axis=AX.X, op=ALU.add),
                 reads=[r_tf], writes=[rs])
            yield
            P.op("dve", lambda: dve.tensor_scalar(out=stt[:, 8:8 + H], in0=stt[:, 8:8 + H],
                                                  scalar1=rsA2[:, eb:eb + 1], scalar2=None, op0=ALU.mult),
                 reads=[rs, R_rsA[eb]], writes=[rs])
            yield
            yield from rstd_chain(stt, rs, 8, 128, H)
            P.op("dve", lambda: dve.tensor_scalar(out=stt[:, 8:8 + H], in0=stt[:, 8:8 + H],
                                                  scalar1=rsA[:, eb:eb + 1], scalar2=None, op0=ALU.mult),
                 reads=[rs, R_rsA[eb]], writes=[rs])
            yield
            P.op("dve", lambda: dve.tensor_tensor(out=tv3, in0=src3,
                                                  in1=stt[:, 8:8 + H].unsqueeze(2).broadcast_to([128, H, 128]),
                                                  op=ALU.mult), reads=[src_res, rs], writes=[r_tf])
            yield ("rel", "srcbank")
            P.op("dve", lambda: dve.tensor_tensor(out=tv3, in0=tv3,
                                                  in1=g[:].unsqueeze(1).broadcast_to([128, H, 128]),
                                                  op=ALU.mult), reads=[r_tf, R_const], writes=[r_tf])
            yield
            cosb = cs[:, eb, :].unsqueeze(1).broadcast_to([128, H, 16])
            sinb = sn[:, eb, :].unsqueeze(1).broadcast_to([128, H, 16])
            x1 = tv3[:, :, 0:16]
            x2 = tv3[:, :, 16:32]
            dst3 = dstb.rearrange("p (h d) -> p h d", d=128)
            r3 = rp[:, 0:H, :, :]
            P.op("act", lambda: act.activation(out=dstb, in_=tv, func=AF.Copy), reads=[r_tf], writes=[r_dst])
            yield
            for (a1, b1, a2, b2, opx, lo) in ((x1, cosb, x2, sinb, ALU.subtract, 0),
                                               (x2, cosb, x1, sinb, ALU.add, 16)):
                P.op("dve", lambda a1=a1, b1=b1: dve.tensor_tensor(out=r3[:, :, 0, :], in0=a1, in1=b1, op=ALU.mult),
                     reads=[r_tf, R_task], writes=[r_rp])
                yield
                P.op("dve", lambda a2=a2, b2=b2: dve.tensor_tensor(out=r3[:, :, 1, :], in0=a2, in1=b2, op=ALU.mult),
                     reads=[r_tf, R_task, r_rp], writes=[r_rp])
                yield
                P.op("dve", lambda opx=opx, lo=lo: dve.tensor_tensor(out=dst3[:, :, lo:lo + 16], in0=r3[:, :, 0, :],
                                                                     in1=r3[:, :, 1, :], op=opx),
                     reads=[r_rp], writes=[r_dst])
                yield

        def g_A_kv(t, eb, si_, slot):
            sset = eb % NSET
            xbt, r_xb = xbS[sset], R_xbS[sset]
            tf, r_tf = tmpfS[sset], R_tmpfS[sset]
            stt, rs = stats[SSTAT[sset]], R_stat[SSTAT[sset]]
            rp, r_rp = ropes[sset], R_rope[sset]
            yield ("acq", f"set{sset}")
            dst_ap, dres, dsem = xdst(eb)
            if eb in (0, 5):
                yield ("acq", "xst")
            if eb == 5:
                P.dma("sp", dsem, lambda: sync.dma_start(out=dst_ap, in_=xe[t, eb * 128:(eb + 1) * 128, :]),
                      writes=[dres])
                yield
            tbname = f"tb{eb % 2}"
            for r in g_rms_to_T(dst_ap, dres, gmixT, hT[:, :, eb * 128:(eb + 1) * 128], R_hT[eb],
                                xbt, r_xb, tf.bitcast(BF16), r_tf, stt, rs, tbname, "A", eb, R_rsA[eb],
                                cast_dma=("done" if (t, eb) in prefetched else
                                          (d_xc[sset], xe[t, eb * 128:(eb + 1) * 128, :]))):
                if r == ("mark_src_read",):
                    if eb in (0, 5):
                        yield ("rel", "xst")
                    else:
                        yield
                else:
                    yield r
            bk = eb % 4
            yield ("acq", f"bank{bk}")
            P.group("pe", [(lambda c=c: pe.matmul(out=pmm[:, bk, :], lhsT=hT[:, c, eb * 128:(eb + 1) * 128],
                                                  rhs=wsl[slot][:, c, :], start=(c == 0), stop=(c == 15)))
                           for c in range(16)],
                    reads=[R_hT[eb], R_wsl[slot]], writes=[R_bank[bk]])
            yield
            P.op("act", lambda: act.activation(out=vv[:, eb, :], in_=pmm[:, bk, 256:512], func=AF.Copy,
                                               scale=rsA[:, eb:eb + 1]),
                 reads=[R_bank[bk], R_rsA[eb]], writes=[R_v[eb]])
            yield
            src3 = pmm[:, bk, 0:256].rearrange("p (h d) -> p h d", d=128)
            for r in g_qk_chain(src3, R_bank[bk], 2, gk, eb, xbt[:, 0:256], r_xb, tf, r_tf, stt, rs, rp, r_rp):
                if r == ("rel", "srcbank"):
                    yield ("rel", f"bank{bk}")
                else:
                    yield r
            ti = eb % 2
            yield ("acq", f"bank{4 + 2 * ti}")
            P.group("pe", [(lambda h=h: pe.transpose(out=TB[ti][:, 0, h * 128:(h + 1) * 128],
                                                     in_=xbt[:, h * 128:(h + 1) * 128], identity=ident[:]))
                           for h in range(2)],
                    reads=[r_xb, R_ident], writes=[R_TB[ti][0]])
            yield
            P.op("dve", lambda: dve.tensor_copy(out=kT[:, :, eb * 128:(eb + 1) * 128],
                                                in_=TB[ti][:, 0, 0:256].rearrange("p (h t) -> p h t", t=128)),
                 reads=[R_TB[ti][0]], writes=[R_kT[eb]])
            yield ("rel", f"bank{4 + 2 * ti}")
            yield ("rel", f"set{sset}")

        def g_q(t, b, slot0, slot1, idx):
            eb = b + 1
            sset = (idx + 2) % NSET
            xbt, r_xb = xbS[sset], R_xbS[sset]
            tf, r_tf = tmpfS[sset], R_tmpfS[sset]
            stt, rs = stats[SSTAT[sset]], R_stat[SSTAT[sset]]
            rp, r_rp = ropes[sset], R_rope[sset]
            yield ("acq", f"set{sset}")
            pr = (idx % 2) * 2
            yield ("acq", f"bank{pr}")
            yield ("acq", f"bank{pr + 1}")
            for j, slot in ((0, slot0), (1, slot1)):
                P.group("pe", [(lambda c=c, j=j, slot=slot: pe.matmul(
                    out=pmm[:, pr + j, :], lhsT=hT[:, c, eb * 128:(eb + 1) * 128], rhs=wsl[slot][:, c, :],
                    start=(c == 0), stop=(c == 15))) for c in range(16)],
                    reads=[R_hT[eb], R_wsl[slot]], writes=[R_bank[pr + j]])
                yield
            src3 = pmm[:, pr:pr + 2, :].rearrange("p a (h d) -> p (a h) d", d=128)

            for r in g_qk_chain2(src3, [R_bank[pr], R_bank[pr + 1]], 8, gq, eb, xbt[:, 0:1024], r_xb,
                                 tf, r_tf, stt, rs, rp, r_rp):
                if r == ("rel", "srcbank"):
                    yield ("rel", f"bank{pr}")
                    yield ("rel", f"bank{pr + 1}")
                else:
                    yield r
            ti = idx % 2
            yield ("acq", f"bank{4 + 2 * ti}")
            P.group("pe", [(lambda h=h: pe.transpose(out=TB[ti][:, 0, h * 128:(h + 1) * 128],
                                                     in_=xbt[:, h * 128:(h + 1) * 128], identity=ident[:]))
                           for h in range(8)],
                    reads=[r_xb, R_ident], writes=[R_TB[ti][0]])
            yield
            P.op("dve", lambda: dve.tensor_copy(out=qT[:, :, b * 128:(b + 1) * 128],
                                                in_=TB[ti][:, 0, :].rearrange("p (h t) -> p h t", t=128)),
                 reads=[R_TB[ti][0]], writes=[R_qT[b]])
            yield ("rel", f"bank{4 + 2 * ti}")
            yield ("rel", f"set{sset}")

        def g_qk_chain2(src3, src_rl, H, g, eb, dstb, r_dst, tf, r_tf, stt, rs, rp, r_rp):
            tv = tf[:, 0:H * 128]
            tv3 = tv.rearrange("p (h d) -> p h d", d=128)
            P.op("act", lambda: act.activation(out=tv3, in_=src3, func=AF.Square), reads=src_rl, writes=[r_tf])
            yield
            P.op("dve", lambda: dve.tensor_reduce(out=stt[:, 8:8 + H], in_=tv3, axis=AX.X, op=ALU.add),
                 reads=[r_tf], writes=[rs])
            yield
            P.op("dve", lambda: dve.tensor_scalar(out=stt[:, 8:8 + H], in0=stt[:, 8:8 + H],
                                                  scalar1=rsA2[:, eb:eb + 1], scalar2=None, op0=ALU.mult),
                 reads=[rs, R_rsA[eb]], writes=[rs])
            yield
            yield from rstd_chain(stt, rs, 8, 128, H)
            P.op("dve", lambda: dve.tensor_scalar(out=stt[:, 8:8 + H], in0=stt[:, 8:8 + H],
                                                  scalar1=rsA[:, eb:eb + 1], scalar2=None, op0=ALU.mult),
                 reads=[rs, R_rsA[eb]], writes=[rs])
            yield
            P.op("dve", lambda: dve.tensor_tensor(out=tv3, in0=src3,
                                                  in1=stt[:, 8:8 + H].unsqueeze(2).broadcast_to([128, H, 128]),
                                                  op=ALU.mult), reads=src_rl + [rs], writes=[r_tf])
            yield ("rel", "srcbank")
            P.op("dve", lambda: dve.tensor_tensor(out=tv3, in0=tv3,
                                                  in1=g[:].unsqueeze(1).broadcast_to([128, H, 128]),
                                                  op=ALU.mult), reads=[r_tf, R_const], writes=[r_tf])
            yield
            cosb = cs[:, eb, :].unsqueeze(1).broadcast_to([128, H, 16])
            sinb = sn[:, eb, :].unsqueeze(1).broadcast_to([128, H, 16])
            x1 = tv3[:, :, 0:16]
            x2 = tv3[:, :, 16:32]
            dst3 = dstb.rearrange("p (h d) -> p h d", d=128)
            r3 = rp[:, 0:H, :, :]
            P.op("act", lambda: act.activation(out=dstb, in_=tv, func=AF.Copy), reads=[r_tf], writes=[r_dst])
            yield
            for (a1, b1, a2, b2, opx, lo) in ((x1, cosb, x2, sinb, ALU.subtract, 0),
                                               (x2, cosb, x1, sinb, ALU.add, 16)):
                P.op("dve", lambda a1=a1, b1=b1: dve.tensor_tensor(out=r3[:, :, 0, :], in0=a1, in1=b1, op=ALU.mult),
                     reads=[r_tf, R_task], writes=[r_rp])
                yield
                P.op("dve", lambda a2=a2, b2=b2: dve.tensor_tensor(out=r3[:, :, 1, :], in0=a2, in1=b2, op=ALU.mult),
                     reads=[r_tf, R_task, r_rp], writes=[r_rp])
                yield
                P.op("dve", lambda opx=opx, lo=lo: dve.tensor_tensor(out=dst3[:, :, lo:lo + 16], in0=r3[:, :, 0, :],
                                                                     in1=r3[:, :, 1, :], op=opx),
                     reads=[r_rp], writes=[r_dst])
                yield

        def g_u(b, j, slot, i):
            eb = b + 1
            bk5 = i % 4
            yield ("acq", f"bank{bk5}")
            P.group("pe", [(lambda c=c: pe.matmul(out=pmm[:, bk5, :], lhsT=hT[:, c, eb * 128:(eb + 1) * 128],
                                                  rhs=wsl[slot][:, c, :], start=(c == 0), stop=(c == 15)))
                           for c in range(16)],
                    reads=[R_hT[eb], R_wsl[slot]], writes=[R_bank[bk5]])
            yield
            P.op("act", lambda: act.activation(out=uu[:, b, j * 512:(j + 1) * 512], in_=pmm[:, bk5, :],
                                               func=AF.Gelu_apprx_tanh, scale=rsA[:, eb:eb + 1]),
                 reads=[R_bank[bk5], R_rsA[eb]], writes=[R_u[b]])
            yield ("rel", f"bank{bk5}")

        def g_vg(b, slot0, slot1, i):
            eb = b + 1
            sset = i % NSET
            stt, rs = stats[SSTAT[sset]], R_stat[SSTAT[sset]]
            tf, r_tf = tmpfS[sset], R_tmpfS[sset]
            bnst, R_bn = bnsts[sset], R_bns[sset]
            yield ("acq", f"set{sset}")
            for j, slot in ((0, slot0), (1, slot1)):
                bk5 = (2 * i + j) % 4
                yield ("acq", f"bank{bk5}")
                P.group("pe", [(lambda c=c, slot=slot, bk5=bk5: pe.matmul(
                    out=pmm[:, bk5, :], lhsT=hT[:, c, eb * 128:(eb + 1) * 128], rhs=wsl[slot][:, c, :],
                    start=(c == 0), stop=(c == 15))) for c in range(16)],
                    reads=[R_hT[eb], R_wsl[slot]], writes=[R_bank[bk5]])
                yield
                P.op("act", lambda j=j, bk5=bk5: act.activation(out=tf[:, j * 512:(j + 1) * 512], in_=pmm[:, bk5, :],
                                                       func=AF.Gelu_apprx_tanh, scale=rsA[:, eb:eb + 1]),
                     reads=[R_bank[bk5], R_rsA[eb]], writes=[r_tf])
                yield ("rel", f"bank{bk5}")
                P.op("dve", lambda j=j: dve.bn_stats(out=bnst[:, j, :], in_=tf[:, j * 512:(j + 1) * 512]),
                     reads=[r_tf], writes=[R_bn])
                yield
            P.op("dve", lambda: dve.bn_aggr(out=stt[:, 16:18], in_=bnst[:].rearrange("p a b -> p (a b)")),
                 reads=[R_bn], writes=[rs])
            yield
            P.op("dve", lambda: dve.tensor_scalar(out=stt[:, 17:18], in0=stt[:, 17:18], scalar1=EPS, scalar2=None,
                                                  op0=ALU.add), reads=[rs], writes=[rs])
            yield
            P.op("act", lambda: act.activation(out=stt[:, 17:18], in_=stt[:, 17:18], func=AF.Ln),
                 reads=[rs], writes=[rs])
            yield
            P.op("act", lambda: act.activation(out=stt[:, 17:18], in_=stt[:, 17:18], func=AF.Exp, scale=-0.5),
                 reads=[rs], writes=[rs])
            yield
            P.op("dve", lambda: dve.tensor_scalar(out=tf[:], in0=tf[:], scalar1=stt[:, 16:17],
                                                  scalar2=stt[:, 17:18], op0=ALU.subtract, op1=ALU.mult),
                 reads=[r_tf, rs], writes=[r_tf])
            yield
            P.op("dve", lambda: dve.tensor_tensor(out=tf[:], in0=tf[:], in1=gln[:], op=ALU.mult),
                 reads=[r_tf, R_const], writes=[r_tf])
            yield
            P.op("dve", lambda: dve.tensor_tensor(out=vgn[:, b, :], in0=tf[:], in1=bln[:], op=ALU.add),
                 reads=[r_tf, R_const], writes=[R_vgn[b]])
            yield ("rel", f"set{sset}")

        def g_att(b, kvg, unit):
            eb = b + 1
            ds = unit % 2
            sm_, pb_, pT_ = smS[ds], pbS[ds], pTsS[ds]
            sti = 3 + unit % 4
            stt, rs = stats[sti], R_stat[sti]
            kind = 0 if b == 0 else (2 if b == 3 else 1)
            mt, r_mt = mixtok[b % 2], R_mix[b % 2]
            yield ("acq", f"st{sti}")
            yield ("acq", f"sm{ds}")
            for k_ in range(4):
                yield ("acq", f"bank{k_}")
            P.group("pe", [(lambda hh=hh: pe.matmul(
                out=pmm[:, hh, 0:384], lhsT=qT[:, 4 * kvg + hh, b * 128:(b + 1) * 128],
                rhs=kT[:, kvg, (eb - 1) * 128:(eb + 2) * 128], start=True, stop=True)) for hh in range(4)],
                reads=[R_qT[b], R_kT[eb - 1], R_kT[eb], R_kT[eb + 1]], writes=R_bank[0:4])
            yield
            P.op("dve", lambda: dve.scalar_tensor_tensor(
                out=sm_, in0=pmm[:, 0:4, 0:384], scalar=SCALE,
                in1=msk[:, kind, :].unsqueeze(1).broadcast_to([128, 4, 384]), op0=ALU.mult, op1=ALU.add),
                reads=R_bank[0:4] + [R_task], writes=[R_sm[ds]])
            for k_ in range(4):
                yield ("rel", f"bank{k_}")
            P.op("dve", lambda: dve.tensor_reduce(out=stt[:, 0:4], in_=sm_, axis=AX.X, op=ALU.max),
                 reads=[R_sm[ds]], writes=[rs])
            yield
            P.op("dve", lambda: dve.scalar_tensor_tensor(out=stt[:, 4:8], in0=stt[:, 0:4], scalar=-1.0,
                                                         in1=nsink[:, 4 * kvg:4 * kvg + 4], op0=ALU.mult,
                                                         op1=ALU.min),
                 reads=[rs, R_nsink], writes=[rs])
            yield
            yield ("acq", f"pb{ds}")
            for hh in range(4):
                P.op("act", lambda hh=hh: act.activation(out=pb_[:, hh, :], in_=sm_[:, hh, :], func=AF.Exp,
                                                         bias=stt[:, 4 + hh:5 + hh], scale=1.0,
                                                         accum_out=stt[:, 8 + hh:9 + hh]),
                     reads=[R_sm[ds], rs], writes=[R_pb[ds], rs])
                yield
            yield ("rel", f"sm{ds}")
            P.op("dve", lambda: dve.tensor_tensor(out=stt[:, 12:16], in0=sinkb[:, 4 * kvg:4 * kvg + 4],
                                                  in1=stt[:, 4:8], op=ALU.add), reads=[rs, R_const], writes=[rs])
            yield
            P.op("act", lambda: act.activation(out=stt[:, 12:16], in_=stt[:, 12:16], func=AF.Exp),
                 reads=[rs], writes=[rs])
            yield
            P.op("dve", lambda: dve.tensor_tensor(out=stt[:, 16:20], in0=stt[:, 8:12], in1=stt[:, 12:16],
                                                  op=ALU.add), reads=[rs], writes=[rs])
            yield
            P.op("dve", lambda: dve.reciprocal(out=stt[:, 20:24], in_=stt[:, 16:20]), reads=[rs], writes=[rs])
            yield
            yield ("acq", "bank6")
            yield ("acq", "bank7")
            tbf = TB16[1]
            P.group("pe", [(lambda hh=hh, kb=kb: pe.transpose(
                out=tbf[:, hh * 3 + kb, :], in_=pb_[:, hh, kb * 128:(kb + 1) * 128], identity=ident[:]))
                for hh in range(4) for kb in range(3)],
                reads=[R_pb[ds], R_ident], writes=R_TB[1])
            yield ("rel", f"pb{ds}")
            yield ("acq", f"pT{ds}")
            P.op("act", lambda: act.activation(out=pT_, in_=tbf[:, 0:12, :], func=AF.Copy),
                 reads=R_TB[1], writes=[R_pTs[ds]])
            yield ("rel", "bank6")
            yield ("rel", "bank7")
            yield ("acq", "bank4")
            P.group("pe", [(lambda hh=hh, kb=kb: pe.matmul(
                out=pmm[:, 4, hh * 128:(hh + 1) * 128], lhsT=pT_[:, hh * 3 + kb, :],
                rhs=vv[:, eb - 1 + kb, kvg * 128:(kvg + 1) * 128], start=(kb == 0), stop=(kb == 2)))
                for hh in range(4) for kb in range(3)],
                reads=[R_pTs[ds], R_v[eb - 1], R_v[eb], R_v[eb + 1]], writes=[R_bank[4]])
            yield ("rel", f"pT{ds}")
            P.op("dve", lambda: dve.tensor_tensor(
                out=mt[:, kvg * 512:(kvg + 1) * 512].rearrange("p (h d) -> p h d", d=128),
                in0=pmm[:, 4, :].rearrange("p (h d) -> p h d", d=128),
                in1=stt[:, 20:24].unsqueeze(2).broadcast_to([128, 4, 128]), op=ALU.mult),
                reads=[R_bank[4], rs], writes=[r_mt])
            yield ("rel", "bank4")
            yield ("rel", f"st{sti}")

        def g_gmn(b):
            mt, r_mt = mixtok[b % 2], R_mix[b % 2]
            stt, rs = stats[7], R_stat[7]
            xbt, r_xb = xbD[b % 2], (R_xbS[0] if b % 2 == 0 else R_tmpfS[0])
            yield ("acq", "bank0")
            yield ("acq", "bank1")
            P.group("pe", [(lambda h=h: pe.matmul(
                out=pmm[:, h // 4, (h % 4) * 128:(h % 4 + 1) * 128], lhsT=wspT[:, h, :],
                rhs=vgn[:, b, h * 128:(h + 1) * 128], start=True, stop=True)) for h in range(8)],
                reads=[R_wsp, R_vgn[b]], writes=[R_bank[0], R_bank[1]])
            yield
            for h in range(8):
                P.op("dve", lambda h=h: dve.scalar_tensor_tensor(
                    out=mt[:, 1024 + h * 128:1024 + (h + 1) * 128],
                    in0=pmm[:, h // 4, (h % 4) * 128:(h % 4 + 1) * 128], scalar=bsp[:, h:h + 1],
                    in1=uu[:, b, h * 128:(h + 1) * 128], op0=ALU.add, op1=ALU.mult),
                    reads=[R_bank[h // 4], R_const, R_u[b]], writes=[r_mt])
                yield
            yield ("rel", "bank0")
            yield ("rel", "bank1")
            yield ("acq", f"xbD{b % 2}")
            yield ("acq", "st7")
            for half in range(2):
                src = mt[:, half * 1024:(half + 1) * 1024]
                dstb = xbt[:, half * 1024:(half + 1) * 1024]
                P.op("act", lambda src=src, dstb=dstb: act.activation(out=dstb, in_=src, func=AF.Square,
                                                                      accum_out=stt[:, half:half + 1]),
                     reads=[r_mt], writes=[r_xb, rs])
                yield
            yield from rstd_chain(stt, rs, 0, 1024, 2)
            for half in range(2):
                src = mt[:, half * 1024:(half + 1) * 1024]
                dstb = xbt[:, half * 1024:(half + 1) * 1024]
                P.op("act", lambda src=src, dstb=dstb, half=half: act.activation(
                    out=dstb, in_=src, func=AF.Copy, scale=stt[:, half:half + 1]),
                    reads=[r_mt, rs], writes=[r_xb])
                yield
            yield ("rel", "st7")
            yield ("mark", f"m{b}")
            yield ("acq", "bank4")
            yield ("acq", "bank5")
            P.group("pe", [(lambda c=c: pe.transpose(out=TB16[0][:, c, :], in_=xbt[:, c * 128:(c + 1) * 128],
                                                     identity=ident[:])) for c in range(16)],
                    reads=[r_xb, R_ident], writes=R_TB[0])
            yield ("rel", f"xbD{b % 2}")
            P.op("dve", lambda: dve.tensor_tensor(out=mh[:, :, b * 128:(b + 1) * 128], in0=TB16[0],
                                                  in1=goutT[:].unsqueeze(2).broadcast_to([128, 16, 128]),
                                                  op=ALU.mult),
                 reads=R_TB[0] + [R_const], writes=[R_mh[b]])
            yield ("rel", "bank4")
            yield ("rel", "bank5")

        def g_E(b, cgs, base_):
            for cg in cgs:
                slot = (base_ + 7 + cg) % NSLOT
                yield ("acq", "bank5")
                P.group("pe", [(lambda c=c, slot=slot: pe.matmul(
                    out=pmm[:, 5, :], lhsT=mh[:, c, b * 128:(b + 1) * 128], rhs=wsl[slot][:, c, :],
                    start=(c == 0), stop=(c == 15))) for c in range(16)],
                    reads=[R_mh[b], R_wsl[slot]], writes=[R_bank[5]])
                yield ("mark", f"e{b}c{cg}")
                P.op("dve", lambda cg=cg: dve.tensor_tensor(
                    out=xres[:, b, cg * 512:(cg + 1) * 512], in0=pmm[:, 5, :],
                    in1=xres[:, b, cg * 512:(cg + 1) * 512], op=ALU.add),
                    reads=[R_bank[5], R_xres[b]], writes=[R_xres[b]])
                yield ("rel", "bank5")

        def g_F(b, i):
            xbt, r_xb = fxb[i % 3], R_fxb[i % 3]
            yield ("acq", f"fxb{i % 3}")
            for r in g_rms_to_T(xres[:, b, :], R_xres[b], gffnT, mh[:, :, b * 128:(b + 1) * 128], R_mh[b],
                                xbt, r_xb, big[:, 4096:6144], R_junkF, stats[i % 3], R_stat[i % 3],
                                f"tb{i % 2}", "F", b, R_rsF[b]):
                yield (None if r == ("mark_src_read",) else r)
            yield ("rel", f"fxb{i % 3}")

        prefetched = set()

        def issue_casts(tt, ebs):
            for eb in ebs:
                sset = eb % NSET
                P.dma("pool", d_xc[sset], lambda sset=sset, eb=eb: pool.dma_start(
                    out=xbS[sset], in_=xe[tt, eb * 128:(eb + 1) * 128, :]), writes=[R_xbS[sset]])
                prefetched.add((tt, eb))

        bank_rot = {"i": 0}

        def next_bank(n=4):
            b = bank_rot["i"]
            bank_rot["i"] = (b + 1) % n
            return b

        try:
          ckpt("const")
          for t in range(n_tasks):
            base = t * n_per_task
            alias_edges(R_mh, mh_scratch)
            alias_edges([R_yst] + R_rl, yst_scratch)
            issue_casts(t, [0, 1, 2, 3])
            ensure_loaded(base + 2)

            for i_, (dst, src) in enumerate(((msk, msk_d), (cs, cs_d), (sn, sn_d))):
                P.dma("sp", d_task, lambda dst=dst, src=src: sync.dma_start(
                    out=dst[:].rearrange("p a b -> p (a b)"), in_=src[t]),
                    writes=[R_task if i_ == 0 else Res("tc")])
            R_task.w = (d_task.key, d_task.count)
            for eb in range(5):
                dst_ap, dres, dsem = xdst(eb)
                P.dma("sp", dsem, lambda dst_ap=dst_ap, eb=eb: sync.dma_start(
                    out=dst_ap, in_=xe[t, eb * 128:(eb + 1) * 128, :]), writes=[dres])

            alias_edges(R_mh, mh_scratch)
            alias_edges([R_yst] + R_rl, yst_scratch)
            alias_edges(R_actT, arena_phase1)

            s_kv, s_q0, s_q1 = base % NSLOT, (base + 1) % NSLOT, (base + 2) % NSLOT
            s_u = [(base + 3) % NSLOT, (base + 4) % NSLOT]
            s_vg = [(base + 5) % NSLOT, (base + 6) % NSLOT]
            allA = [f"A{eb}" for eb in range(6)]
            allq = [f"q{b}" for b in range(4)]
            items = [Item(f"A{eb}", g_A_kv(t, eb, 0, s_kv)) for eb in range(6)]
            items += [Item(f"q{b}", g_q(t, b, s_q0, s_q1, b), after=[f"A{b + 1}"]) for b in range(4)]
            items.append(Item("ld_u0", g_load(base + 3), after=allA))
            items += [Item(f"u0{b}", g_u(b, 0, s_u[0], b)) for b in range(4)]
            items.append(Item("ld_u1", g_load(base + 5), after=allq))
            items += [Item(f"u1{b}", g_u(b, 1, s_u[1], b)) for b in range(4)]
            items.append(Item("ld_vg1", g_load(base + 6), after=[f"u0{b}" for b in range(4)]))
            items += [Item(f"vg{b}", g_vg(b, s_vg[0], s_vg[1], b)) for b in range(4)]
            run_pipeline(items, width=4, stagger=5)
            ckpt("AB")

            ensure_loaded(base + 9)
            alias_edges(R_hT, dscratch_core)
            alias_edges(mh_scratch, R_mh)
            alias_edges(yst_scratch, [R_yst])
            items = []
            unit = 0
            for b in range(4):
                for kvg in range(2):
                    items.append(Item(f"a{b}{kvg}", g_att(b, kvg, unit), after=([f"m{b - 2}"] if b >= 2 else [])))
                    unit += 1
                if b >= 1:
                    items.append(Item(f"g{b - 1}", g_gmn(b - 1), after=[f"a{b - 1}0", f"a{b - 1}1"]))
                    items.append(Item(f"E{b - 1}", g_E(b - 1, (0, 1, 2), base), after=[f"g{b - 1}"]))
            items.append(Item("g3", g_gmn(3), after=["a30", "a31"]))
            items.append(Item("E3", g_E(3, (0, 1, 2), base), after=["g3"]))
            items.append(Item("ld_wo3", g_load(base + 10), after=[f"e{b}c0" for b in range(4)]))
            items.append(Item("ld_up0", g_load(base + 11), after=["ld_wo3"] + [f"e{b}c1" for b in range(4)]))
            items.append(Item("ld_up1", g_load(base + 12), after=["ld_up0"] + [f"e{b}c2" for b in range(4)]))
            run_pipeline(items, width=8, stagger=2, lookahead=4)
            ckpt("D")
            si = 10

            bank_rot["i"] = 0
            for cg in range(3, 4):
                n = base + si
                ensure_loaded(n + 2)
                s = n % NSLOT
                for b in range(4):
                    bk = next_bank()
                    P.group("pe", [(lambda c=c, b=b, s=s, bk=bk: pe.matmul(
                        out=pmm[:, bk, :], lhsT=mh[:, c, b * 128:(b + 1) * 128], rhs=wsl[s][:, c, :],
                        start=(c == 0), stop=(c == 15))) for c in range(16)],
                        reads=[R_mh[b], R_wsl[s]], writes=[R_bank[bk]])
                    P.op("dve", lambda b=b, cg=cg, bk=bk: dve.tensor_tensor(
                        out=xres[:, b, cg * 512:(cg + 1) * 512], in0=pmm[:, bk, :],
                        in1=xres[:, b, cg * 512:(cg + 1) * 512], op=ALU.add),
                        reads=[R_bank[bk], R_xres[b]], writes=[R_xres[b]])
                si += 1
            ckpt("E")

            alias_edges(dscratch_core, fxb_res)
            items = [Item(f"F{b}", g_F(b, b)) for b in range(4)]
            run_pipeline(items, width=3, stagger=4)
            ckpt("F")

            bank_rot["i"] = 0
            for fg in range(16):
                n = base + si
                ensure_loaded(n + 2)
                s = n % NSLOT
                for fi in range(4):
                    fc = fg * 4 + fi
                    bk = next_bank(2)
                    P.group("pe", [(lambda c=c, fi=fi, s=s, bk=bk: pe.matmul(
                        out=pmm[:, bk, :], lhsT=wsl[s][:, c, fi * 128:(fi + 1) * 128], rhs=mh[:, c, :],
                        start=(c == 0), stop=(c == 15))) for c in range(16)],
                        reads=R_mh + [R_wsl[s]], writes=[R_bank[bk]])
                    rl = yst[:, bk, :]
                    P.op("act", lambda bk=bk, rl=rl: act.activation(out=rl, in_=pmm[:, bk, :], func=AF.Relu),
                         reads=[R_bank[bk]], writes=[R_rl[bk], R_yst])
                    extra = all_phase1 if (fg == 0 and fi == 0) else []
                    P.op("dve", lambda fc=fc, rl=rl: dve.tensor_tensor(out=actT[:, fc, :], in0=rl, in1=rl,
                                                                       op=ALU.mult),
                         reads=[R_rl[bk]], writes=[R_actT[fg]] + extra)
                si += 1
            ckpt("G")

            for cg in range(4):
                for sub in range(4):
                    n = base + si
                    ensure_loaded(n + 2)
                    s = n % NSLOT
                    for b in range(4):
                        P.group("pe", [(lambda fc=fc, b=b, s=s, sub=sub: pe.matmul(
                            out=pmm[:, 2 + b, :], lhsT=actT[:, sub * 16 + fc, b * 128:(b + 1) * 128],
                            rhs=wsl[s][:, fc, :], start=(sub == 0 and fc == 0),
                            stop=(sub == 3 and fc == 15))) for fc in range(16)],
                            reads=[R_actT[sub * 4 + k] for k in range(4)] + [R_wsl[s]],
                            writes=[R_bank[2 + b]])
                        if sub == 3:
                            P.op("dve", lambda b=b, cg=cg: dve.scalar_tensor_tensor(
                                out=yst[:, b, :], in0=pmm[:, 2 + b, :], scalar=rsF[:, b:b + 1],
                                in1=xres[:, b, cg * 512:(cg + 1) * 512], op0=ALU.mult, op1=ALU.add),
                                reads=[R_bank[2 + b], R_xres[b], R_rsF[b]],
                                writes=[R_yst] + ([R_rl[b]] if b < 2 else []))
                    si += 1
                P.dma("sp", d_y, lambda cg=cg: sync.dma_start(
                    out=y[t].rearrange("(b p) n -> p b n", p=128)[:, :, cg * 512:(cg + 1) * 512],
                    in_=yst[:]), reads=[R_yst])
            assert si == n_per_task
        except _Stop:
            pass
        for e in ("pe", "act", "dve", "pool"):
            if P.cnt[e]:
                sync.wait_ge(P.semh[P.sem[e]], P.cnt[e])
        P.wait_all("sp", [d_const, d_task, d_xst, d_yh, d_wsp] + d_xres + d_w + d_xc + [d_y])
    return nc


def _task_table():
    tasks = []
    for s in range(2):
        for j in range(8):
            tasks.append(("p", s, j * T))
    for s in range(4):
        for j in range(4):
            tasks.append(("s", s, j * T))
    return tasks


def _rope_tables(pos):
    inv_freq = 500000.0 ** (-np.arange(0, 32, 2, dtype=np.float64) / 32.0)
    ang = pos.astype(np.float64)[:, None] * inv_freq[None, :]
    return np.cos(ang).astype(np.float32), np.sin(ang).astype(np.float32)


_NC_CACHE = {}


def kernel(x_prompt, x_sample, g_mix, w_in, g_q, g_k, sink, g_v_ln, b_v_ln, w_spatial,
           b_spatial, g_attn_out, g_gmlp_out, w_out, g_ffn, w_up, w_down):
    f32 = np.float32
    x_prompt = np.asarray(x_prompt, f32)
    x_sample = np.asarray(x_sample, f32)
    tasks = _task_table()

    def colT(v):
        return np.ascontiguousarray(np.asarray(v, f32).reshape(16, 128).T)

    def bcast(v, n):
        return np.ascontiguousarray(np.broadcast_to(np.asarray(v, f32).reshape(1, n), (128, n)))

    shared = {
        "w_in": np.ascontiguousarray(
            np.asarray(w_in, f32)[0].reshape(16, 128, 7, 512).transpose(2, 1, 0, 3)).reshape(7, 128, 8192),
        "w_out": np.ascontiguousarray(
            np.asarray(w_out, f32)[0].reshape(16, 128, 4, 512).transpose(2, 1, 0, 3)).reshape(4, 128, 8192),
        "w_up": np.ascontiguousarray(
            np.asarray(w_up, f32)[0].reshape(16, 128, 16, 512).transpose(2, 1, 0, 3)).reshape(16, 128, 8192),
        "w_down": np.ascontiguousarray(
            np.asarray(w_down, f32)[0].reshape(4, 16, 128, 4, 512).transpose(3, 0, 2, 1, 4)).reshape(16, 128, 8192),
        "gmixT": colT(g_mix[0]),
        "gffnT": colT(g_ffn[0]),
        "goutT": colT(np.concatenate([np.asarray(g_attn_out, f32)[0], np.asarray(g_gmlp_out, f32)[0]])),
        "gq": bcast(g_q[0], 128),
        "gk": bcast(g_k[0], 128),
        "gln": bcast(g_v_ln[0], 1024),
        "bln": bcast(b_v_ln[0], 1024),
        "sinkb": bcast(sink[0], 8),
        "wspT": np.ascontiguousarray(np.transpose(np.asarray(w_spatial, f32)[0], (2, 0, 1)).reshape(128, 1024)),
        "bsp": np.ascontiguousarray(np.asarray(b_spatial, f32)[0].T),
    }

    qi = np.arange(128)[:, None]
    kj = np.arange(384)[None, :]
    band = (kj >= qi) & (kj <= qi + 256)

    in_maps = []
    for c in range(N_CORES):
        xe = np.zeros((NT, TE, D), f32)
        msk = np.empty((NT, 128, 3, 384), f32)
        cs = np.empty((NT, 128, 6, 16), f32)
        sn = np.empty((NT, 128, 6, 16), f32)
        for i in range(NT):
            grp, s, start = tasks[c * NT + i]
            xs = x_prompt[s] if grp == "p" else x_sample[s]
            S = xs.shape[0]
            lo, hi = start - 128, start + T + 128
            a, b = max(lo, 0), min(hi, S)
            xe[i, a - lo:b - lo] = xs[a:b]
            left_ok = lo >= 0
            right_ok = hi <= S
            m0 = band if left_ok else (band & (kj >= 128))
            m2 = band if right_ok else (band & (kj < 256))
            for kind, m in enumerate((m0, band, m2)):
                msk[i, :, kind, :] = np.where(m, 0.0, NEG).astype(f32)
            pos = np.arange(lo, hi)
            co, si_ = _rope_tables(pos)
            cs[i] = co.reshape(6, 128, 16).transpose(1, 0, 2)
            sn[i] = si_.reshape(6, 128, 16).transpose(1, 0, 2)
        m = dict(shared)
        m["xe"] = xe
        m["msk"] = msk.reshape(NT, 128, 3 * 384)
        m["cs"] = cs.reshape(NT, 128, 96)
        m["sn"] = sn.reshape(NT, 128, 96)
        in_maps.append(m)

    if "nc" not in _NC_CACHE:
        _NC_CACHE["nc"] = build_program()
    nc = _NC_CACHE["nc"]
    res = run_bass_kernel_spmd(nc, in_maps, core_ids=list(range(N_CORES)))

    y_prompt = np.empty((2, 4096, D), f32)
    y_sample = np.empty((4, 2048, D), f32)
    for c in range(N_CORES):
        yc = np.asarray(res.results[c]["y"], f32)
        for i in range(NT):
            grp, s, start = tasks[c * NT + i]
            if grp == "p":
                y_prompt[s, start:start + T] = yc[i]
            else:
                y_sample[s, start:start + T] = yc[i]
    return (y_prompt, y_sample)
```
